# Optimizing a Trainium2 kernel written in Bass

```python
import math, functools
import jax, jax.numpy as jnp
from jax import lax
import numpy as np

D_MODEL = 4096
BATCH = 8
SEQ = 2048
DEPTH = 2
DEC_BATCH = 8
DEC_SEQ = 16
PAST_LEN = 4096

CHUNK = 64
Q_BLOCK = 128
ROPE_THETA = 10000.0
EPS = 1e-6
MIX_WIDTH = D_MODEL
POOL_WINDOWS = (2, 4, 8, 16)
POOL_WIDTH = MIX_WIDTH // 4
POOL_GROUP = POOL_WIDTH // len(POOL_WINDOWS)
POOL_STATE = max(POOL_WINDOWS) - 1
DIFF_D = 64
DIFF_HD = 2 * DIFF_D
DIFF_HEADS = MIX_WIDTH // 4 // DIFF_HD
DIFF_WIDTH = DIFF_HEADS * DIFF_HD
MLA_NOPE = 128
MLA_ROPE = 64
MLA_V = 128
MLA_HEADS = MIX_WIDTH // 2 // MLA_V
MLA_Q_RANK = D_MODEL // 4
MLA_KV_RANK = 512
MLA_WIDTH = MLA_HEADS * MLA_V
MEM_LEN = 256
MEM_HEADS = 4
MEM_HD = 128
MEM_WIDTH = MEM_HEADS * MEM_HD
D_FF = ((-(-8 * D_MODEL // 3)) + 255) // 256 * 256
IN_SPLITS = (POOL_WIDTH, DIFF_WIDTH, DIFF_WIDTH, DIFF_WIDTH, MLA_Q_RANK, MLA_KV_RANK, MLA_ROPE)
IN_WIDTH = sum(IN_SPLITS)

kernel_name = 'hybrid_pool_diff_mla_streaming_step'


def rms_norm(x, g):
    xf = x.astype(jnp.float32)
    y = xf * lax.rsqrt(jnp.mean(xf * xf, axis=-1, keepdims=True) + EPS)
    return (y * g.astype(jnp.float32)).astype(x.dtype)


def rope(x, pos):
    half = x.shape[-1] // 2
    inv = ROPE_THETA ** (-jnp.arange(half, dtype=jnp.float32) / half)
    ang = pos.astype(jnp.float32)[:, None] * inv[None, :]
    bshape = (pos.shape[0],) + (1,) * (x.ndim - 3) + (half,)
    cos = jnp.cos(ang).reshape(bshape)
    sin = jnp.sin(ang).reshape(bshape)
    xf = x.astype(jnp.float32)
    x1, x2 = xf[..., :half], xf[..., half:]
    return jnp.concatenate([x1 * cos - x2 * sin, x2 * cos + x1 * sin], axis=-1).astype(x.dtype)


def chunk_mask(qpos, kpos):
    return (kpos[None, :] // CHUNK) <= (qpos[:, None] // CHUNK)


def diff_attend(q, k, v, qpos, kpos, lam):
    s = jnp.einsum('bqhcd,bkhcd->bhcqk', q, k).astype(jnp.float32) * (DIFF_D ** -0.5)
    s = jnp.where(chunk_mask(qpos, kpos), s, -jnp.inf)
    pr = jax.nn.softmax(s, axis=-1)
    w = pr[:, :, 0] - lam * pr[:, :, 1]
    return jnp.einsum('bhqk,bkhe->bqhe', w.astype(v.dtype), v)


def mla_attend(q_lat, q_pe, ckv, kpe, qpos, kpos):
    s = (jnp.einsum('bqhr,bkr->bhqk', q_lat, ckv) + jnp.einsum('bqhp,bkp->bhqk', q_pe, kpe)).astype(jnp.float32)
    s = jnp.where(chunk_mask(qpos, kpos), s * ((MLA_NOPE + MLA_ROPE) ** -0.5), -jnp.inf)
    pr = jax.nn.softmax(s, axis=-1)
    return jnp.einsum('bhqk,bkr->bqhr', pr.astype(ckv.dtype), ckv)


def prompt_blocks(fn, qs, ks, pos):
    T = pos.shape[0]
    outs = []
    for lo in range(0, T, Q_BLOCK):
        hi = min(lo + Q_BLOCK, T)
        outs.append(fn(*[q[:, lo:hi] for q in qs], *[k[:, :hi] for k in ks], pos[lo:hi], pos[:hi]))
    return jnp.concatenate(outs, axis=1)


def pool_mix(u_ext, pos, w_pool, scale):
    B, L, C = u_ext.shape
    T = L - POOL_STATE
    cs = jnp.cumsum(u_ext.astype(jnp.float32), axis=1)
    cs = jnp.concatenate([jnp.zeros((B, 1, C), jnp.float32), cs], axis=1)
    end = cs[:, POOL_STATE + 1:]
    u = u_ext[:, POOL_STATE:].astype(jnp.float32)
    outs = []
    for g, w in enumerate(POOL_WINDOWS):
        sl = slice(g * POOL_GROUP, (g + 1) * POOL_GROUP)
        start = cs[:, POOL_STATE + 1 - w:POOL_STATE + 1 - w + T, sl]
        cnt = jnp.minimum(w, pos + 1).astype(jnp.float32)[None, :, None]
        outs.append((end[..., sl] - start) / cnt - u[..., sl])
    d = jnp.stack(outs, axis=2).astype(u_ext.dtype)
    y = jnp.einsum('btgc,gce->btge', d, w_pool).reshape(B, T, C)
    return y * scale


def memory_kv(mem, g, wk, wv):
    B, M, _ = mem.shape
    m = rms_norm(mem, g)
    return (m @ wk).reshape(B, M, MEM_HEADS, MEM_HD), (m @ wv).reshape(B, M, MEM_HEADS, MEM_HD)


def cross_attend(h, mk, mv, wq, wo):
    B, T, _ = h.shape
    q = (h @ wq).reshape(B, T, MEM_HEADS, MEM_HD)
    s = jnp.einsum('bqhd,bkhd->bhqk', q, mk).astype(jnp.float32) * (MEM_HD ** -0.5)
    pr = jax.nn.softmax(s, axis=-1)
    o = jnp.einsum('bhqk,bkhd->bqhd', pr.astype(mv.dtype), mv).reshape(B, T, MEM_WIDTH)
    return o @ wo


def hybrid_layer(p, l, x, pos, mem_k, mem_v, past):
    B, T, _ = x.shape
    lam_init = 0.8 - 0.6 * math.exp(-0.3 * l)
    h = rms_norm(x, p['g_mix_pre'][l])
    z = h @ p['w_in'][l]
    offs = np.cumsum(IN_SPLITS)[:-1].tolist()
    u, qd, kd, vd, cq, ckv, kpe = jnp.split(z, offs, axis=-1)
    u_prev = jnp.zeros((B, POOL_STATE, POOL_WIDTH), u.dtype) if past is None else past['pool']
    u_ext = jnp.concatenate([u_prev, u], axis=1)
    y_pool = pool_mix(u_ext, pos, p['w_pool'][l], p['pool_scale'][l])
    new_pool = u_ext[:, -POOL_STATE:]
    qd = rope(qd.reshape(B, T, DIFF_HEADS, 2, DIFF_D), pos)
    kd = rope(kd.reshape(B, T, DIFF_HEADS, 2, DIFF_D), pos)
    vd = vd.reshape(B, T, DIFF_HEADS, DIFF_HD)
    lq = p['diff_lambda'][l].astype(jnp.float32)
    lam = jnp.exp(jnp.sum(lq[0] * lq[1])) - jnp.exp(jnp.sum(lq[2] * lq[3])) + lam_init
    diff_fn = functools.partial(diff_attend, lam=lam)
    q = (rms_norm(cq, p['g_mla_q'][l]) @ p['w_mla_uq'][l]).reshape(B, T, MLA_HEADS, MLA_NOPE + MLA_ROPE)
    q_lat = jnp.einsum('bthn,rhn->bthr', q[..., :MLA_NOPE], p['w_mla_uk'][l])
    q_pe = rope(q[..., MLA_NOPE:], pos)
    ckv = rms_norm(ckv, p['g_mla_kv'][l])
    kpe = rope(kpe, pos)
    if past is None:
        o_diff = prompt_blocks(diff_fn, (qd,), (kd, vd), pos)
        o_lat = prompt_blocks(mla_attend, (q_lat, q_pe), (ckv, kpe), pos)
    else:
        P = past['ckv'].shape[1]
        kpos = jnp.arange(P + T)
        kd_all = jnp.concatenate([past['diff_k'].reshape(B, P, DIFF_HEADS, 2, DIFF_D), kd], axis=1)
        vd_all = jnp.concatenate([past['diff_v'], vd], axis=1)
        o_diff = diff_fn(qd, kd_all, vd_all, pos, kpos)
        o_lat = mla_attend(q_lat, q_pe, jnp.concatenate([past['ckv'], ckv], axis=1),
                           jnp.concatenate([past['kpe'], kpe], axis=1), pos, kpos)
    o_diff = rms_norm(o_diff, p['g_diff_sub'][l]) * (1.0 - lam_init)
    o_mla = jnp.einsum('bthr,rhv->bthv', o_lat, p['w_mla_uv'][l])
    mix = jnp.concatenate([y_pool, o_diff.reshape(B, T, DIFF_WIDTH), o_mla.reshape(B, T, MLA_WIDTH)], axis=-1)
    x = x + rms_norm(mix @ p['w_out'][l], p['g_mix_post'][l])
    h = rms_norm(x, p['g_x_pre'][l])
    x = x + rms_norm(cross_attend(h, mem_k, mem_v, p['w_mem_q'][l], p['w_mem_o'][l]), p['g_x_post'][l])
    h = rms_norm(x, p['g_ff_pre'][l])
    f = (jax.nn.silu(h @ p['w_gate'][l]) * (h @ p['w_up'][l])) @ p['w_down'][l]
    x = x + rms_norm(f, p['g_ff_post'][l])
    new = (kd.reshape(B, T, DIFF_HEADS, DIFF_HD), vd, ckv, kpe, new_pool)
    return x, new


def setup_inputs(seed: int = 0) -> dict:
    key = jax.random.key(seed)
    ks = iter(jax.random.split(key, 48))

    def nrm(shape, scale=1.0):
        return jax.random.normal(next(ks), shape, jnp.float32) * scale

    def gain(shape):
        return 1.0 + 0.05 * nrm(shape)

    L, D = DEPTH, D_MODEL
    return {
        'x_prompt': nrm((BATCH, SEQ, D)),
        'x_sample': nrm((DEC_BATCH, DEC_SEQ, D)),
        'cache_diff_k': nrm((L, DEC_BATCH, PAST_LEN, DIFF_HEADS, DIFF_HD)),
        'cache_diff_v': nrm((L, DEC_BATCH, PAST_LEN, DIFF_HEADS, DIFF_HD)),
        'cache_mla_ckv': nrm((L, DEC_BATCH, PAST_LEN, MLA_KV_RANK)),
        'cache_mla_kpe': nrm((L, DEC_BATCH, PAST_LEN, MLA_ROPE)),
        'cache_pool': nrm((L, DEC_BATCH, POOL_STATE, POOL_WIDTH)),
        'cache_mem_k': nrm((L, DEC_BATCH, MEM_LEN, MEM_HEADS, MEM_HD)),
        'cache_mem_v': nrm((L, DEC_BATCH, MEM_LEN, MEM_HEADS, MEM_HD)),
        'mem_prompt': nrm((BATCH, MEM_LEN, D)),
        'g_mix_pre': gain((L, D)),
        'w_in': nrm((L, D, IN_WIDTH), D ** -0.5),
        'w_pool': nrm((L, len(POOL_WINDOWS), POOL_GROUP, POOL_GROUP), POOL_GROUP ** -0.5),
        'pool_scale': 1.0 + 0.1 * nrm((L, POOL_WIDTH)),
        'diff_lambda': nrm((L, 4, DIFF_D), 0.1),
        'g_diff_sub': gain((L, DIFF_HD)),
        'g_mla_q': gain((L, MLA_Q_RANK)),
        'w_mla_uq': nrm((L, MLA_Q_RANK, MLA_HEADS * (MLA_NOPE + MLA_ROPE)), MLA_Q_RANK ** -0.5),
        'w_mla_uk': nrm((L, MLA_KV_RANK, MLA_HEADS, MLA_NOPE), MLA_KV_RANK ** -0.5),
        'w_mla_uv': nrm((L, MLA_KV_RANK, MLA_HEADS, MLA_V), MLA_KV_RANK ** -0.5),
        'g_mla_kv': gain((L, MLA_KV_RANK)),
        'w_out': nrm((L, MIX_WIDTH, D), MIX_WIDTH ** -0.5),
        'g_mix_post': gain((L, D)),
        'g_mem': gain((L, D)),
        'w_mem_k': nrm((L, D, MEM_WIDTH), D ** -0.5),
        'w_mem_v': nrm((L, D, MEM_WIDTH), D ** -0.5),
        'w_mem_q': nrm((L, D, MEM_WIDTH), D ** -0.5),
        'w_mem_o': nrm((L, MEM_WIDTH, D), MEM_WIDTH ** -0.5),
        'g_x_pre': gain((L, D)),
        'g_x_post': gain((L, D)),
        'g_ff_pre': gain((L, D)),
        'w_gate': nrm((L, D, D_FF), D ** -0.5),
        'w_up': nrm((L, D, D_FF), D ** -0.5),
        'w_down': nrm((L, D_FF, D), D_FF ** -0.5),
        'g_ff_post': gain((L, D)),
    }


def reference(x_prompt, x_sample, cache_diff_k, cache_diff_v, cache_mla_ckv, cache_mla_kpe, cache_pool,
              cache_mem_k, cache_mem_v, mem_prompt, g_mix_pre, w_in, w_pool, pool_scale, diff_lambda,
              g_diff_sub, g_mla_q, w_mla_uq, w_mla_uk, w_mla_uv, g_mla_kv, w_out, g_mix_post, g_mem,
              w_mem_k, w_mem_v, w_mem_q, w_mem_o, g_x_pre, g_x_post, g_ff_pre, w_gate, w_up, w_down,
              g_ff_post):
    p = {'g_mix_pre': g_mix_pre, 'w_in': w_in, 'w_pool': w_pool, 'pool_scale': pool_scale,
         'diff_lambda': diff_lambda, 'g_diff_sub': g_diff_sub, 'g_mla_q': g_mla_q, 'w_mla_uq': w_mla_uq,
         'w_mla_uk': w_mla_uk, 'w_mla_uv': w_mla_uv, 'g_mla_kv': g_mla_kv, 'w_out': w_out,
         'g_mix_post': g_mix_post, 'w_mem_q': w_mem_q, 'w_mem_o': w_mem_o, 'g_x_pre': g_x_pre,
         'g_x_post': g_x_post, 'g_ff_pre': g_ff_pre, 'w_gate': w_gate, 'w_up': w_up, 'w_down': w_down,
         'g_ff_post': g_ff_post}
    pos_p = jnp.arange(x_prompt.shape[1])
    pos_s = cache_mla_ckv.shape[2] + jnp.arange(x_sample.shape[1])
    xp, xs = x_prompt, x_sample
    pk, pv, pc, pe, pp, pmk, pmv = [], [], [], [], [], [], []
    sk, sv, sc, se, sp = [], [], [], [], []
    for l in range(DEPTH):
        mk, mv = memory_kv(mem_prompt, g_mem[l], w_mem_k[l], w_mem_v[l])
        xp, (a, b, c, d, e) = hybrid_layer(p, l, xp, pos_p, mk, mv, None)
        pk.append(a); pv.append(b); pc.append(c); pe.append(d); pp.append(e); pmk.append(mk); pmv.append(mv)
        past = {'diff_k': cache_diff_k[l], 'diff_v': cache_diff_v[l], 'ckv': cache_mla_ckv[l],
                'kpe': cache_mla_kpe[l], 'pool': cache_pool[l]}
        xs, (a, b, c, d, e) = hybrid_layer(p, l, xs, pos_s, cache_mem_k[l], cache_mem_v[l], past)
        sk.append(a); sv.append(b); sc.append(c); se.append(d); sp.append(e)
    return (xp, xs,
            jnp.stack(pk), jnp.stack(pv), jnp.stack(pc), jnp.stack(pe), jnp.stack(pp),
            jnp.stack(pmk), jnp.stack(pmv),
            jnp.stack(sk), jnp.stack(sv), jnp.stack(sc), jnp.stack(se), jnp.stack(sp))
```

```python
import math
from contextlib import ExitStack

import numpy as np
import ml_dtypes
import concourse.bass as bass
import concourse.mybir as mybir
from concourse.bass_utils import run_bass_kernel_spmd

F32 = mybir.dt.float32
BF = mybir.dt.bfloat16
AF = mybir.ActivationFunctionType
ALU = mybir.AluOpType
AX = mybir.AxisListType
EPS = 1e-6
NEG = -30000.0


class Cfg:
    def __init__(s, D=4096, SEQ=2048, DEC=16, PAST=4096, MEM=256, DFF=11008, L=2):
        s.D, s.P, s.S, s.PAST, s.MEM, s.DFF, s.L = D, SEQ, DEC, PAST, MEM, DFF, L
        s.KC = D // 128
        s.PW = D // 4
        s.PCH = s.PW // 128
        s.PG = s.PW // 4
        s.CPG = s.PG // 128
        s.DH = D // 4 // 128
        s.DW = s.DH * 128
        s.MH = D // 2 // 128
        s.QR = D // 4
        s.QRC = s.QR // 128
        s.UQW = s.MH * 192
        s.INW = s.PW + 3 * s.DW + s.QR + 512 + 64
        s.NT = SEQ + DEC
        s.NB = SEQ // 128
        s.FC = DFF // 128
        s.NKS = PAST + DEC
        s.tbs = [(i * 128, 128) for i in range(s.NB)] + [(SEQ, DEC)]
        s.tgs = []
        c = 0
        while c < SEQ:
            w = min(512, SEQ - c)
            s.tgs.append((c, w))
            c += w
        s.tgs.append((SEQ, DEC))


class Res:
    __slots__ = ("name", "w", "r", "sem", "cnt")

    def __init__(s, name):
        s.name, s.w, s.r, s.sem, s.cnt = name, {}, {}, None, 0


class T:
    def __init__(s, h, name):
        s.h, s.r = h, Res(name)

    def __getitem__(s, k):
        return s.h[k]


class Sched:
    def __init__(s, nc, es):
        s.nc, s.es = nc, es
        s.E = {"pe": nc.tensor, "act": nc.scalar, "dve": nc.vector, "pool": nc.gpsimd, "sp": nc.sync}
        s.csem = {e: es.enter_context(nc.semaphore("c_" + e)) for e in ("pe", "act", "dve", "pool")}
        s.ccnt = {e: 0 for e in s.csem}
        s.waited = {e: {} for e in s.E}
        s.nsem = 4
        s.phys = []
        s.inuse = []
        s.free = {"hw": [], "sw": []}

    def _deps(s, reads, writes):
        ev = {}
        for r in reads:
            for k, v in r.w.items():
                if v[1] > ev.get(k, (None, 0))[1]:
                    ev[k] = v
        for w in writes:
            for d in (w.w, w.r):
                for k, v in d.items():
                    if v[1] > ev.get(k, (None, 0))[1]:
                        ev[k] = v
        return ev

    def _wait(s, e, ev):
        for k, (sem, val) in ev.items():
            if e == "pe" and k == "c_pe":
                continue
            if s.waited[e].get(k, 0) < val:
                s.E[e].wait_ge(sem, val)
                s.waited[e][k] = val

    def _commit(s, key, evt, reads, writes):
        for w in writes:
            w.w[key] = evt
        for r in reads:
            r.r[key] = evt

    def op(s, e, emit, reads=(), writes=()):
        s._wait(e, s._deps(reads, writes))
        ins = emit(s.E[e])
        s.ccnt[e] += 1
        ins.then_inc(s.csem[e], 1)
        s._commit("c_" + e, (s.csem[e], s.ccnt[e]), reads, writes)

    def dma(s, q, out, in_, lane, reads=(), writes=()):
        s._wait(q, s._deps(reads, writes))
        kind = "sw" if q == "pool" else "hw"
        if lane.sem is None:
            lane.sem = {}
        if kind not in lane.sem:
            if s.free[kind]:
                ph = s.free[kind].pop()
            else:
                nm = "d%s%d" % (kind, len(s.phys))
                ph = [s.es.enter_context(s.nc.semaphore(nm)), nm, 0]
                s.phys.append(ph)
                s.nsem += 1
            lane.sem[kind] = ph
            s.inuse.append((lane, kind))
        ph = lane.sem[kind]
        ins = s.E[q].dma_start(out=out, in_=in_)
        ph[2] += 16
        ins.then_inc(ph[0], 16)
        s._commit(ph[1], (ph[0], ph[2]), reads, writes)

    def barrier(s):
        ev = {"c_" + e: (s.csem[e], s.ccnt[e]) for e in s.csem if s.ccnt[e]}
        for ph in s.phys:
            if ph[2]:
                ev[ph[1]] = (ph[0], ph[2])
        for e in s.E:
            s._wait(e, ev)
        for lane, kind in s.inuse:
            s.free[kind].append(lane.sem[kind])
            del lane.sem[kind]
        s.inuse = []

    def finish(s):
        s.barrier()


class Builder:
    def __init__(s, cfg, upto=99):
        s.c = cfg
        s.upto = upto
        s.nc = bass.Bass("TRN2", target_bir_lowering=False)
        s.es = ExitStack()
        s.sch = Sched(s.nc, s.es)
        s.uid = 0

    def scope(s):
        b = s

        class _Scope(ExitStack):
            def __exit__(self, *a):
                if a[0] is None:
                    b.sch.barrier()
                return super().__exit__(*a)
        return _Scope()

    def din(s, name, shape, dt=F32):
        return s.nc.dram_tensor(name, list(shape), dt, kind="ExternalInput").ap()

    def dout(s, name, shape, dt=F32):
        return s.nc.dram_tensor(name, list(shape), dt, kind="ExternalOutput").ap()

    def dscr(s, name, shape, dt=BF):
        return s.nc.dram_tensor(name, list(shape), dt, kind="Internal").ap()

    def sb(s, st, name, shape, dt):
        s.uid += 1
        nm = "%s_%d" % (name, s.uid)
        return T(st.enter_context(s.nc.sbuf_tensor(nm, list(shape), dt)), nm)

    def act(s, out, in_, func, reads, writes, **kw):
        s.sch.op("act", lambda e: e.activation(out=out, in_=in_, func=func, **kw), reads, writes)

    def dve(s, emit, reads, writes):
        s.sch.op("dve", emit, reads, writes)

    def pool(s, emit, reads, writes):
        s.sch.op("pool", emit, reads, writes)

    def load(s, dst_ap, src_ap, dst_t, src_res=(), q="sp"):
        s.sch.dma(q, dst_ap, src_ap, dst_t.r, reads=list(src_res), writes=[dst_t.r])

    def store(s, dst_ap, src_ap, src_t, dst_res=(), q="sp"):
        s.sch.dma(q, dst_ap, src_ap, src_t.r, reads=[src_t.r], writes=list(dst_res))

    def mm_group(s, emit, reads, bank):
        s.sch.op("pe", emit, reads, [bank])

    def rstd(s, out, ss, n, dim, reads, writes):
        s.act(out, ss, AF.Sqrt, list(reads) + [s.epsb.r], writes, bias=s.epsb[0:n, 0:1], scale=1.0 / dim)
        s.dve(lambda e: e.reciprocal(out=out, in_=out), [], writes)

    def transposes(s, src_t, src_ap_fn, n, nrows, dst_fn, bf=True, ncols=128):
        per = (8 if bf else 4)
        ident = s.identb if bf else s.identf
        j = 0
        while j < n:
            cnt = min(per, n - j)
            bk = s.tbank()
            pv = s.ps[:, bk, :]
            if bf:
                pv = pv.bitcast(BF)
            pv = pv.rearrange("p (c t) -> p c t", c=per)

            def emit(pe, j=j, cnt=cnt, pv=pv):
                ins = None
                for i in range(cnt):
                    ins = pe.transpose(out=pv[0:ncols, i, 0:nrows], in_=src_ap_fn(j + i),
                                       identity=ident[0:nrows, 0:nrows])
                return ins
            s.sch.op("pe", emit, [src_t.r, s.identb.r], [s.psr[bk]])
            dst_fn(j, cnt, pv[0:ncols, 0:cnt, 0:nrows], s.psr[bk])
            j += cnt

    def tbank(s):
        s._tb = (s._tb + 1) % 2
        return 6 + s._tb

    def mbank(s):
        s._mb = (s._mb + 1) % 6
        return s._mb

    def build(s):
        c, nc, es = s.c, s.nc, s.es
        L = c.L
        I = {}
        I["xp"] = s.din("xp", [c.P, c.D])
        I["xs"] = s.din("xs", [c.S, c.D])
        I["cdk"] = s.din("cdk", [L, c.PAST, c.DW])
        I["cdv"] = s.din("cdv", [L, c.PAST, c.DW])
        I["cckv"] = s.din("cckv", [L, c.PAST, 512])
        I["ckpe"] = s.din("ckpe", [L, c.PAST, 64])
        I["cpool"] = s.din("cpool", [L, 15, c.PW])
        I["cmk"] = s.din("cmk", [L, c.MEM, 512])
        I["cmv"] = s.din("cmv", [L, c.MEM, 512])
        I["memp"] = s.din("memp", [c.MEM, c.D])
        for nm in ("g_mix_pre", "g_mem", "g_x_pre", "g_ff_pre"):
            I[nm] = s.din(nm, [L, 128, c.KC])
        for nm in ("g_mix_post", "g_x_post", "g_ff_post"):
            I[nm] = s.din(nm, [L, 128, c.D])
        I["w_in"] = s.din("w_in", [L, c.D, c.INW])
        I["w_pool"] = s.din("w_pool", [L, 4, c.PG, c.PG])
        I["pool_scale"] = s.din("pool_scale", [L, 128, c.PCH])
        I["diff_lambda"] = s.din("diff_lambda", [L, 128, 256])
        I["g_diff_sub"] = s.din("g_diff_sub", [L, 128, 128])
        I["g_mla_q"] = s.din("g_mla_q", [L, 128, c.QRC])
        I["w_mla_uq"] = s.din("w_mla_uq", [L, c.QR, c.UQW])
        I["w_mla_uk"] = s.din("w_mla_uk", [L, 512, c.MH * 128])
        I["w_mla_uv"] = s.din("w_mla_uv", [L, 512, c.MH * 128])
        I["g_mla_kv"] = s.din("g_mla_kv", [L, 128, 512])
        I["w_out"] = s.din("w_out", [L, c.D, c.D])
        I["w_mem_k"] = s.din("w_mem_k", [L, c.D, 512])
        I["w_mem_v"] = s.din("w_mem_v", [L, c.D, 512])
        I["w_mem_q"] = s.din("w_mem_q", [L, c.D, 512])
        I["w_mem_o"] = s.din("w_mem_o", [L, 512, c.D])
        I["w_gate"] = s.din("w_gate", [L, c.D, c.DFF])
        I["w_up"] = s.din("w_up", [L, c.D, c.DFF])
        I["w_down"] = s.din("w_down", [L, c.DFF, c.D])
        I["rcos"] = s.din("rcos", [128, c.NB + 1, 32])
        I["rsin"] = s.din("rsin", [128, c.NB + 1, 32])
        I["rcnt"] = s.din("rcnt", [128, 4, 16])
        I["identb"] = s.din("identb", [128, 128], BF)
        I["identf"] = s.din("identf", [128, 128])
        s.I = I
        O = {}
        O["y_p"] = s.dout("y_p", [c.P, c.D])
        O["y_s"] = s.dout("y_s", [c.S, c.D])
        O["pdk"] = s.dout("pdk", [L, c.P, c.DW])
        O["pdv"] = s.dout("pdv", [L, c.P, c.DW])
        O["pckv"] = s.dout("pckv", [L, c.P, 512])
        O["pkpe"] = s.dout("pkpe", [L, c.P, 64])
        O["ppool"] = s.dout("ppool", [L, 15, c.PW])
        O["pmk"] = s.dout("pmk", [L, c.MEM, 512])
        O["pmv"] = s.dout("pmv", [L, c.MEM, 512])
        O["sdk"] = s.dout("sdk", [L, c.S, c.DW])
        O["sdv"] = s.dout("sdv", [L, c.S, c.DW])
        O["sckv"] = s.dout("sckv", [L, c.S, 512])
        O["skpe"] = s.dout("skpe", [L, c.S, 64])
        O["spool"] = s.dout("spool", [L, 15, c.PW])
        s.O = O
        X = {}
        X["xres"] = s.dscr("xres", [c.NT, c.D], F32)
        X["fs"] = s.dscr("fs", [c.NT, c.D], F32)
        X["hT"] = s.dscr("hT", [c.KC, 128, c.NT])
        X["mixT"] = s.dscr("mixT", [c.KC, 128, c.NT])
        X["dT"] = s.dscr("dT", [c.PCH, 128, c.NT])
        X["qT"] = s.dscr("qT", [c.DH, 128, c.NT])
        X["kT"] = s.dscr("kT", [c.DH, 128, c.P])
        X["kTs"] = s.dscr("kTs", [c.DH, 128, c.NKS])
        X["vv"] = s.dscr("vv", [c.P, c.DW])
        X["vs"] = s.dscr("vs", [c.NKS, c.DW])
        X["cq"] = s.dscr("cq", [c.NT, c.QR])
        X["ckvT"] = s.dscr("ckvT", [4, 128, c.P])
        X["ckvTs"] = s.dscr("ckvTs", [4, 128, c.NKS])
        X["kpeT"] = s.dscr("kpeT", [64, c.P])
        X["kpeTs"] = s.dscr("kpeTs", [64, c.NKS])
        X["knT"] = s.dscr("knT", [c.MH, 128, c.P])
        X["knTs"] = s.dscr("knTs", [c.MH, 128, c.NKS])
        X["vh"] = s.dscr("vh", [c.P, c.MH * 128])
        X["vhs"] = s.dscr("vhs", [c.NKS, c.MH * 128])
        X["qnT"] = s.dscr("qnT", [c.MH, 128, c.NT])
        X["qpT"] = s.dscr("qpT", [c.MH, 64, c.NT])
        X["mkT"] = s.dscr("mkT", [4, 128, c.MEM])
        X["mkTs"] = s.dscr("mkTs", [4, 128, c.MEM])
        X["mv"] = s.dscr("mv", [c.MEM, 512])
        X["mvs"] = s.dscr("mvs", [c.MEM, 512])
        X["actT"] = s.dscr("actT", [c.FC, 128, c.NT])
        s.X = X
        s.XR = {k: Res("x_" + k) for k in X}
        s.OR = Res("outputs")

        g = es
        s.ps_h = g.enter_context(nc.psum_tensor("ps", [128, 8, 512], F32))
        s.ps = s.ps_h
        s.psr = [Res("psb%d" % i) for i in range(8)]
        s._tb = 0
        s._mb = 0
        s.identb = s.sb(g, "identb", [128, 128], BF)
        s.identf = s.sb(g, "identf", [128, 128], F32)
        s.rcos = s.sb(g, "rcos", [128, c.NB + 1, 32], F32)
        s.rsin = s.sb(g, "rsin", [128, c.NB + 1, 32], F32)
        s.rcnt = s.sb(g, "rcnt", [128, 4, 16], F32)
        s.st = s.sb(g, "stats", [128, 16], F32)
        s.epsb = s.sb(g, "epsb", [128, 1], F32)
        s.dve(lambda e: e.memset(s.epsb[:], EPS), [], [s.epsb.r])
        s.load(s.identb[:], I["identb"], s.identb)
        s.load(s.identf[:], I["identf"], s.identf)
        s.load(s.rcos[:], I["rcos"], s.rcos)
        s.load(s.rsin[:], I["rsin"], s.rsin)
        s.load(s.rcnt[:], I["rcnt"], s.rcnt)

        ph = 0
        for l in range(L):
            steps = [
                lambda l=l: s.ph_memkv(l),
                lambda l=l: s.ph_norm(l, 0),
                lambda l=l: s.ph_pool(l),
                lambda l=l: s.ph_inproj(l),
                lambda l=l: s.ph_cache(l),
                lambda l=l: s.ph_mlaproj(l),
                lambda l=l: s.ph_diffattn(l),
                lambda l=l: s.ph_mlaattn(l),
                lambda l=l: s.ph_outproj(l),
                lambda l=l: s.ph_norm(l, 1),
                lambda l=l: s.ph_cross(l),
                lambda l=l: s.ph_norm(l, 2),
                lambda l=l: s.ph_ffn1(l),
                lambda l=l: s.ph_ffn2(l),
            ]
            for st in steps:
                if ph < s.upto:
                    st()
                    s.sch.barrier()
                ph += 1
        if ph < s.upto + 1:
            s.ph_norm(L, 0, final=True)
        s.sch.finish()
        es.close()
        return nc

    def ph_norm(s, l, which, final=False):
        c, I, X, O = s.c, s.I, s.X, s.O
        with s.scope() as st:
            has_f = not (l == 0 and which == 0)
            if which == 0:
                gpost = I["g_ff_post"][l - 1] if has_f else None
                gpre = None if final else I["g_mix_pre"][l]
            elif which == 1:
                gpost, gpre = I["g_mix_post"][l], I["g_x_pre"][l]
            else:
                gpost, gpre = I["g_x_post"][l], I["g_ff_pre"][l]
            xb = [s.sb(st, "nx", [128, c.D], F32) for _ in range(2)]
            fb = [s.sb(st, "nf", [128, c.D], F32) for _ in range(2)] if has_f else None
            junk = s.sb(st, "njunk", [128, c.D], BF)
            G = s.sb(st, "nG", [128, c.D], F32) if has_f else None
            gc = s.sb(st, "ngc", [128, c.KC], F32) if gpre is not None else None
            hb = [s.sb(st, "nhb", [128, c.D], BF) for _ in range(2)] if gpre is not None else None
            hs = [s.sb(st, "nhs", [128, c.KC, 512], BF) for _ in range(2)] if gpre is not None else None
            stt = s.sb(st, "nst", [128, 8], F32)
            if has_f:
                s.load(G[:], gpost, G)
            if gpre is not None:
                s.load(gc[:], gpre, gc)
            first = (l == 0 and which == 0)
            hslot = -1
            hcount = 0
            for bi, (c0, nt) in enumerate(c.tbs):
                sl = bi % 2
                x, = (xb[sl],)
                if first:
                    src = I["xp"][c0:c0 + nt, :] if c0 < c.P else I["xs"][:, :]
                    s.load(x[0:nt, :], src, x)
                else:
                    s.load(x[0:nt, :], X["xres"][c0:c0 + nt, :], x, [s.XR["xres"]])
                if has_f:
                    f = fb[sl]
                    s.load(f[0:nt, :], X["fs"][c0:c0 + nt, :], f, [s.XR["fs"]])
                    s.act(junk[0:nt, :], f[0:nt, :], AF.Square, [f.r], [junk.r, stt.r], accum_out=stt[0:nt, 0:1])
                    s.rstd(stt[0:nt, 1:2], stt[0:nt, 0:1], nt, c.D, [], [stt.r])
                    s.dve(lambda e, f=f, nt=nt: e.scalar_tensor_tensor(
                        out=f[0:nt, :], in0=f[0:nt, :], scalar=stt[0:nt, 1:2], in1=G[0:nt, :],
                        op0=ALU.mult, op1=ALU.mult), [stt.r, G.r], [f.r])
                    s.pool(lambda e, f=f, x=x, nt=nt: e.tensor_tensor(
                        out=x[0:nt, :], in0=x[0:nt, :], in1=f[0:nt, :], op=ALU.add), [f.r], [x.r])
                    if final:
                        dst = O["y_p"][c0:c0 + nt, :] if c0 < c.P else O["y_s"][:, :]
                        s.store(dst, x[0:nt, :], x, [s.OR])
                    else:
                        s.store(X["xres"][c0:c0 + nt, :], x[0:nt, :], x, [s.XR["xres"]])
                elif first:
                    s.store(X["xres"][c0:c0 + nt, :], x[0:nt, :], x, [s.XR["xres"]])
                if gpre is None:
                    continue
                h = hb[sl]
                s.act(junk[0:nt, :], x[0:nt, :], AF.Square, [x.r], [junk.r, stt.r], accum_out=stt[0:nt, 2:3])
                s.rstd(stt[0:nt, 3:4], stt[0:nt, 2:3], nt, c.D, [], [stt.r])
                s.act(h[0:nt, :], x[0:nt, :], AF.Copy, [x.r, stt.r], [h.r], scale=stt[0:nt, 3:4])
                if hcount % 4 == 0 or nt != 128:
                    hslot += 1
                    hcol0 = c0
                    hoff = 0
                hst = hs[hslot % 2]

                def evac(j0, cnt, pv, bank, hst=hst, hoff=hoff, nt=nt):
                    s.dve(lambda e: e.tensor_tensor(
                        out=hst[:, j0:j0 + cnt, hoff:hoff + nt], in0=pv,
                        in1=gc[:, j0:j0 + cnt, None].broadcast_to([128, cnt, nt]), op=ALU.mult),
                        [bank, gc.r], [hst.r])
                s.transposes(h, lambda j, h=h, nt=nt: h[0:nt, j * 128:(j + 1) * 128], c.KC, nt, evac)
                hoff += nt
                hcount += 1
                last = (bi + 1 == len(c.tbs)) or (hcount % 4 == 0) or (c.tbs[bi + 1][1] != 128)
                if last:
                    s.store(X["hT"].rearrange("k p t -> p k t")[:, :, hcol0:hcol0 + hoff],
                            hst[:, :, 0:hoff], hst, [s.XR["hT"]])
                    hcount = 0

    def load_xT(s, st, name, key, col0=0, ncol=None):
        src = s.X[key]
        nch = src.shape[0]
        ncol = src.shape[2] - col0 if ncol is None else ncol
        t = s.sb(st, name, [128, nch, ncol], BF)
        step = max(1, nch // 4)
        for k0 in range(0, nch, step):
            k1 = min(nch, k0 + step)
            s.load(t[:, k0:k1, :], src.rearrange("k p t -> p k t")[:, k0:k1, col0:col0 + ncol], t, [s.XR[key]])
        return t

    def load_slab(s, slab, w_ap, c0, cw, kc0=0, kcn=None):
        kcn = (w_ap.shape[0] // 128 - kc0) if kcn is None else kcn
        src = w_ap[kc0 * 128:(kc0 + kcn) * 128, c0:c0 + cw].rearrange("(k p) n -> p k n", p=128)
        step = max(1, kcn // 2)
        for k0 in range(0, kcn, step):
            k1 = min(kcn, k0 + step)
            s.load(slab[:, k0:k1, 0:cw], src[:, k0:k1, :], slab, q="pool")

    def ph_memkv(s, l):
        c, I, X, O = s.c, s.I, s.X, s.O
        nmb = c.MEM // 128
        with s.scope() as st:
            xb = [s.sb(st, "mx", [128, c.D], F32) for _ in range(2)]
            junk = s.sb(st, "mjunk", [128, c.D], BF)
            hb = s.sb(st, "mhb", [128, c.D], BF)
            gc = s.sb(st, "mgc", [128, c.KC], F32)
            mT = s.sb(st, "mT", [128, c.KC, c.MEM], BF)
            slab = [s.sb(st, "mslab", [128, c.KC, 512], BF) for _ in range(2)]
            o32 = [s.sb(st, "mo32", [128, 512], F32) for _ in range(2)]
            ob = [s.sb(st, "mob", [128, 512], BF) for _ in range(2)]
            kts = s.sb(st, "mkts", [128, 4, c.MEM], BF)
            stt = s.sb(st, "mst", [128, 4], F32)
            s.load(gc[:], I["g_mem"][l], gc)
            s.load_slab(slab[0], I["w_mem_k"][l], 0, 512)
            s.load_slab(slab[1], I["w_mem_v"][l], 0, 512)
            for mb in range(nmb):
                x = xb[mb % 2]
                s.load(x[:], I["memp"][mb * 128:(mb + 1) * 128, :], x)
                s.act(junk[:], x[:], AF.Square, [x.r], [junk.r, stt.r], accum_out=stt[:, 0:1])
                s.rstd(stt[:, 1:2], stt[:, 0:1], 128, c.D, [], [stt.r])
                s.act(hb[:], x[:], AF.Copy, [x.r, stt.r], [hb.r], scale=stt[:, 1:2])

                def evac(j0, cnt, pv, bank, mb=mb):
                    s.dve(lambda e: e.tensor_tensor(
                        out=mT[:, j0:j0 + cnt, mb * 128:(mb + 1) * 128], in0=pv,
                        in1=gc[:, j0:j0 + cnt, None].broadcast_to([128, cnt, 128]), op=ALU.mult),
                        [bank, gc.r], [mT.r])
                s.transposes(hb, lambda j: hb[:, j * 128:(j + 1) * 128], c.KC, 128, evac)

            def kpost(obt, mb, dst_key, final):
                def ev2(j0, cnt, pv, bank):
                    s.dve(lambda e: e.tensor_copy(out=kts[:, j0:j0 + cnt, mb * 128:(mb + 1) * 128], in_=pv),
                          [bank], [kts.r])
                s.transposes(obt, lambda j: obt[:, j * 128:(j + 1) * 128], 4, 128, ev2)
                if final:
                    s.store(X[dst_key].rearrange("h p m -> p h m"), kts[:], kts, [s.XR[dst_key]])

            for wi in range(2):
                for mb in range(nmb):
                    bk = s.mbank()

                    def emit(pe, wi=wi, mb=mb, bk=bk):
                        ins = None
                        for k in range(c.KC):
                            ins = pe.matmul(s.ps[:, bk, :], lhsT=mT[:, k, mb * 128:(mb + 1) * 128],
                                            rhs=slab[wi][:, k, :], start=(k == 0), stop=(k == c.KC - 1))
                        return ins
                    s.mm_group(emit, [mT.r, slab[wi].r], s.psr[bk])
                    o, b = o32[mb % 2], ob[mb % 2]
                    s.act(o[:], s.ps[:, bk, :], AF.Copy, [s.psr[bk]], [o.r])
                    s.dve(lambda e, o=o, b=b: e.tensor_copy(out=b[:], in_=o[:]), [o.r], [b.r])
                    okey = "pmk" if wi == 0 else "pmv"
                    s.store(O[okey][l, mb * 128:(mb + 1) * 128, :], o[:], o, [s.OR])
                    if wi == 0:
                        kpost(b, mb, "mkT", mb == nmb - 1)
                    else:
                        s.store(X["mv"][mb * 128:(mb + 1) * 128, :], b[:], b, [s.XR["mv"]])
            for mb in range(nmb):
                b = ob[mb % 2]
                s.load(b[:], I["cmk"][l, mb * 128:(mb + 1) * 128, :], b, q="pool")
                kpost(b, mb, "mkTs", mb == nmb - 1)
            for mb in range(nmb):
                b = ob[mb % 2]
                s.load(b[:], I["cmv"][l, mb * 128:(mb + 1) * 128, :], b, q="pool")
                s.store(X["mvs"][mb * 128:(mb + 1) * 128, :], b[:], b, [s.XR["mvs"]])

    def ph_pool(s, l):
        c, I, X, O = s.c, s.I, s.X, s.O
        EP = 15 + c.P
        ES = 15 + c.S
        with s.scope() as st:
            hT = s.load_xT(st, "hTr", "hT")
            slab = [s.sb(st, "pslab", [128, c.KC, 128], BF) for _ in range(2)]
            ue = s.sb(st, "ue", [128, EP + ES], F32)
            A = s.sb(st, "pA", [128, EP + ES], F32)
            B = s.sb(st, "pB", [128, EP + ES], F32)
            dst = [s.sb(st, "pd", [128, c.NT], BF) for _ in range(2)]
            cp = s.sb(st, "cp", [15, c.PW], F32)
            po = s.sb(st, "po", [15, 2, c.PW], F32)
            s.load(cp[:], I["cpool"][l], cp)
            s.dve(lambda e: e.memset(ue[:, 0:15], 0.0), [], [ue.r])
            s.dve(lambda e: e.memset(A[:], 0.0), [], [A.r])
            s.pool(lambda e: e.memset(B[:], 0.0), [], [B.r])
            s.load_slab(slab[0], I["w_in"][l], 0, 128)
            for ch in range(c.PCH):
                sl = slab[ch % 2]
                if ch + 1 < c.PCH:
                    s.load_slab(slab[(ch + 1) % 2], I["w_in"][l], (ch + 1) * 128, 128)

                def evc(j0, cnt, pv, bank):
                    s.dve(lambda e: e.tensor_copy(out=ue[:, EP:EP + 15], in_=pv[:, 0, :]), [bank], [ue.r])
                s.transposes(cp, lambda j, ch=ch: cp[0:15, ch * 128:(ch + 1) * 128], 1, 15, evc, bf=False)
                for (c0, w) in c.tgs:
                    bk = s.mbank()

                    def emit(pe, sl=sl, c0=c0, w=w, bk=bk):
                        ins = None
                        for k in range(c.KC):
                            ins = pe.matmul(s.ps[:, bk, 0:w], lhsT=sl[:, k, :], rhs=hT[:, k, c0:c0 + w],
                                            start=(k == 0), stop=(k == c.KC - 1))
                        return ins
                    s.mm_group(emit, [sl.r, hT.r], s.psr[bk])
                    e0 = 15 + c0 if c0 < c.P else EP + 15
                    s.act(ue[:, e0:e0 + w], s.ps[:, bk, 0:w], AF.Copy, [s.psr[bk]], [ue.r])
                for gi, e0 in ((0, EP - 15), (1, EP + ES - 15)):
                    def evp(j0, cnt, pv, bank, gi=gi, ch=ch):
                        s.dve(lambda e: e.tensor_copy(out=po[0:15, gi, ch * 128:(ch + 1) * 128], in_=pv[:, 0, :]),
                              [bank], [po.r])
                    s.transposes(ue, lambda j, e0=e0: ue[:, e0:e0 + 15], 1, 128, evp, bf=False, ncols=15)
                g = ch // c.CPG
                cur, bufs = ue, [A, B]
                sh = 1
                for lev in range(g + 1):
                    nxt = bufs[lev % 2]
                    s.dve(lambda e, cur=cur, nxt=nxt, sh=sh: e.tensor_tensor(
                        out=nxt[:, sh:EP + ES], in0=cur[:, sh:EP + ES], in1=cur[:, 0:EP + ES - sh], op=ALU.add),
                        [cur.r], [nxt.r])
                    cur = nxt
                    sh *= 2
                wdw = 2 ** (g + 1)
                other = bufs[(g + 1) % 2]
                d = dst[ch % 2]
                s.dve(lambda e, cur=cur, other=other, wdw=wdw: e.scalar_tensor_tensor(
                    out=other[:, 15:EP + ES], in0=cur[:, 15:EP + ES], scalar=1.0 / wdw, in1=ue[:, 15:EP + ES],
                    op0=ALU.mult, op1=ALU.subtract), [cur.r, ue.r], [other.r])
                s.dve(lambda e, cur=cur, g=g: e.tensor_tensor(
                    out=cur[:, 15:30], in0=cur[:, 15:30], in1=s.rcnt[:, g, 0:15], op=ALU.mult), [s.rcnt.r], [cur.r])
                s.dve(lambda e, cur=cur, other=other: e.tensor_tensor(
                    out=other[:, 15:30], in0=cur[:, 15:30], in1=ue[:, 15:30], op=ALU.subtract),
                    [cur.r, ue.r], [other.r])
                s.act(d[:, 0:c.P], other[:, 15:EP], AF.Copy, [other.r], [d.r])
                s.act(d[:, c.P:c.NT], other[:, EP + 15:EP + ES], AF.Copy, [other.r], [d.r])
                s.store(X["dT"][ch], d[:], d, [s.XR["dT"]])
            s.store(O["ppool"][l], po[0:15, 0, :], po, [s.OR])
            s.store(O["spool"][l], po[0:15, 1, :], po, [s.OR])
        with s.scope() as st:
            dT = s.load_xT(st, "dTr", "dT")
            wp = s.sb(st, "wp", [128, 4, c.CPG, c.PG], BF)
            psc = s.sb(st, "psc", [128, c.PCH], F32)
            ms = [s.sb(st, "pms", [128, c.NT], BF) for _ in range(2)]
            s.load(wp[:], I["w_pool"][l].rearrange("g (cc p) e -> p g cc e", p=128), wp, q="pool")
            s.load(psc[:], I["pool_scale"][l], psc)
            for g in range(4):
                for ec in range(c.CPG):
                    och = g * c.CPG + ec
                    m = ms[och % 2]
                    for (c0, w) in c.tgs:
                        bk = s.mbank()

                        def emit(pe, g=g, ec=ec, c0=c0, w=w, bk=bk):
                            ins = None
                            for cc in range(c.CPG):
                                ins = pe.matmul(s.ps[:, bk, 0:w], lhsT=wp[:, g, cc, ec * 128:(ec + 1) * 128],
                                                rhs=dT[:, g * c.CPG + cc, c0:c0 + w],
                                                start=(cc == 0), stop=(cc == c.CPG - 1))
                            return ins
                        s.mm_group(emit, [wp.r, dT.r], s.psr[bk])
                        s.act(m[:, c0:c0 + w], s.ps[:, bk, 0:w], AF.Copy, [s.psr[bk], psc.r], [m.r],
                              scale=psc[:, och:och + 1])
                    s.store(X["mixT"][och], m[:], m, [s.XR["mixT"]])

    def rope_tm(s, xs, ro, nt, W, bi):
        U = W // 64
        X4 = xs[0:nt, 0:W].rearrange("p (u h f) -> p u h f", h=2, f=32)
        R4 = ro[0:nt, 0:W].rearrange("p (u h f) -> p u h f", h=2, f=32)
        cs = s.rcos[0:nt, bi, None, :].broadcast_to([nt, U, 32])
        sn = s.rsin[0:nt, bi, None, :].broadcast_to([nt, U, 32])
        x1, x2 = X4[:, :, 0, :], X4[:, :, 1, :]
        t = s.ropet
        T4 = t[0:nt, 0:W].rearrange("p (u h f) -> p u h f", h=2, f=32)
        ta, tb = T4[:, :, 0, :], T4[:, :, 1, :]
        s.dve(lambda e: e.tensor_tensor(out=ta, in0=x2, in1=sn, op=ALU.mult), [xs.r, s.rsin.r], [t.r])
        s.dve(lambda e: e.tensor_tensor(out=R4[:, :, 0, :], in0=x1, in1=cs, op=ALU.mult), [xs.r, s.rcos.r], [ro.r])
        s.dve(lambda e: e.tensor_tensor(out=R4[:, :, 0, :], in0=R4[:, :, 0, :], in1=ta, op=ALU.subtract),
              [t.r], [ro.r])
        s.pool(lambda e: e.tensor_tensor(out=tb, in0=x1, in1=sn, op=ALU.mult), [xs.r, s.rsin.r], [t.r])
        s.pool(lambda e: e.tensor_tensor(out=R4[:, :, 1, :], in0=x2, in1=cs, op=ALU.mult), [xs.r, s.rcos.r], [ro.r])
        s.pool(lambda e: e.tensor_tensor(out=R4[:, :, 1, :], in0=R4[:, :, 1, :], in1=tb, op=ALU.add),
               [t.r], [ro.r])

    def ph_inproj(s, l):
        c, I, X, O = s.c, s.I, s.X, s.O
        W = I["w_in"][l]
        groups = []
        off = c.PW
        for kind, width in (("q", c.DW), ("k", c.DW), ("v", c.DW), ("cq", c.QR), ("ckv", 512), ("kpe", 64)):
            o = 0
            while o < width:
                gw = min(512, width - o)
                groups.append((kind, o, off + o, gw))
                o += gw
            off += width
        nbh = (c.NB + 1) // 2
        halves = [c.tbs[0:nbh], c.tbs[nbh:]] if c.NB > 1 else [c.tbs]
        for hv in halves:
            s.inproj_half(l, groups, hv)
            s.sch.barrier()

    def inproj_half(s, l, groups, tbl):
        c, I, X, O = s.c, s.I, s.X, s.O
        W = I["w_in"][l]
        hc0 = tbl[0][0]
        hcn = sum(nt for _, nt in tbl)
        bi0 = c.tbs.index(tbl[0])
        with s.scope() as st:
            hT = s.load_xT(st, "hTr", "hT", hc0, hcn)
            slab = [s.sb(st, "islab", [128, c.KC, 512], BF) for _ in range(2)]
            x32 = [s.sb(st, "ix32", [128, 512], F32) for _ in range(2)]
            r32 = [s.sb(st, "ir32", [128, 512], F32) for _ in range(2)]
            rb = [s.sb(st, "irb", [128, 512], BF) for _ in range(2)]
            s.ropet = s.sb(st, "ropet", [128, 512], F32)
            tst = [s.sb(st, "itst", [128, 4, 128], BF) for _ in range(2)]
            gkv = s.sb(st, "gkv", [128, 512], F32)
            junk = s.sb(st, "ijunk", [128, 512], F32)
            stt = s.sb(st, "ist", [128, 4], F32)
            s.load(gkv[:], I["g_mla_kv"][l], gkv)
            s.load_slab(slab[0], W, groups[0][2], groups[0][3])
            it = 0
            for gi, (kind, o, wc0, gw) in enumerate(groups):
                sl = slab[gi % 2]
                if gi + 1 < len(groups):
                    s.load_slab(slab[(gi + 1) % 2], W, groups[gi + 1][2], groups[gi + 1][3])
                tslot = 0
                for bi_, (c0, nt) in enumerate(tbl):
                    bi = bi0 + bi_
                    bk = s.mbank()
                    samp = c0 >= c.P

                    def emit(pe, sl=sl, c0=c0, nt=nt, bk=bk, gw=gw):
                        ins = None
                        for k in range(c.KC):
                            ins = pe.matmul(s.ps[0:nt, bk, 0:gw], lhsT=hT[:, k, c0 - hc0:c0 - hc0 + nt], rhs=sl[:, k, 0:gw],
                                            start=(k == 0), stop=(k == c.KC - 1))
                        return ins
                    s.mm_group(emit, [sl.r, hT.r], s.psr[bk])
                    it += 1
                    xs, ro, b = x32[it % 2], r32[it % 2], rb[it % 2]
                    pv = s.ps[0:nt, bk, 0:gw]
                    nh = gw // 128 if gw >= 128 else 1
                    tw = 128 if gw >= 128 else gw

                    def tr_store(b, key, skey, hb0):
                        nonlocal tslot
                        tslot += 1
                        ts = tst[tslot % 2]

                        def ev(j0, cnt, pv2, bank):
                            s.act(ts[0:tw, j0:j0 + cnt, 0:nt], pv2, AF.Copy, [bank], [ts.r])
                        s.transposes(b, lambda j: b[0:nt, j * tw:(j + 1) * tw], nh, nt, ev, ncols=tw)
                        kk = skey if samp else key
                        cc = c.PAST if (samp and skey != key) else c0
                        if len(X[kk].shape) == 3:
                            dstap = X[kk][hb0:hb0 + nh, 0:tw, cc:cc + nt].rearrange("h p t -> p h t")
                            srcap = ts[0:tw, 0:nh, 0:nt]
                        else:
                            dstap = X[kk][0:tw, cc:cc + nt]
                            srcap = ts[0:tw, 0, 0:nt]
                        s.store(dstap, srcap, ts, [s.XR[kk]])

                    if kind in ("q", "k", "kpe"):
                        s.act(xs[0:nt, 0:gw], pv, AF.Copy, [s.psr[bk]], [xs.r])
                        s.rope_tm(xs, ro, nt, gw, bi)
                        s.act(b[0:nt, 0:gw], ro[0:nt, 0:gw], AF.Copy, [ro.r], [b.r])
                        if kind == "k":
                            dst = (O["sdk"][l, :, o:o + gw] if samp else O["pdk"][l, c0:c0 + nt, o:o + gw])
                            s.store(dst, ro[0:nt, 0:gw], ro, [s.OR])
                            tr_store(b, "kT", "kTs", o // 128)
                        elif kind == "kpe":
                            dst = (O["skpe"][l, :, :] if samp else O["pkpe"][l, c0:c0 + nt, :])
                            s.store(dst, ro[0:nt, 0:gw], ro, [s.OR])
                            tr_store(b, "kpeT", "kpeTs", 0)
                        else:
                            tr_store(b, "qT", "qT", o // 128)
                    elif kind == "v":
                        s.act(ro[0:nt, 0:gw], pv, AF.Copy, [s.psr[bk]], [ro.r])
                        s.dve(lambda e, ro=ro, b=b, nt=nt, gw=gw: e.tensor_copy(out=b[0:nt, 0:gw], in_=ro[0:nt, 0:gw]),
                              [ro.r], [b.r])
                        dst = (O["sdv"][l, :, o:o + gw] if samp else O["pdv"][l, c0:c0 + nt, o:o + gw])
                        s.store(dst, ro[0:nt, 0:gw], ro, [s.OR])
                        if samp:
                            s.store(X["vs"][c.PAST:c.PAST + nt, o:o + gw], b[0:nt, 0:gw], b, [s.XR["vs"]])
                        else:
                            s.store(X["vv"][c0:c0 + nt, o:o + gw], b[0:nt, 0:gw], b, [s.XR["vv"]])
                    elif kind == "cq":
                        s.act(b[0:nt, 0:gw], pv, AF.Copy, [s.psr[bk]], [b.r])
                        s.store(X["cq"][c0:c0 + nt, o:o + gw], b[0:nt, 0:gw], b, [s.XR["cq"]])
                    elif kind == "ckv":
                        s.act(junk[0:nt, :], pv, AF.Square, [s.psr[bk]], [junk.r, stt.r], accum_out=stt[0:nt, 0:1])
                        s.rstd(stt[0:nt, 1:2], stt[0:nt, 0:1], nt, 512, [], [stt.r])
                        s.dve(lambda e, ro=ro, pv=pv, nt=nt: e.scalar_tensor_tensor(
                            out=ro[0:nt, :], in0=pv, scalar=stt[0:nt, 1:2], in1=gkv[0:nt, :],
                            op0=ALU.mult, op1=ALU.mult), [s.psr[bk], stt.r, gkv.r], [ro.r])
                        s.act(b[0:nt, :], ro[0:nt, :], AF.Copy, [ro.r], [b.r])
                        dst = (O["sckv"][l, :, :] if samp else O["pckv"][l, c0:c0 + nt, :])
                        s.store(dst, ro[0:nt, :], ro, [s.OR])
                        tr_store(b, "ckvT", "ckvTs", 0)

    def ph_cache(s, l):
        c, I, X = s.c, s.I, s.X
        nkb = c.PAST // 128
        with s.scope() as st:
            cb = [s.sb(st, "ccb", [128, 4, c.DW], BF) for _ in range(2)]
            ts = [s.sb(st, "cts", [128, c.DH, 512], BF) for _ in range(2)]
            for key, skey, width, vkey in (("cdk", "kTs", c.DW, None), ("cdv", None, c.DW, "vs"),
                                           ("cckv", "ckvTs", 512, None), ("ckpe", "kpeTs", 64, None)):
                nh = max(1, width // 128)
                tw = min(128, width)
                for q0 in range(0, nkb, 4):
                    nq = min(4, nkb - q0)
                    b = cb[(q0 // 4) % 2]
                    t = ts[(q0 // 4) % 2]
                    s.load(b[:, 0:nq, 0:width],
                           I[key][l, q0 * 128:(q0 + nq) * 128, :].rearrange("(q p) w -> p q w", p=128), b, q="pool")
                    if vkey is not None:
                        s.store(X[vkey][q0 * 128:(q0 + nq) * 128, :].rearrange("(q p) w -> p q w", p=128),
                                b[:, 0:nq, 0:width], b, [s.XR[vkey]])
                        continue
                    for qq in range(nq):
                        def ev(j0, cnt, pv, bank, qq=qq, t=t):
                            s.dve(lambda e: e.tensor_copy(out=t[0:tw, j0:j0 + cnt, qq * 128:(qq + 1) * 128], in_=pv),
                                  [bank], [t.r])
                        s.transposes(b, lambda j, qq=qq, b=b: b[:, qq, j * tw:(j + 1) * tw], nh, 128, ev, ncols=tw)
                    if len(X[skey].shape) == 3:
                        s.store(X[skey][:, :, q0 * 128:(q0 + nq) * 128].rearrange("h p t -> p h t"),
                                t[0:tw, 0:nh, 0:nq * 128], t, [s.XR[skey]])
                    else:
                        s.store(X[skey][:, q0 * 128:(q0 + nq) * 128], t[0:tw, 0, 0:nq * 128], t, [s.XR[skey]])

    def ph_mlaproj(s, l):
        c, I, X = s.c, s.I, s.X
        with s.scope() as st:
            cqnT = s.sb(st, "cqnT", [128, c.QRC, c.NT], BF)
            wuq = s.sb(st, "wuq", [128, c.QRC, c.UQW], BF)
            gq = s.sb(st, "gq", [128, c.QRC], F32)
            s.load(gq[:], I["g_mla_q"][l], gq)
            nsp = 4
            step = c.UQW // nsp
            for i in range(nsp):
                s.load(wuq[:, :, i * step:(i + 1) * step],
                       I["w_mla_uq"][l][:, i * step:(i + 1) * step].rearrange("(k p) n -> p k n", p=128), wuq, q="pool")
            with s.scope() as st2:
                cb = [s.sb(st2, "cqb", [128, c.QR], BF) for _ in range(2)]
                cn = [s.sb(st2, "cqn", [128, c.QR], BF) for _ in range(2)]
                junk = s.sb(st2, "cqj", [128, c.QR], BF)
                stt = s.sb(st2, "cqs", [128, 4], F32)
                for bi, (c0, nt) in enumerate(c.tbs):
                    b, n = cb[bi % 2], cn[bi % 2]
                    s.load(b[0:nt, :], X["cq"][c0:c0 + nt, :], b, [s.XR["cq"]])
                    s.act(junk[0:nt, :], b[0:nt, :], AF.Square, [b.r], [junk.r, stt.r], accum_out=stt[0:nt, 0:1])
                    s.rstd(stt[0:nt, 1:2], stt[0:nt, 0:1], nt, c.QR, [], [stt.r])
                    s.act(n[0:nt, :], b[0:nt, :], AF.Copy, [b.r, stt.r], [n.r], scale=stt[0:nt, 1:2])

                    def ev(j0, cnt, pv, bank, c0=c0, nt=nt):
                        s.dve(lambda e: e.tensor_tensor(
                            out=cqnT[:, j0:j0 + cnt, c0:c0 + nt], in0=pv,
                            in1=gq[:, j0:j0 + cnt, None].broadcast_to([128, cnt, nt]), op=ALU.mult),
                            [bank, gq.r], [cqnT.r])
                    s.transposes(n, lambda j, n=n, nt=nt: n[0:nt, j * 128:(j + 1) * 128], c.QRC, nt, ev)
            with s.scope() as st2:
                qs = [s.sb(st2, "qns", [128, c.NT], BF) for _ in range(2)]
                for h in range(c.MH):
                    q = qs[h % 2]
                    for ti, (c0, w) in enumerate(c.tgs):
                        bk = s.mbank()

                        def emit(pe, h=h, c0=c0, w=w, bk=bk):
                            ins = None
                            for k in range(c.QRC):
                                ins = pe.matmul(s.ps[:, bk, 0:w], lhsT=wuq[:, k, h * 192:h * 192 + 128],
                                                rhs=cqnT[:, k, c0:c0 + w], start=(k == 0), stop=(k == c.QRC - 1))
                            return ins
                        s.mm_group(emit, [wuq.r, cqnT.r], s.psr[bk])
                        if ti % 2 == 0:
                            s.act(q[:, c0:c0 + w], s.ps[:, bk, 0:w], AF.Copy, [s.psr[bk]], [q.r])
                        else:
                            s.dve(lambda e, q=q, c0=c0, w=w, bk=bk: e.tensor_copy(out=q[:, c0:c0 + w], in_=s.ps[:, bk, 0:w]),
                                  [s.psr[bk]], [q.r])
                    s.store(X["qnT"][h], q[:], q, [s.XR["qnT"]])
                x32 = [s.sb(st2, "qx32", [128, 512], F32) for _ in range(2)]
                r32 = [s.sb(st2, "qr32", [128, 512], F32) for _ in range(2)]
                rb = [s.sb(st2, "qrb", [128, 512], BF) for _ in range(2)]
                s.ropet = s.sb(st2, "ropet2", [128, 512], F32)
                tst = [s.sb(st2, "qtst", [64, 8, 128], BF) for _ in range(2)]
                wv = wuq[:, :, :].rearrange("p k (h c) -> p k h c", c=192)
                it = 0
                for hg0 in range(0, c.MH, 8):
                    nh = min(8, c.MH - hg0)
                    gw = nh * 64
                    for bi, (c0, nt) in enumerate(c.tbs):
                        bk = s.mbank()

                        def emit(pe, hg0=hg0, nh=nh, c0=c0, nt=nt, bk=bk, gw=gw):
                            ins = None
                            for k in range(c.QRC):
                                ins = pe.matmul(s.ps[0:nt, bk, 0:gw].rearrange("p (h c) -> p h c", c=64),
                                                lhsT=cqnT[:, k, c0:c0 + nt], rhs=wv[:, k, hg0:hg0 + nh, 128:192],
                                                start=(k == 0), stop=(k == c.QRC - 1))
                            return ins
                        s.mm_group(emit, [wuq.r, cqnT.r], s.psr[bk])
                        it += 1
                        xs, ro, b, ts = x32[it % 2], r32[it % 2], rb[it % 2], tst[it % 2]
                        s.act(xs[0:nt, 0:gw], s.ps[0:nt, bk, 0:gw], AF.Copy, [s.psr[bk]], [xs.r])
                        s.rope_tm(xs, ro, nt, gw, bi)
                        s.act(b[0:nt, 0:gw], ro[0:nt, 0:gw], AF.Copy, [ro.r], [b.r])

                        def ev(j0, cnt, pv2, bank, ts=ts, nt=nt):
                            s.dve(lambda e: e.tensor_copy(out=ts[0:64, j0:j0 + cnt, 0:nt], in_=pv2), [bank], [ts.r])
                        s.transposes(b, lambda j, b=b, nt=nt: b[0:nt, j * 64:(j + 1) * 64], nh, nt, ev, ncols=64)
                        s.store(X["qpT"][hg0:hg0 + nh, :, c0:c0 + nt].rearrange("h p t -> p h t"),
                                ts[0:64, 0:nh, 0:nt], ts, [s.XR["qpT"]])
        with s.scope() as st:
            wuk = s.sb(st, "wuk", [128, 4, c.MH * 128], BF)
            wuv = s.sb(st, "wuv", [128, 4, c.MH * 128], BF)
            s.load(wuk[:], I["w_mla_uk"][l].rearrange("(k p) n -> p k n", p=128), wuk, q="pool")
            s.load(wuv[:], I["w_mla_uv"][l].rearrange("(k p) n -> p k n", p=128), wuv, q="pool")
            for ckey, nkey, vkey, nk in (("ckvT", "knT", "vh", c.P), ("ckvTs", "knTs", "vhs", c.NKS)):
                with s.scope() as st2:
                    ck = s.load_xT(st2, "ckr", ckey)
                    ks = [s.sb(st2, "kns", [128, nk], BF) for _ in range(2)]
                    vs_ = [s.sb(st2, "vhs", [128, c.MH * 128], BF) for _ in range(2)]
                    it = 0
                    for h in range(c.MH):
                        kk = ks[h % 2]
                        for k0 in range(0, nk, 512):
                            w = min(512, nk - k0)
                            bk = s.mbank()

                            def emit(pe, h=h, k0=k0, w=w, bk=bk):
                                ins = None
                                for r in range(4):
                                    ins = pe.matmul(s.ps[:, bk, 0:w], lhsT=wuk[:, r, h * 128:(h + 1) * 128],
                                                    rhs=ck[:, r, k0:k0 + w], start=(r == 0), stop=(r == 3))
                                return ins
                            s.mm_group(emit, [wuk.r, ck.r], s.psr[bk])
                            it += 1
                            if it % 2:
                                s.act(kk[:, k0:k0 + w], s.ps[:, bk, 0:w], AF.Copy, [s.psr[bk]], [kk.r])
                            else:
                                s.dve(lambda e, kk=kk, k0=k0, w=w, bk=bk: e.tensor_copy(out=kk[:, k0:k0 + w], in_=s.ps[:, bk, 0:w]),
                                      [s.psr[bk]], [kk.r])
                        s.store(X[nkey][h], kk[:], kk, [s.XR[nkey]])
                    for kb in range((nk + 127) // 128):
                        nr = min(128, nk - kb * 128)
                        vv_ = vs_[kb % 2]
                        for cg in range(0, c.MH * 128, 512):
                            bk = s.mbank()

                            def emit(pe, kb=kb, nr=nr, cg=cg, bk=bk):
                                ins = None
                                for r in range(4):
                                    ins = pe.matmul(s.ps[0:nr, bk, :], lhsT=ck[:, r, kb * 128:kb * 128 + nr],
                                                    rhs=wuv[:, r, cg:cg + 512], start=(r == 0), stop=(r == 3))
                                return ins
                            s.mm_group(emit, [wuv.r, ck.r], s.psr[bk])
                            it += 1
                            if it % 2:
                                s.act(vv_[0:nr, cg:cg + 512], s.ps[0:nr, bk, :], AF.Copy, [s.psr[bk]], [vv_.r])
                            else:
                                s.dve(lambda e, vv_=vv_, nr=nr, cg=cg, bk=bk: e.tensor_copy(
                                    out=vv_[0:nr, cg:cg + 512], in_=s.ps[0:nr, bk, :]), [s.psr[bk]], [vv_.r])
                        s.store(X[vkey][kb * 128:kb * 128 + nr, :], vv_[0:nr, :], vv_, [s.XR[vkey]])

    def sm_scores(s, chunks, nq, nk, scale, mask, Ssb, stt, col):
        for ci, k0 in enumerate(range(0, nk, 512)):
            w = min(512, nk - k0)
            bk = s.mbank()

            def emit(pe, k0=k0, w=w, bk=bk):
                ins = None
                for i, (qa, qr, kf, kr) in enumerate(chunks):
                    ins = pe.matmul(s.ps[0:nq, bk, 0:w], lhsT=qa, rhs=kf(k0, w),
                                    start=(i == 0), stop=(i == len(chunks) - 1))
                return ins
            s.mm_group(emit, [x[1] for x in chunks] + [x[3] for x in chunks], s.psr[bk])
            if ci % 3 == 2:
                s.dve(lambda e, k0=k0, w=w, bk=bk: e.tensor_scalar(
                    out=Ssb[0:nq, k0:k0 + w], in0=s.ps[0:nq, bk, 0:w], scalar1=scale, scalar2=None, op0=ALU.mult),
                    [s.psr[bk]], [Ssb.r])
            else:
                s.act(Ssb[0:nq, k0:k0 + w], s.ps[0:nq, bk, 0:w], AF.Copy, [s.psr[bk]], [Ssb.r], scale=scale)
        if mask:
            s.dve(lambda e: e.memset(Ssb[0:64, nk - 64:nk], NEG), [], [Ssb.r])
        s.dve(lambda e: e.tensor_reduce(out=stt[0:nq, col:col + 1], in_=Ssb[0:nq, 0:nk], axis=AX.X, op=ALU.max),
              [Ssb.r], [stt.r])
        s.dve(lambda e: e.tensor_scalar(out=stt[0:nq, col + 1:col + 2], in0=stt[0:nq, col:col + 1], scalar1=-1.0,
                                        scalar2=None, op0=ALU.mult), [], [stt.r])

    def sm_exp(s, nq, nk, Ssb, E, stt, col):
        s.act(E[0:nq, 0:nk], Ssb[0:nq, 0:nk], AF.Exp, [Ssb.r, stt.r], [E.r, stt.r],
              bias=stt[0:nq, col + 1:col + 2], accum_out=stt[0:nq, col + 2:col + 3])

    def softmax(s, chunks, nq, nk, scale, mask, Ssb, E, stt, col):
        s.sm_scores(chunks, nq, nk, scale, mask, Ssb, stt, col)
        s.sm_exp(nq, nk, Ssb, E, stt, col)

    def pv(s, Wt, nq, nk, WT, vfn, vres, dv):
        nfull = nk // 128
        rem = nk - nfull * 128

        def ev(j0, cnt, pv2, bank):
            s.dve(lambda e: e.tensor_copy(out=WT[:, j0:j0 + cnt, 0:nq], in_=pv2), [bank], [WT.r])
        if nfull:
            s.transposes(Wt, lambda j: Wt[0:nq, j * 128:(j + 1) * 128], nfull, nq, ev)
        if rem:
            def ev2(j0, cnt, pv2, bank):
                s.dve(lambda e: e.tensor_copy(out=WT[0:rem, nfull:nfull + 1, 0:nq], in_=pv2), [bank], [WT.r])
            s.transposes(Wt, lambda j: Wt[0:nq, nfull * 128:nk], 1, nq, ev2, ncols=rem)
        nkb = nfull + (1 if rem else 0)
        bk = s.mbank()

        def emit(pe):
            ins = None
            for kb in range(nkb):
                nr = 128 if kb < nfull else rem
                ins = pe.matmul(s.ps[0:nq, bk, 0:dv], lhsT=WT[0:nr, kb, 0:nq], rhs=vfn(kb, nr),
                                start=(kb == 0), stop=(kb == nkb - 1))
            return ins
        s.mm_group(emit, [WT.r] + list(vres), s.psr[bk])
        return bk

    def compute_lam(s, l):
        c, I = s.c, s.I
        lam_init = 0.8 - 0.6 * math.exp(-0.3 * l)
        with s.scope() as st:
            dl = s.sb(st, "dl", [128, 256], F32)
            pr = s.sb(st, "dlp", [128, 128], F32)
            s.load(dl[:], I["diff_lambda"][l], dl)
            s.dve(lambda e: e.tensor_tensor(out=pr[:, 0:64], in0=dl[:, 0:64], in1=dl[:, 64:128], op=ALU.mult), [dl.r], [pr.r])
            s.dve(lambda e: e.tensor_tensor(out=pr[:, 64:128], in0=dl[:, 128:192], in1=dl[:, 192:256], op=ALU.mult), [dl.r], [pr.r])
            s.dve(lambda e: e.tensor_reduce(out=s.st[:, 0:2], in_=pr[:, :].rearrange("p (a b) -> p a b", a=2),
                                            axis=AX.X, op=ALU.add), [pr.r], [s.st.r])
            s.act(s.st[:, 2:4], s.st[:, 0:2], AF.Exp, [s.st.r], [s.st.r])
            s.dve(lambda e: e.tensor_tensor(out=s.st[:, 4:5], in0=s.st[:, 3:4], in1=s.st[:, 2:3], op=ALU.subtract), [], [s.st.r])
            s.dve(lambda e: e.tensor_scalar(out=s.st[:, 4:5], in0=s.st[:, 4:5], scalar1=-lam_init, scalar2=None, op0=ALU.add),
                  [], [s.st.r])
            s.sch.barrier()
        return lam_init

    def ph_attn(s, l, kind):
        c, I, X = s.c, s.I, s.X
        diff = kind == "diff"
        heads = c.DH if diff else c.MH
        mixbase = c.PCH if diff else c.PCH + c.DH
        scale = 64 ** -0.5 if diff else 192 ** -0.5
        kkey, vkey, kskey, vskey = ("kT", "vv", "kTs", "vs") if diff else ("knT", "vh", "knTs", "vhs")
        NCMP = 2 if diff else 1
        if diff:
            lam_init = s.compute_lam(l)
        for hg0 in range(0, heads, 8):
            nhg = min(8, heads - hg0)
            for samp in (False, True):
                with s.scope() as st:
                    nkmax = c.NKS if samp else c.P
                    nkbmax = (nkmax + 127) // 128
                    nqmax = c.S if samp else 128
                    if not samp:
                        KT = s.sb(st, "KT", [128, nhg, c.P], BF)
                        s.load(KT[:], X[kkey][hg0:hg0 + nhg].rearrange("h p t -> p h t"), KT, [s.XR[kkey]])
                        V = s.sb(st, "V", [128, c.NB, nhg * 128], BF)
                        s.load(V[:], X[vkey][:, hg0 * 128:(hg0 + nhg) * 128].rearrange("(kb p) e -> p kb e", p=128),
                               V, [s.XR[vkey]])
                        if not diff:
                            KP = s.sb(st, "KP", [64, c.P], BF)
                            s.load(KP[:], X["kpeT"], KP, [s.XR["kpeT"]])
                    else:
                        KTs = [s.sb(st, "KTs", [128, c.NKS], BF) for _ in range(2)]
                        Vs = [s.sb(st, "Vs", [128, nkbmax, 128], BF) for _ in range(3)]
                        if not diff:
                            KP = s.sb(st, "KPs", [64, c.NKS], BF)
                            s.load(KP[:], X["kpeTs"], KP, [s.XR["kpeTs"]])
                    qb = [s.sb(st, "qb", [128, nhg, nqmax], BF) for _ in range(2)]
                    qp = [s.sb(st, "qp", [64, nhg, nqmax], BF) for _ in range(2)] if not diff else None
                    Ssb = [[s.sb(st, "Ssb", [nqmax, nkmax], F32) for _ in range(NCMP)] for _ in range(2)]
                    E = [[s.sb(st, "E", [nqmax, nkmax], BF) for _ in range(NCMP)] for _ in range(2)]
                    stts = [s.sb(st, "astt", [128, 16], F32) for _ in range(3)]
                    if diff:
                        nW = 1 if samp else 2
                        Wc = [s.sb(st, "Wc", [nqmax, nkmax], BF) for _ in range(nW)]
                        tmp = [s.sb(st, "Wtmp", [nqmax, nkmax], F32) for _ in range(nW)]
                        oall = [s.sb(st, "oall", [nqmax, nhg, 128], F32) for _ in range(2)]
                        osq = [s.sb(st, "osq", [nqmax, nhg, 128], F32) for _ in range(2)]
                        estt = s.sb(st, "estt", [128, 16], F32)
                        gsub = s.sb(st, "gsub", [128, 128], F32)
                        s.load(gsub[:], I["g_diff_sub"][l], gsub)
                    WT = [s.sb(st, "WT", [128, nkbmax, nqmax], BF) for _ in range(2)]
                    ob = [s.sb(st, "ob", [nqmax, nhg * 128], BF) for _ in range(2)]
                    ts = [s.sb(st, "ats", [128, nhg, nqmax], BF) for _ in range(2)]
                    blocks = [(c.NB, c.P, c.S)] if samp else [(i, i * 128, 128) for i in range(c.NB)]
                    items = [(bx, h) for bx in range(len(blocks)) for h in range(nhg)]
                    qkey = "qT" if diff else "qnT"
                    nfull_s = c.NKS // 128
                    rem_s = c.NKS - nfull_s * 128

                    def load_q(bx):
                        bi, c0, nq = blocks[bx]
                        q = qb[bx % 2]
                        s.load(q[:, :, 0:nq], X[qkey][hg0:hg0 + nhg, :, c0:c0 + nq].rearrange("h p t -> p h t"),
                               q, [s.XR[qkey]])
                        if not diff:
                            qq = qp[bx % 2]
                            s.load(qq[:, :, 0:nq], X["qpT"][hg0:hg0 + nhg, :, c0:c0 + nq].rearrange("h p t -> p h t"),
                                   qq, [s.XR["qpT"]])

                    def load_sample_k(h):
                        s.load(KTs[h % 2][:], X[kskey][hg0 + h], KTs[h % 2], [s.XR[kskey]])

                    def load_sample_v(h):
                        hh = hg0 + h
                        Vt = Vs[h % 3]
                        s.load(Vt[:, 0:nfull_s, :],
                               X[vskey][0:nfull_s * 128, hh * 128:(hh + 1) * 128].rearrange("(kb p) e -> p kb e", p=128),
                               Vt, [s.XR[vskey]])
                        if rem_s:
                            s.load(Vt[0:rem_s, nfull_s, :], X[vskey][nfull_s * 128:c.NKS, hh * 128:(hh + 1) * 128],
                                   Vt, [s.XR[vskey]])

                    def geom(t):
                        bx, h = items[t]
                        bi, c0, nq = blocks[bx]
                        nk = c.NKS if samp else (bi + 1) * 128
                        return bx, h, bi, c0, nq, nk

                    def S1(t):
                        bx, h, bi, c0, nq, nk = geom(t)
                        if h == 0 and bx + 1 < len(blocks):
                            load_q(bx + 1)
                        q = qb[bx % 2]
                        stt = stts[t % 3]
                        if samp:
                            if h + 1 < nhg:
                                load_sample_k(h + 1)
                            load_sample_v(h)
                            Kt = KTs[h % 2]
                            kf_full = lambda k0, w: Kt[:, k0:k0 + w]
                            kf_lo = lambda k0, w: Kt[0:64, k0:k0 + w]
                            kf_hi = lambda k0, w: Kt[64:128, k0:k0 + w]
                            kres = Kt.r
                        else:
                            kf_full = lambda k0, w: KT[:, h, k0:k0 + w]
                            kf_lo = lambda k0, w: KT[0:64, h, k0:k0 + w]
                            kf_hi = lambda k0, w: KT[64:128, h, k0:k0 + w]
                            kres = KT.r
                        mask = not samp
                        if diff:
                            s.sm_scores([(q[0:64, h, 0:nq], q.r, kf_lo, kres)], nq, nk, scale, mask, Ssb[t % 2][0], stt, 0)
                            s.sm_scores([(q[64:128, h, 0:nq], q.r, kf_hi, kres)], nq, nk, scale, mask, Ssb[t % 2][1], stt, 4)
                        else:
                            qq = qp[bx % 2]
                            kp_f = lambda k0, w: KP[0:64, k0:k0 + w]
                            s.sm_scores([(q[:, h, 0:nq], q.r, kf_full, kres), (qq[0:64, h, 0:nq], qq.r, kp_f, KP.r)],
                                        nq, nk, scale, mask, Ssb[t % 2][0], stt, 0)

                    def S2(t):
                        bx, h, bi, c0, nq, nk = geom(t)
                        stt = stts[t % 3]
                        for cm in range(NCMP):
                            s.sm_exp(nq, nk, Ssb[t % 2][cm], E[t % 2][cm], stt, 4 * cm)

                    def S3(t):
                        bx, h, bi, c0, nq, nk = geom(t)
                        stt = stts[t % 3]
                        o = ob[bx % 2]
                        wt = WT[t % 2]
                        if samp:
                            Vt = Vs[h % 3]
                            vfn = lambda kb, nr: Vt[0:nr, kb, :]
                            vres = [Vt.r]
                        else:
                            vfn = lambda kb, nr: V[0:nr, kb, h * 128:(h + 1) * 128]
                            vres = [V.r]
                        if diff:
                            E0, E1 = E[t % 2]
                            wc, tm = Wc[t % nW], tmp[t % nW]
                            s.dve(lambda e: e.reciprocal(out=stt[0:nq, 8:9], in_=stt[0:nq, 2:3]), [], [stt.r])
                            s.dve(lambda e: e.reciprocal(out=stt[0:nq, 9:10], in_=stt[0:nq, 6:7]), [], [stt.r])
                            s.dve(lambda e: e.tensor_tensor(out=stt[0:nq, 9:10], in0=stt[0:nq, 9:10],
                                                            in1=s.st[0:nq, 4:5], op=ALU.mult), [s.st.r], [stt.r])
                            s.pool(lambda e: e.tensor_tensor(out=tm[0:nq, 0:nk], in0=E1[0:nq, 0:nk],
                                                             in1=stt[0:nq, 9:10].broadcast_to([nq, nk]), op=ALU.mult),
                                   [E1.r, stt.r], [tm.r])
                            s.dve(lambda e: e.scalar_tensor_tensor(out=wc[0:nq, 0:nk], in0=E0[0:nq, 0:nk],
                                                                   scalar=stt[0:nq, 8:9], in1=tm[0:nq, 0:nk],
                                                                   op0=ALU.mult, op1=ALU.add),
                                  [E0.r, tm.r, stt.r], [wc.r])
                            bk = s.pv(wc, nq, nk, wt, vfn, vres, 128)
                            oa = oall[bx % 2]
                            s.act(oa[0:nq, h, :], s.ps[0:nq, bk, 0:128], AF.Copy, [s.psr[bk]], [oa.r])
                        else:
                            Eh = E[t % 2][0]
                            s.dve(lambda e: e.reciprocal(out=stt[0:nq, 8:9], in_=stt[0:nq, 2:3]), [], [stt.r])
                            bk = s.pv(Eh, nq, nk, wt, vfn, vres, 128)
                            s.act(o[0:nq, h * 128:(h + 1) * 128], s.ps[0:nq, bk, 0:128], AF.Copy,
                                  [s.psr[bk], stt.r], [o.r], scale=stt[0:nq, 8:9])
                        if h != nhg - 1:
                            return
                        if diff:
                            oa, oq = oall[bx % 2], osq[bx % 2]
                            s.pool(lambda e: e.tensor_tensor(out=oq[0:nq], in0=oa[0:nq], in1=oa[0:nq], op=ALU.mult),
                                   [oa.r], [oq.r])
                            s.dve(lambda e: e.tensor_reduce(out=estt[0:nq, 0:nhg], in_=oq[0:nq], axis=AX.X, op=ALU.add),
                                  [oq.r], [estt.r])
                            s.rstd(estt[0:nq, 0:nhg], estt[0:nq, 0:nhg], nq, 128, [], [estt.r])
                            s.dve(lambda e: e.tensor_tensor(out=oq[0:nq], in0=oa[0:nq],
                                                            in1=estt[0:nq, 0:nhg, None].broadcast_to([nq, nhg, 128]),
                                                            op=ALU.mult), [oa.r, estt.r], [oq.r])
                            s.dve(lambda e: e.scalar_tensor_tensor(
                                out=o[0:nq, :].rearrange("p (h e) -> p h e", e=128), in0=oq[0:nq],
                                scalar=1.0 - lam_init, in1=gsub[0:nq, None, :].broadcast_to([nq, nhg, 128]),
                                op0=ALU.mult, op1=ALU.mult), [oq.r, gsub.r], [o.r])
                        tt = ts[bx % 2]

                        def ev(j0, cnt, pv2, bank):
                            s.act(tt[:, j0:j0 + cnt, 0:nq], pv2, AF.Copy, [bank], [tt.r])
                        s.transposes(o, lambda j: o[0:nq, j * 128:(j + 1) * 128], nhg, nq, ev)
                        mb = mixbase + hg0
                        s.store(X["mixT"][mb:mb + nhg, :, c0:c0 + nq].rearrange("h p t -> p h t"),
                                tt[:, 0:nhg, 0:nq], tt, [s.XR["mixT"]])

                    load_q(0)
                    if samp:
                        load_sample_k(0)
                    n = len(items)
                    for t in range(n + 2):
                        if t < n:
                            S1(t)
                        if 0 <= t - 1 < n:
                            S2(t - 1)
                        if 0 <= t - 2 < n:
                            S3(t - 2)
                s.sch.barrier()

    def ph_diffattn(s, l):
        s.ph_attn(l, "diff")

    def ph_mlaattn(s, l):
        s.ph_attn(l, "mla")

    def tm_gemm_to_fs(s, st, xT, KCn, W, l_unused=None):
        c, X = s.c, s.X
        slab = [s.sb(st, "oslab", [128, KCn, 512], BF) for _ in range(2)]
        o32 = [s.sb(st, "oo32", [128, 512], F32) for _ in range(3)]
        ng = c.D // 512
        s.load_slab(slab[0], W, 0, 512)
        it = 0
        for g in range(ng):
            sl = slab[g % 2]
            if g + 1 < ng:
                s.load_slab(slab[(g + 1) % 2], W, (g + 1) * 512, 512)
            for (c0, nt) in c.tbs:
                bk = s.mbank()

                def emit(pe, sl=sl, c0=c0, nt=nt, bk=bk):
                    ins = None
                    for k in range(KCn):
                        ins = pe.matmul(s.ps[0:nt, bk, :], lhsT=xT[:, k, c0:c0 + nt], rhs=sl[:, k, :],
                                        start=(k == 0), stop=(k == KCn - 1))
                    return ins
                s.mm_group(emit, [sl.r, xT.r], s.psr[bk])
                it += 1
                o = o32[it % 3]
                if it % 2:
                    s.act(o[0:nt, :], s.ps[0:nt, bk, :], AF.Copy, [s.psr[bk]], [o.r])
                else:
                    s.dve(lambda e, o=o, nt=nt, bk=bk: e.tensor_copy(out=o[0:nt, :], in_=s.ps[0:nt, bk, :]),
                          [s.psr[bk]], [o.r])
                s.store(X["fs"][c0:c0 + nt, g * 512:(g + 1) * 512], o[0:nt, :], o, [s.XR["fs"]])

    def ph_outproj(s, l):
        with s.scope() as st:
            mixT = s.load_xT(st, "mixr", "mixT")
            s.tm_gemm_to_fs(st, mixT, s.c.KC, s.I["w_out"][l])

    def ph_cross(s, l):
        c, I, X = s.c, s.I, s.X
        with s.scope() as st0:
            cqT = s.sb(st0, "xqT", [128, 4, c.NT], BF)
            oT = s.sb(st0, "xoT", [128, 4, c.NT], BF)
            with s.scope() as st:
                hT = s.load_xT(st, "hTr", "hT")
                slab = s.sb(st, "xslab", [128, c.KC, 512], BF)
                s.load_slab(slab, I["w_mem_q"][l], 0, 512)
                it = 0
                for ch in range(4):
                    for (c0, w) in c.tgs:
                        bk = s.mbank()

                        def emit(pe, ch=ch, c0=c0, w=w, bk=bk):
                            ins = None
                            for k in range(c.KC):
                                ins = pe.matmul(s.ps[:, bk, 0:w], lhsT=slab[:, k, ch * 128:(ch + 1) * 128],
                                                rhs=hT[:, k, c0:c0 + w], start=(k == 0), stop=(k == c.KC - 1))
                            return ins
                        s.mm_group(emit, [slab.r, hT.r], s.psr[bk])
                        it += 1
                        if it % 2:
                            s.act(cqT[:, ch, c0:c0 + w], s.ps[:, bk, 0:w], AF.Copy, [s.psr[bk]], [cqT.r])
                        else:
                            s.dve(lambda e, ch=ch, c0=c0, w=w, bk=bk: e.tensor_copy(out=cqT[:, ch, c0:c0 + w],
                                                                                   in_=s.ps[:, bk, 0:w]),
                                  [s.psr[bk]], [cqT.r])
            s.sch.barrier()
            with s.scope() as st:
                nmb = c.MEM // 128
                KM = {False: s.sb(st, "xkm", [128, 4, c.MEM], BF), True: s.sb(st, "xkms", [128, 4, c.MEM], BF)}
                VM = {False: s.sb(st, "xvm", [128, nmb, 512], BF), True: s.sb(st, "xvms", [128, nmb, 512], BF)}
                s.load(KM[False][:], X["mkT"].rearrange("h p m -> p h m"), KM[False], [s.XR["mkT"]])
                s.load(KM[True][:], X["mkTs"].rearrange("h p m -> p h m"), KM[True], [s.XR["mkTs"]])
                s.load(VM[False][:], X["mv"].rearrange("(kb p) e -> p kb e", p=128), VM[False], [s.XR["mv"]])
                s.load(VM[True][:], X["mvs"].rearrange("(kb p) e -> p kb e", p=128), VM[True], [s.XR["mvs"]])
                Ssb = s.sb(st, "xS", [128, c.MEM], F32)
                E = [s.sb(st, "xE", [128, c.MEM], BF) for _ in range(2)]
                WT = [s.sb(st, "xWT", [128, nmb, 128], BF) for _ in range(2)]
                ob = [s.sb(st, "xob", [128, 512], BF) for _ in range(2)]
                stt = s.sb(st, "xstt", [128, 16], F32)
                for bi, (c0, nq) in enumerate(c.tbs):
                    samp = c0 >= c.P
                    K, V = KM[samp], VM[samp]
                    o = ob[bi % 2]
                    for h in range(4):
                        Eh = E[h % 2]
                        s.softmax([(cqT[:, h, c0:c0 + nq], cqT.r, lambda k0, w, h=h, K=K: K[:, h, k0:k0 + w], K.r)],
                                  nq, c.MEM, 128 ** -0.5, False, Ssb, Eh, stt, 0)
                        s.dve(lambda e: e.reciprocal(out=stt[0:nq, 8:9], in_=stt[0:nq, 2:3]), [], [stt.r])
                        bk = s.pv(Eh, nq, c.MEM, WT[h % 2], lambda kb, nr, h=h, V=V: V[0:nr, kb, h * 128:(h + 1) * 128],
                                  [V.r], 128)
                        s.act(o[0:nq, h * 128:(h + 1) * 128], s.ps[0:nq, bk, 0:128], AF.Copy, [s.psr[bk], stt.r], [o.r],
                              scale=stt[0:nq, 8:9])

                    def ev(j0, cnt, pv2, bank, c0=c0, nq=nq):
                        s.dve(lambda e: e.tensor_copy(out=oT[:, j0:j0 + cnt, c0:c0 + nq], in_=pv2), [bank], [oT.r])
                    s.transposes(o, lambda j, o=o, nq=nq: o[0:nq, j * 128:(j + 1) * 128], 4, nq, ev)
            s.sch.barrier()
            with s.scope() as st:
                s.tm_gemm_to_fs(st, oT, 4, I["w_mem_o"][l])

    def ph_ffn1(s, l):
        c, I, X = s.c, s.I, s.X
        GW = 128
        ng = c.DFF // GW
        with s.scope() as st:
            hT = s.load_xT(st, "hTr", "hT")
            sg = [s.sb(st, "fsg", [128, c.KC, GW], BF) for _ in range(2)]
            su = [s.sb(st, "fsu", [128, c.KC, GW], BF) for _ in range(2)]
            g32 = [s.sb(st, "fg32", [128, 512], F32) for _ in range(2)]
            ast = [s.sb(st, "fast", [128, c.NT], BF) for _ in range(2)]
            s.load_slab(sg[0], I["w_gate"][l], 0, GW)
            s.load_slab(su[0], I["w_up"][l], 0, GW)
            it = 0
            for g in range(ng):
                wg, wu = sg[g % 2], su[g % 2]
                if g + 1 < ng:
                    s.load_slab(sg[(g + 1) % 2], I["w_gate"][l], (g + 1) * GW, GW)
                    s.load_slab(su[(g + 1) % 2], I["w_up"][l], (g + 1) * GW, GW)
                for j in range(GW // 128):
                    a = ast[(g * (GW // 128) + j) % 2]
                    for (c0, w) in c.tgs:
                        bg, bu = s.mbank(), s.mbank()

                        def emit(pe, wt=wg, j=j, c0=c0, w=w, bk=bg):
                            ins = None
                            for k in range(c.KC):
                                ins = pe.matmul(s.ps[:, bk, 0:w], lhsT=wt[:, k, j * 128:(j + 1) * 128],
                                                rhs=hT[:, k, c0:c0 + w], start=(k == 0), stop=(k == c.KC - 1))
                            return ins
                        s.mm_group(emit, [wg.r, hT.r], s.psr[bg])
                        s.mm_group(lambda pe, e_=emit, wt=wu, bk=bu: e_(pe, wt=wt, bk=bk), [wu.r, hT.r], s.psr[bu])
                        it += 1
                        t32 = g32[it % 2]
                        s.act(t32[:, 0:w], s.ps[:, bg, 0:w], AF.Silu, [s.psr[bg]], [t32.r])
                        s.dve(lambda e, a=a, t32=t32, c0=c0, w=w, bu=bu: e.tensor_tensor(
                            out=a[:, c0:c0 + w], in0=t32[:, 0:w], in1=s.ps[:, bu, 0:w], op=ALU.mult),
                            [t32.r, s.psr[bu]], [a.r])
                    s.store(X["actT"][g * (GW // 128) + j], a[:], a, [s.XR["actT"]])

    def ph_ffn2(s, l):
        c, I, X = s.c, s.I, s.X
        FB = 8
        RB = 6
        regions = [c.tbs[i:i + RB] for i in range(0, len(c.tbs), RB)]
        nfg = (c.FC + FB - 1) // FB
        ndg = c.D // 512
        with s.scope() as st:
            rcmax = max(sum(nt for _, nt in r) for r in regions)
            wsl = [s.sb(st, "f2w", [128, FB, 512], BF) for _ in range(nfg)]
            ab = [s.sb(st, "f2a", [128, FB, rcmax], BF) for _ in range(3)]
            o32 = [s.sb(st, "f2o", [128, 512], F32) for _ in range(3)]
            Wd = I["w_down"][l]

            def load_w(dg, fg):
                f0 = fg * FB
                nb = min(FB, c.FC - f0)
                s.load(wsl[fg][:, 0:nb, :],
                       Wd[f0 * 128:(f0 + nb) * 128, dg * 512:(dg + 1) * 512].rearrange("(f p) n -> p f n", p=128),
                       wsl[fg], q="pool")
            for fg in range(nfg):
                load_w(0, fg)
            it = 0
            io = 0
            for dg in range(ndg):
                for ri, reg in enumerate(regions):
                    r0 = reg[0][0]
                    rc = sum(nt for _, nt in reg)
                    banks = list(range(len(reg)))
                    for fg in range(nfg):
                        f0 = fg * FB
                        nb = min(FB, c.FC - f0)
                        a, w = ab[it % 3], wsl[fg]
                        it += 1
                        s.load(a[:, 0:nb, 0:rc], X["actT"][f0:f0 + nb, :, r0:r0 + rc].rearrange("f p t -> p f t"),
                               a, [s.XR["actT"]])

                        def emit(pe, a=a, w=w, f0=f0, nb=nb, reg=reg, r0=r0):
                            ins = None
                            for f in range(nb):
                                for i, (c0, nt) in enumerate(reg):
                                    ins = pe.matmul(s.ps[0:nt, i, :], lhsT=a[:, f, c0 - r0:c0 - r0 + nt], rhs=w[:, f, :],
                                                    start=(f0 + f == 0), stop=(f0 + f == c.FC - 1))
                            if f0 + nb < c.FC:
                                ins = pe.matmul(s.ps[0:1, 7, 0:1], lhsT=a[:, 0, 0:1], rhs=w[:, 0, 0:1], start=True, stop=True)
                            return ins
                        s.sch.op("pe", emit, [a.r, w.r], [s.psr[7]] + [s.psr[i] for i in banks])
                        if ri == len(regions) - 1 and dg + 1 < ndg:
                            load_w(dg + 1, fg)
                    for i, (c0, nt) in enumerate(reg):
                        o = o32[io % 3]
                        io += 1
                        if io % 2:
                            s.act(o[0:nt, :], s.ps[0:nt, i, :], AF.Copy, [s.psr[i]], [o.r])
                        else:
                            s.dve(lambda e, o=o, nt=nt, i=i: e.tensor_copy(out=o[0:nt, :], in_=s.ps[0:nt, i, :]),
                                  [s.psr[i]], [o.r])
                        s.store(X["fs"][c0:c0 + nt, dg * 512:(dg + 1) * 512], o[0:nt, :], o, [s.XR["fs"]])


def _rope_tables(cfg):
    half = 32
    inv = (np.float32(10000.0) ** (-np.arange(half, dtype=np.float32) / np.float32(half))).astype(np.float32)
    pos = np.concatenate([np.arange(cfg.P), cfg.PAST + np.arange(cfg.S)]).astype(np.float32)
    ang = (pos[:, None] * inv[None, :]).astype(np.float32)
    cos, sin = np.cos(ang).astype(np.float32), np.sin(ang).astype(np.float32)
    rc = np.zeros((128, cfg.NB + 1, 32), np.float32)
    rs = np.zeros((128, cfg.NB + 1, 32), np.float32)
    for bi, (c0, nt) in enumerate(cfg.tbs):
        rc[:nt, bi] = cos[c0:c0 + nt]
        rs[:nt, bi] = sin[c0:c0 + nt]
    return rc, rs


def _col(g, L):
    n = g.shape[1]
    return np.ascontiguousarray(g.reshape(L, n // 128, 128).transpose(0, 2, 1))


def _rep(g, L):
    return np.ascontiguousarray(np.broadcast_to(g.reshape(L, 1, -1), (L, 128, g.reshape(L, -1).shape[1])))


def make_in_maps(cfg, inp, ncores):
    L = cfg.L
    f = lambda a: np.ascontiguousarray(np.asarray(a, dtype=np.float32))
    rc, rs = _rope_tables(cfg)
    rcnt = np.zeros((128, 4, 16), np.float32)
    for g in range(4):
        w = 2 ** (g + 1)
        rcnt[:, g, :] = (1.0 / np.minimum(w, np.arange(16) + 1)).astype(np.float32)[None, :]
    shared = {
        "rcos": rc, "rsin": rs, "rcnt": rcnt,
        "identb": np.eye(128).astype(ml_dtypes.bfloat16), "identf": np.eye(128, dtype=np.float32),
        "memp": None,
    }
    for nm in ("g_mix_pre", "g_mem", "g_x_pre", "g_ff_pre", "g_mla_q", "pool_scale"):
        shared[nm] = _col(f(inp[nm]), L)
    for nm in ("g_mix_post", "g_x_post", "g_ff_post", "g_mla_kv", "g_diff_sub"):
        shared[nm] = _rep(f(inp[nm]), L)
    shared["diff_lambda"] = _rep(f(inp["diff_lambda"]).reshape(L, -1), L)
    for nm in ("w_in", "w_pool", "w_mla_uq", "w_out", "w_mem_k", "w_mem_v", "w_mem_q", "w_mem_o",
               "w_gate", "w_up", "w_down"):
        shared[nm] = f(inp[nm])
    shared["w_mla_uk"] = f(inp["w_mla_uk"]).reshape(L, 512, -1)
    shared["w_mla_uv"] = f(inp["w_mla_uv"]).reshape(L, 512, -1)
    maps = []
    for b in range(ncores):
        m = dict(shared)
        m["xp"] = f(inp["x_prompt"][b])
        m["xs"] = f(inp["x_sample"][b])
        m["cdk"] = f(np.asarray(inp["cache_diff_k"])[:, b]).reshape(L, cfg.PAST, -1)
        m["cdv"] = f(np.asarray(inp["cache_diff_v"])[:, b]).reshape(L, cfg.PAST, -1)
        m["cckv"] = f(np.asarray(inp["cache_mla_ckv"])[:, b])
        m["ckpe"] = f(np.asarray(inp["cache_mla_kpe"])[:, b])
        m["cpool"] = f(np.asarray(inp["cache_pool"])[:, b])
        m["cmk"] = f(np.asarray(inp["cache_mem_k"])[:, b]).reshape(L, cfg.MEM, -1)
        m["cmv"] = f(np.asarray(inp["cache_mem_v"])[:, b]).reshape(L, cfg.MEM, -1)
        m["memp"] = f(inp["mem_prompt"][b])
        maps.append(m)
    return maps


def gather_outputs(cfg, res, ncores):
    L = cfg.L
    st = lambda k: np.stack([np.asarray(res[b][k], dtype=np.float32) for b in range(ncores)])
    yp, ys = st("y_p"), st("y_s")

    def lb(k, tail):
        a = st(k)
        a = np.swapaxes(a, 0, 1)
        return np.ascontiguousarray(a.reshape(a.shape[:3] + tuple(tail)))
    return (yp, ys,
            lb("pdk", (cfg.DH, 128)), lb("pdv", (cfg.DH, 128)), lb("pckv", (512,)), lb("pkpe", (64,)),
            lb("ppool", (cfg.PW,)), lb("pmk", (4, 128)), lb("pmv", (4, 128)),
            lb("sdk", (cfg.DH, 128)), lb("sdv", (cfg.DH, 128)), lb("sckv", (512,)), lb("skpe", (64,)),
            lb("spool", (cfg.PW,)))


_CACHE = {}


def kernel(**inputs):
    cfg = Cfg()
    ncores = 8
    if "nc" not in _CACHE:
        _CACHE["nc"] = Builder(cfg).build()
    nc = _CACHE["nc"]
    in_maps = make_in_maps(cfg, inputs, ncores)
    res = run_bass_kernel_spmd(nc, in_maps, core_ids=list(range(ncores)))
    return gather_outputs(cfg, res.results, ncores)
```

```python
import math
from contextlib import ExitStack

import numpy as np
import ml_dtypes
import concourse.bass as bass
import concourse.mybir as mybir
from concourse.bass_utils import run_bass_kernel_spmd

F32 = mybir.dt.float32
BF = mybir.dt.bfloat16
AF = mybir.ActivationFunctionType
ALU = mybir.AluOpType
AX = mybir.AxisListType
EPS = 1e-6
NEG = -30000.0


class Cfg:
    def __init__(s, D=4096, SEQ=2048, DEC=16, PAST=4096, MEM=256, DFF=11008, L=2):
        s.D, s.P, s.S, s.PAST, s.MEM, s.DFF, s.L = D, SEQ, DEC, PAST, MEM, DFF, L
        s.KC = D // 128
        s.PW = D // 4
        s.PCH = s.PW // 128
        s.PG = s.PW // 4
        s.CPG = s.PG // 128
        s.DH = D // 4 // 128
        s.DW = s.DH * 128
        s.MH = D // 2 // 128
        s.QR = D // 4
        s.QRC = s.QR // 128
        s.UQW = s.MH * 192
        s.INW = s.PW + 3 * s.DW + s.QR + 512 + 64
        s.NT = SEQ + DEC
        s.NB = SEQ // 128
        s.FC = DFF // 128
        s.NKS = PAST + DEC
        s.tbs = [(i * 128, 128) for i in range(s.NB)] + [(SEQ, DEC)]
        s.tgs = []
        c = 0
        while c < SEQ:
            w = min(512, SEQ - c)
            s.tgs.append((c, w))
            c += w
        s.tgs.append((SEQ, DEC))


class Res:
    __slots__ = ("name", "w", "r", "sem", "cnt")

    def __init__(s, name):
        s.name, s.w, s.r, s.sem, s.cnt = name, {}, {}, None, 0


class T:
    def __init__(s, h, name):
        s.h, s.r = h, Res(name)

    def __getitem__(s, k):
        return s.h[k]


class Sched:
    def __init__(s, nc, es):
        s.nc, s.es = nc, es
        s.E = {"pe": nc.tensor, "act": nc.scalar, "dve": nc.vector, "pool": nc.gpsimd, "sp": nc.sync}
        s.csem = {e: es.enter_context(nc.semaphore("c_" + e)) for e in ("pe", "act", "dve", "pool")}
        s.ccnt = {e: 0 for e in s.csem}
        s.waited = {e: {} for e in s.E}
        s.nsem = 4
        s.phys = []
        s.inuse = []
        s.free = {"hw": [], "sw": []}

    def _deps(s, reads, writes):
        ev = {}
        for r in reads:
            for k, v in r.w.items():
                if v[1] > ev.get(k, (None, 0))[1]:
                    ev[k] = v
        for w in writes:
            for d in (w.w, w.r):
                for k, v in d.items():
                    if v[1] > ev.get(k, (None, 0))[1]:
                        ev[k] = v
        return ev

    def _wait(s, e, ev):
        for k, (sem, val) in ev.items():
            if e == "pe" and k == "c_pe":
                continue
            if s.waited[e].get(k, 0) < val:
                s.E[e].wait_ge(sem, val)
                s.waited[e][k] = val

    def _commit(s, key, evt, reads, writes):
        for w in writes:
            w.w[key] = evt
        for r in reads:
            r.r[key] = evt

    def op(s, e, emit, reads=(), writes=()):
        s._wait(e, s._deps(reads, writes))
        ins = emit(s.E[e])
        s.ccnt[e] += 1
        ins.then_inc(s.csem[e], 1)
        s._commit("c_" + e, (s.csem[e], s.ccnt[e]), reads, writes)

    def dma(s, q, out, in_, lane, reads=(), writes=()):
        s._wait(q, s._deps(reads, writes))
        kind = "sw" if q == "pool" else "hw"
        if lane.sem is None:
            lane.sem = {}
        if kind not in lane.sem:
            if s.free[kind]:
                ph = s.free[kind].pop()
            else:
                nm = "d%s%d" % (kind, len(s.phys))
                ph = [s.es.enter_context(s.nc.semaphore(nm)), nm, 0]
                s.phys.append(ph)
                s.nsem += 1
            lane.sem[kind] = ph
            s.inuse.append((lane, kind))
        ph = lane.sem[kind]
        ins = s.E[q].dma_start(out=out, in_=in_)
        ph[2] += 16
        ins.then_inc(ph[0], 16)
        s._commit(ph[1], (ph[0], ph[2]), reads, writes)

    def barrier(s):
        ev = {"c_" + e: (s.csem[e], s.ccnt[e]) for e in s.csem if s.ccnt[e]}
        for ph in s.phys:
            if ph[2]:
                ev[ph[1]] = (ph[0], ph[2])
        for e in s.E:
            s._wait(e, ev)
        for lane, kind in s.inuse:
            s.free[kind].append(lane.sem[kind])
            del lane.sem[kind]
        s.inuse = []

    def finish(s):
        s.barrier()


class Builder:
    def __init__(s, cfg, upto=99):
        s.c = cfg
        s.upto = upto
        s.nc = bass.Bass("TRN2", target_bir_lowering=False)
        s.es = ExitStack()
        s.sch = Sched(s.nc, s.es)
        s.uid = 0

    def scope(s):
        b = s

        class _Scope(ExitStack):
            def __exit__(self, *a):
                if a[0] is None:
                    b.sch.barrier()
                return super().__exit__(*a)
        return _Scope()

    def din(s, name, shape, dt=F32):
        return s.nc.dram_tensor(name, list(shape), dt, kind="ExternalInput").ap()

    def dout(s, name, shape, dt=F32):
        return s.nc.dram_tensor(name, list(shape), dt, kind="ExternalOutput").ap()

    def dscr(s, name, shape, dt=BF):
        return s.nc.dram_tensor(name, list(shape), dt, kind="Internal").ap()

    def sb(s, st, name, shape, dt):
        s.uid += 1
        nm = "%s_%d" % (name, s.uid)
        return T(st.enter_context(s.nc.sbuf_tensor(nm, list(shape), dt)), nm)

    def act(s, out, in_, func, reads, writes, **kw):
        s.sch.op("act", lambda e: e.activation(out=out, in_=in_, func=func, **kw), reads, writes)

    def dve(s, emit, reads, writes):
        s.sch.op("dve", emit, reads, writes)

    def pool(s, emit, reads, writes):
        s.sch.op("pool", emit, reads, writes)

    def load(s, dst_ap, src_ap, dst_t, src_res=(), q="sp"):
        s.sch.dma(q, dst_ap, src_ap, dst_t.r, reads=list(src_res), writes=[dst_t.r])

    def store(s, dst_ap, src_ap, src_t, dst_res=(), q="sp"):
        s.sch.dma(q, dst_ap, src_ap, src_t.r, reads=[src_t.r], writes=list(dst_res))

    def mm_group(s, emit, reads, bank):
        s.sch.op("pe", emit, reads, [bank])

    def rstd(s, out, ss, n, dim, reads, writes):
        s.act(out, ss, AF.Sqrt, list(reads) + [s.epsb.r], writes, bias=s.epsb[0:n, 0:1], scale=1.0 / dim)
        s.dve(lambda e: e.reciprocal(out=out, in_=out), [], writes)

    def transposes(s, src_t, src_ap_fn, n, nrows, dst_fn, bf=True, ncols=128):
        per = (8 if bf else 4)
        ident = s.identb if bf else s.identf
        j = 0
        while j < n:
            cnt = min(per, n - j)
            bk = s.tbank()
            pv = s.ps[:, bk, :]
            if bf:
                pv = pv.bitcast(BF)
            pv = pv.rearrange("p (c t) -> p c t", c=per)

            def emit(pe, j=j, cnt=cnt, pv=pv):
                ins = None
                for i in range(cnt):
                    ins = pe.transpose(out=pv[0:ncols, i, 0:nrows], in_=src_ap_fn(j + i),
                                       identity=ident[0:nrows, 0:nrows])
                return ins
            s.sch.op("pe", emit, [src_t.r, s.identb.r], [s.psr[bk]])
            dst_fn(j, cnt, pv[0:ncols, 0:cnt, 0:nrows], s.psr[bk])
            j += cnt

    def tbank(s):
        s._tb = (s._tb + 1) % 2
        return 6 + s._tb

    def mbank(s):
        s._mb = (s._mb + 1) % 6
        return s._mb

    def build(s):
        c, nc, es = s.c, s.nc, s.es
        L = c.L
        I = {}
        I["xp"] = s.din("xp", [c.P, c.D])
        I["xs"] = s.din("xs", [c.S, c.D])
        I["cdk"] = s.din("cdk", [L, c.PAST, c.DW])
        I["cdv"] = s.din("cdv", [L, c.PAST, c.DW])
        I["cckv"] = s.din("cckv", [L, c.PAST, 512])
        I["ckpe"] = s.din("ckpe", [L, c.PAST, 64])
        I["cpool"] = s.din("cpool", [L, 15, c.PW])
        I["cmk"] = s.din("cmk", [L, c.MEM, 512])
        I["cmv"] = s.din("cmv", [L, c.MEM, 512])
        I["memp"] = s.din("memp", [c.MEM, c.D])
        for nm in ("g_mix_pre", "g_mem", "g_x_pre", "g_ff_pre"):
            I[nm] = s.din(nm, [L, 128, c.KC])
        for nm in ("g_mix_post", "g_x_post", "g_ff_post"):
            I[nm] = s.din(nm, [L, 128, c.D])
        I["w_in"] = s.din("w_in", [L, c.D, c.INW])
        I["w_pool"] = s.din("w_pool", [L, 4, c.PG, c.PG])
        I["pool_scale"] = s.din("pool_scale", [L, 128, c.PCH])
        I["diff_lambda"] = s.din("diff_lambda", [L, 128, 256])
        I["g_diff_sub"] = s.din("g_diff_sub", [L, 128, 128])
        I["g_mla_q"] = s.din("g_mla_q", [L, 128, c.QRC])
        I["w_mla_uq"] = s.din("w_mla_uq", [L, c.QR, c.UQW])
        I["w_mla_uk"] = s.din("w_mla_uk", [L, 512, c.MH * 128])
        I["w_mla_uv"] = s.din("w_mla_uv", [L, 512, c.MH * 128])
        I["g_mla_kv"] = s.din("g_mla_kv", [L, 128, 512])
        I["w_out"] = s.din("w_out", [L, c.D, c.D])
        I["w_mem_k"] = s.din("w_mem_k", [L, c.D, 512])
        I["w_mem_v"] = s.din("w_mem_v", [L, c.D, 512])
        I["w_mem_q"] = s.din("w_mem_q", [L, c.D, 512])
        I["w_mem_o"] = s.din("w_mem_o", [L, 512, c.D])
        I["w_gate"] = s.din("w_gate", [L, c.D, c.DFF])
        I["w_up"] = s.din("w_up", [L, c.D, c.DFF])
        I["w_down"] = s.din("w_down", [L, c.DFF, c.D])
        I["rcos"] = s.din("rcos", [128, c.NB + 1, 32])
        I["rsin"] = s.din("rsin", [128, c.NB + 1, 32])
        I["rcnt"] = s.din("rcnt", [128, 4, 16])
        I["identb"] = s.din("identb", [128, 128], BF)
        I["identf"] = s.din("identf", [128, 128])
        s.I = I
        O = {}
        O["y_p"] = s.dout("y_p", [c.P, c.D])
        O["y_s"] = s.dout("y_s", [c.S, c.D])
        O["pdk"] = s.dout("pdk", [L, c.P, c.DW])
        O["pdv"] = s.dout("pdv", [L, c.P, c.DW])
        O["pckv"] = s.dout("pckv", [L, c.P, 512])
        O["pkpe"] = s.dout("pkpe", [L, c.P, 64])
        O["ppool"] = s.dout("ppool", [L, 15, c.PW])
        O["pmk"] = s.dout("pmk", [L, c.MEM, 512])
        O["pmv"] = s.dout("pmv", [L, c.MEM, 512])
        O["sdk"] = s.dout("sdk", [L, c.S, c.DW])
        O["sdv"] = s.dout("sdv", [L, c.S, c.DW])
        O["sckv"] = s.dout("sckv", [L, c.S, 512])
        O["skpe"] = s.dout("skpe", [L, c.S, 64])
        O["spool"] = s.dout("spool", [L, 15, c.PW])
        s.O = O
        X = {}
        X["xres"] = s.dscr("xres", [c.NT, c.D], F32)
        X["fs"] = s.dscr("fs", [c.NT, c.D], F32)
        X["hT"] = s.dscr("hT", [c.KC, 128, c.NT])
        X["mixT"] = s.dscr("mixT", [c.KC, 128, c.NT])
        X["dT"] = s.dscr("dT", [c.PCH, 128, c.NT])
        X["qT"] = s.dscr("qT", [c.DH, 128, c.NT])
        X["kT"] = s.dscr("kT", [c.DH, 128, c.P])
        X["kTs"] = s.dscr("kTs", [c.DH, 128, c.NKS])
        X["vv"] = s.dscr("vv", [c.P, c.DW])
        X["vs"] = s.dscr("vs", [c.NKS, c.DW])
        X["cq"] = s.dscr("cq", [c.NT, c.QR])
        X["ckvT"] = s.dscr("ckvT", [4, 128, c.P])
        X["ckvTs"] = s.dscr("ckvTs", [4, 128, c.NKS])
        X["kpeT"] = s.dscr("kpeT", [64, c.P])
        X["kpeTs"] = s.dscr("kpeTs", [64, c.NKS])
        X["knT"] = s.dscr("knT", [c.MH, 128, c.P])
        X["knTs"] = s.dscr("knTs", [c.MH, 128, c.NKS])
        X["vh"] = s.dscr("vh", [c.P, c.MH * 128])
        X["vhs"] = s.dscr("vhs", [c.NKS, c.MH * 128])
        X["qnT"] = s.dscr("qnT", [c.MH, 128, c.NT])
        X["qpT"] = s.dscr("qpT", [c.MH, 64, c.NT])
        X["mkT"] = s.dscr("mkT", [4, 128, c.MEM])
        X["mkTs"] = s.dscr("mkTs", [4, 128, c.MEM])
        X["mv"] = s.dscr("mv", [c.MEM, 512])
        X["mvs"] = s.dscr("mvs", [c.MEM, 512])
        X["actT"] = s.dscr("actT", [c.FC, 128, c.NT])
        s.X = X
        s.XR = {k: Res("x_" + k) for k in X}
        s.OR = Res("outputs")

        g = es
        s.ps_h = g.enter_context(nc.psum_tensor("ps", [128, 8, 512], F32))
        s.ps = s.ps_h
        s.psr = [Res("psb%d" % i) for i in range(8)]
        s._tb = 0
        s._mb = 0
        s.identb = s.sb(g, "identb", [128, 128], BF)
        s.identf = s.sb(g, "identf", [128, 128], F32)
        s.rcos = s.sb(g, "rcos", [128, c.NB + 1, 32], F32)
        s.rsin = s.sb(g, "rsin", [128, c.NB + 1, 32], F32)
        s.rcnt = s.sb(g, "rcnt", [128, 4, 16], F32)
        s.st = s.sb(g, "stats", [128, 16], F32)
        s.epsb = s.sb(g, "epsb", [128, 1], F32)
        s.dve(lambda e: e.memset(s.epsb[:], EPS), [], [s.epsb.r])
        s.load(s.identb[:], I["identb"], s.identb)
        s.load(s.identf[:], I["identf"], s.identf)
        s.load(s.rcos[:], I["rcos"], s.rcos)
        s.load(s.rsin[:], I["rsin"], s.rsin)
        s.load(s.rcnt[:], I["rcnt"], s.rcnt)

        ph = 0
        for l in range(L):
            steps = [
                lambda l=l: s.ph_memkv(l),
                lambda l=l: s.ph_norm(l, 0),
                lambda l=l: s.ph_pool(l),
                lambda l=l: s.ph_inproj(l),
                lambda l=l: s.ph_cache(l),
                lambda l=l: s.ph_mlaproj(l),
                lambda l=l: s.ph_diffattn(l),
                lambda l=l: s.ph_mlaattn(l),
                lambda l=l: s.ph_outproj(l),
                lambda l=l: s.ph_norm(l, 1),
                lambda l=l: s.ph_cross(l),
                lambda l=l: s.ph_norm(l, 2),
                lambda l=l: s.ph_ffn1(l),
                lambda l=l: s.ph_ffn2(l),
            ]
            for st in steps:
                if ph < s.upto:
                    st()
                    s.sch.barrier()
                ph += 1
        if ph < s.upto + 1:
            s.ph_norm(L, 0, final=True)
        s.sch.finish()
        es.close()
        return nc

    def ph_norm(s, l, which, final=False):
        c, I, X, O = s.c, s.I, s.X, s.O
        with s.scope() as st:
            has_f = not (l == 0 and which == 0)
            if which == 0:
                gpost = I["g_ff_post"][l - 1] if has_f else None
                gpre = None if final else I["g_mix_pre"][l]
            elif which == 1:
                gpost, gpre = I["g_mix_post"][l], I["g_x_pre"][l]
            else:
                gpost, gpre = I["g_x_post"][l], I["g_ff_pre"][l]
            xb = [s.sb(st, "nx", [128, c.D], F32) for _ in range(2)]
            fb = [s.sb(st, "nf", [128, c.D], F32) for _ in range(2)] if has_f else None
            junk = s.sb(st, "njunk", [128, c.D], BF)
            G = s.sb(st, "nG", [128, c.D], F32) if has_f else None
            gc = s.sb(st, "ngc", [128, c.KC], F32) if gpre is not None else None
            hb = [s.sb(st, "nhb", [128, c.D], BF) for _ in range(2)] if gpre is not None else None
            hs = [s.sb(st, "nhs", [128, c.KC, 512], BF) for _ in range(2)] if gpre is not None else None
            stt = s.sb(st, "nst", [128, 8], F32)
            if has_f:
                s.load(G[:], gpost, G)
            if gpre is not None:
                s.load(gc[:], gpre, gc)
            first = (l == 0 and which == 0)
            hslot = -1
            hcount = 0
            fg = s.sb(st, "nfg", [128, c.D], F32) if has_f else None

            def issue_loads(bi):
                c0, nt = c.tbs[bi]
                x = xb[bi % 2]
                if first:
                    src = I["xp"][c0:c0 + nt, :] if c0 < c.P else I["xs"][:, :]
                    s.load(x[0:nt, :], src, x)
                else:
                    s.load(x[0:nt, :], X["xres"][c0:c0 + nt, :], x, [s.XR["xres"]])
                if has_f:
                    f = fb[bi % 2]
                    s.load(f[0:nt, :], X["fs"][c0:c0 + nt, :], f, [s.XR["fs"]])
            issue_loads(0)
            for bi, (c0, nt) in enumerate(c.tbs):
                sl = bi % 2
                x, = (xb[sl],)
                if bi + 1 < len(c.tbs):
                    issue_loads(bi + 1)
                if has_f:
                    f = fb[sl]
                    s.act(junk[0:nt, :], f[0:nt, :], AF.Square, [f.r], [junk.r, stt.r], accum_out=stt[0:nt, 0:1])
                    s.dve(lambda e, f=f, nt=nt: e.tensor_tensor(
                        out=fg[0:nt, :], in0=f[0:nt, :], in1=G[0:nt, :], op=ALU.mult), [f.r, G.r], [fg.r])
                    s.rstd(stt[0:nt, 1:2], stt[0:nt, 0:1], nt, c.D, [], [stt.r])
                    s.dve(lambda e, x=x, nt=nt: e.scalar_tensor_tensor(
                        out=x[0:nt, :], in0=fg[0:nt, :], scalar=stt[0:nt, 1:2], in1=x[0:nt, :],
                        op0=ALU.mult, op1=ALU.add), [stt.r, fg.r], [x.r])
                    if final:
                        dst = O["y_p"][c0:c0 + nt, :] if c0 < c.P else O["y_s"][:, :]
                        s.store(dst, x[0:nt, :], x, [s.OR])
                    else:
                        s.store(X["xres"][c0:c0 + nt, :], x[0:nt, :], x, [s.XR["xres"]])
                elif first:
                    s.store(X["xres"][c0:c0 + nt, :], x[0:nt, :], x, [s.XR["xres"]])
                if gpre is None:
                    continue
                h = hb[sl]
                s.act(junk[0:nt, :], x[0:nt, :], AF.Square, [x.r], [junk.r, stt.r], accum_out=stt[0:nt, 2:3])
                s.rstd(stt[0:nt, 3:4], stt[0:nt, 2:3], nt, c.D, [], [stt.r])
                s.act(h[0:nt, :], x[0:nt, :], AF.Copy, [x.r, stt.r], [h.r], scale=stt[0:nt, 3:4])
                if hcount % 4 == 0 or nt != 128:
                    hslot += 1
                    hcol0 = c0
                    hoff = 0
                hst = hs[hslot % 2]

                def evac(j0, cnt, pv, bank, hst=hst, hoff=hoff, nt=nt):
                    s.dve(lambda e: e.tensor_tensor(
                        out=hst[:, j0:j0 + cnt, hoff:hoff + nt], in0=pv,
                        in1=gc[:, j0:j0 + cnt, None].broadcast_to([128, cnt, nt]), op=ALU.mult),
                        [bank, gc.r], [hst.r])
                s.transposes(h, lambda j, h=h, nt=nt: h[0:nt, j * 128:(j + 1) * 128], c.KC, nt, evac)
                hoff += nt
                hcount += 1
                last = (bi + 1 == len(c.tbs)) or (hcount % 4 == 0) or (c.tbs[bi + 1][1] != 128)
                if last:
                    s.store(X["hT"].rearrange("k p t -> p k t")[:, :, hcol0:hcol0 + hoff],
                            hst[:, :, 0:hoff], hst, [s.XR["hT"]])
                    hcount = 0

    def load_xT(s, st, name, key, col0=0, ncol=None):
        src = s.X[key]
        nch = src.shape[0]
        ncol = src.shape[2] - col0 if ncol is None else ncol
        t = s.sb(st, name, [128, nch, ncol], BF)
        step = max(1, nch // 4)
        for k0 in range(0, nch, step):
            k1 = min(nch, k0 + step)
            s.load(t[:, k0:k1, :], src.rearrange("k p t -> p k t")[:, k0:k1, col0:col0 + ncol], t, [s.XR[key]])
        return t

    def load_slab(s, slab, w_ap, c0, cw, kc0=0, kcn=None):
        kcn = (w_ap.shape[0] // 128 - kc0) if kcn is None else kcn
        src = w_ap[kc0 * 128:(kc0 + kcn) * 128, c0:c0 + cw].rearrange("(k p) n -> p k n", p=128)
        step = max(1, kcn // 2)
        for k0 in range(0, kcn, step):
            k1 = min(kcn, k0 + step)
            s.load(slab[:, k0:k1, 0:cw], src[:, k0:k1, :], slab, q="pool")

    def ph_memkv(s, l):
        c, I, X, O = s.c, s.I, s.X, s.O
        nmb = c.MEM // 128
        with s.scope() as st:
            xb = [s.sb(st, "mx", [128, c.D], F32) for _ in range(2)]
            junk = s.sb(st, "mjunk", [128, c.D], BF)
            hb = s.sb(st, "mhb", [128, c.D], BF)
            gc = s.sb(st, "mgc", [128, c.KC], F32)
            mT = s.sb(st, "mT", [128, c.KC, c.MEM], BF)
            slab = [s.sb(st, "mslab", [128, c.KC, 512], BF) for _ in range(2)]
            o32 = [s.sb(st, "mo32", [128, 512], F32) for _ in range(2)]
            ob = [s.sb(st, "mob", [128, 512], BF) for _ in range(2)]
            kts = s.sb(st, "mkts", [128, 4, c.MEM], BF)
            stt = s.sb(st, "mst", [128, 4], F32)
            s.load(gc[:], I["g_mem"][l], gc)
            s.load_slab(slab[0], I["w_mem_k"][l], 0, 512)
            s.load_slab(slab[1], I["w_mem_v"][l], 0, 512)
            for mb in range(nmb):
                x = xb[mb % 2]
                s.load(x[:], I["memp"][mb * 128:(mb + 1) * 128, :], x)
                s.act(junk[:], x[:], AF.Square, [x.r], [junk.r, stt.r], accum_out=stt[:, 0:1])
                s.rstd(stt[:, 1:2], stt[:, 0:1], 128, c.D, [], [stt.r])
                s.act(hb[:], x[:], AF.Copy, [x.r, stt.r], [hb.r], scale=stt[:, 1:2])

                def evac(j0, cnt, pv, bank, mb=mb):
                    s.dve(lambda e: e.tensor_tensor(
                        out=mT[:, j0:j0 + cnt, mb * 128:(mb + 1) * 128], in0=pv,
                        in1=gc[:, j0:j0 + cnt, None].broadcast_to([128, cnt, 128]), op=ALU.mult),
                        [bank, gc.r], [mT.r])
                s.transposes(hb, lambda j: hb[:, j * 128:(j + 1) * 128], c.KC, 128, evac)

            def kpost(obt, mb, dst_key, final):
                def ev2(j0, cnt, pv, bank):
                    s.dve(lambda e: e.tensor_copy(out=kts[:, j0:j0 + cnt, mb * 128:(mb + 1) * 128], in_=pv),
                          [bank], [kts.r])
                s.transposes(obt, lambda j: obt[:, j * 128:(j + 1) * 128], 4, 128, ev2)
                if final:
                    s.store(X[dst_key].rearrange("h p m -> p h m"), kts[:], kts, [s.XR[dst_key]])

            for wi in range(2):
                for mb in range(nmb):
                    bk = s.mbank()

                    def emit(pe, wi=wi, mb=mb, bk=bk):
                        ins = None
                        for k in range(c.KC):
                            ins = pe.matmul(s.ps[:, bk, :], lhsT=mT[:, k, mb * 128:(mb + 1) * 128],
                                            rhs=slab[wi][:, k, :], start=(k == 0), stop=(k == c.KC - 1))
                        return ins
                    s.mm_group(emit, [mT.r, slab[wi].r], s.psr[bk])
                    o, b = o32[mb % 2], ob[mb % 2]
                    s.act(o[:], s.ps[:, bk, :], AF.Copy, [s.psr[bk]], [o.r])
                    s.dve(lambda e, o=o, b=b: e.tensor_copy(out=b[:], in_=o[:]), [o.r], [b.r])
                    okey = "pmk" if wi == 0 else "pmv"
                    s.store(O[okey][l, mb * 128:(mb + 1) * 128, :], o[:], o, [s.OR])
                    if wi == 0:
                        kpost(b, mb, "mkT", mb == nmb - 1)
                    else:
                        s.store(X["mv"][mb * 128:(mb + 1) * 128, :], b[:], b, [s.XR["mv"]])
            for mb in range(nmb):
                b = ob[mb % 2]
                s.load(b[:], I["cmk"][l, mb * 128:(mb + 1) * 128, :], b, q="pool")
                kpost(b, mb, "mkTs", mb == nmb - 1)
            for mb in range(nmb):
                b = ob[mb % 2]
                s.load(b[:], I["cmv"][l, mb * 128:(mb + 1) * 128, :], b, q="pool")
                s.store(X["mvs"][mb * 128:(mb + 1) * 128, :], b[:], b, [s.XR["mvs"]])

    def ph_pool(s, l):
        c, I, X, O = s.c, s.I, s.X, s.O
        EP = 15 + c.P
        ES = 15 + c.S
        with s.scope() as st:
            hT = s.load_xT(st, "hTr", "hT")
            slab = [s.sb(st, "pslab", [128, c.KC, 128], BF) for _ in range(2)]
            ue = s.sb(st, "ue", [128, EP + ES], F32)
            A = s.sb(st, "pA", [128, EP + ES], F32)
            B = s.sb(st, "pB", [128, EP + ES], F32)
            dst = [s.sb(st, "pd", [128, c.NT], BF) for _ in range(2)]
            cp = s.sb(st, "cp", [15, c.PW], F32)
            po = s.sb(st, "po", [15, 2, c.PW], F32)
            s.load(cp[:], I["cpool"][l], cp)
            s.dve(lambda e: e.memset(ue[:, 0:15], 0.0), [], [ue.r])
            s.dve(lambda e: e.memset(A[:], 0.0), [], [A.r])
            s.pool(lambda e: e.memset(B[:], 0.0), [], [B.r])
            s.load_slab(slab[0], I["w_in"][l], 0, 128)
            for ch in range(c.PCH):
                sl = slab[ch % 2]
                if ch + 1 < c.PCH:
                    s.load_slab(slab[(ch + 1) % 2], I["w_in"][l], (ch + 1) * 128, 128)

                def evc(j0, cnt, pv, bank):
                    s.dve(lambda e: e.tensor_copy(out=ue[:, EP:EP + 15], in_=pv[:, 0, :]), [bank], [ue.r])
                s.transposes(cp, lambda j, ch=ch: cp[0:15, ch * 128:(ch + 1) * 128], 1, 15, evc, bf=False)
                for (c0, w) in c.tgs:
                    bk = s.mbank()

                    def emit(pe, sl=sl, c0=c0, w=w, bk=bk):
                        ins = None
                        for k in range(c.KC):
                            ins = pe.matmul(s.ps[:, bk, 0:w], lhsT=sl[:, k, :], rhs=hT[:, k, c0:c0 + w],
                                            start=(k == 0), stop=(k == c.KC - 1))
                        return ins
                    s.mm_group(emit, [sl.r, hT.r], s.psr[bk])
                    e0 = 15 + c0 if c0 < c.P else EP + 15
                    s.act(ue[:, e0:e0 + w], s.ps[:, bk, 0:w], AF.Copy, [s.psr[bk]], [ue.r])
                for gi, e0 in ((0, EP - 15), (1, EP + ES - 15)):
                    def evp(j0, cnt, pv, bank, gi=gi, ch=ch):
                        s.dve(lambda e: e.tensor_copy(out=po[0:15, gi, ch * 128:(ch + 1) * 128], in_=pv[:, 0, :]),
                              [bank], [po.r])
                    s.transposes(ue, lambda j, e0=e0: ue[:, e0:e0 + 15], 1, 128, evp, bf=False, ncols=15)
                g = ch // c.CPG
                cur, bufs = ue, [A, B]
                sh = 1
                for lev in range(g + 1):
                    nxt = bufs[lev % 2]
                    s.dve(lambda e, cur=cur, nxt=nxt, sh=sh: e.tensor_tensor(
                        out=nxt[:, sh:EP + ES], in0=cur[:, sh:EP + ES], in1=cur[:, 0:EP + ES - sh], op=ALU.add),
                        [cur.r], [nxt.r])
                    cur = nxt
                    sh *= 2
                wdw = 2 ** (g + 1)
                other = bufs[(g + 1) % 2]
                d = dst[ch % 2]
                s.dve(lambda e, cur=cur, other=other, wdw=wdw: e.scalar_tensor_tensor(
                    out=other[:, 15:EP + ES], in0=cur[:, 15:EP + ES], scalar=1.0 / wdw, in1=ue[:, 15:EP + ES],
                    op0=ALU.mult, op1=ALU.subtract), [cur.r, ue.r], [other.r])
                s.dve(lambda e, cur=cur, g=g: e.tensor_tensor(
                    out=cur[:, 15:30], in0=cur[:, 15:30], in1=s.rcnt[:, g, 0:15], op=ALU.mult), [s.rcnt.r], [cur.r])
                s.dve(lambda e, cur=cur, other=other: e.tensor_tensor(
                    out=other[:, 15:30], in0=cur[:, 15:30], in1=ue[:, 15:30], op=ALU.subtract),
                    [cur.r, ue.r], [other.r])
                s.act(d[:, 0:c.P], other[:, 15:EP], AF.Copy, [other.r], [d.r])
                s.act(d[:, c.P:c.NT], other[:, EP + 15:EP + ES], AF.Copy, [other.r], [d.r])
                s.store(X["dT"][ch], d[:], d, [s.XR["dT"]])
            s.store(O["ppool"][l], po[0:15, 0, :], po, [s.OR])
            s.store(O["spool"][l], po[0:15, 1, :], po, [s.OR])
        with s.scope() as st:
            dT = s.load_xT(st, "dTr", "dT")
            wp = s.sb(st, "wp", [128, 4, c.CPG, c.PG], BF)
            psc = s.sb(st, "psc", [128, c.PCH], F32)
            ms = [s.sb(st, "pms", [128, c.NT], BF) for _ in range(2)]
            s.load(wp[:], I["w_pool"][l].rearrange("g (cc p) e -> p g cc e", p=128), wp, q="pool")
            s.load(psc[:], I["pool_scale"][l], psc)
            for g in range(4):
                for ec in range(c.CPG):
                    och = g * c.CPG + ec
                    m = ms[och % 2]
                    for (c0, w) in c.tgs:
                        bk = s.mbank()

                        def emit(pe, g=g, ec=ec, c0=c0, w=w, bk=bk):
                            ins = None
                            for cc in range(c.CPG):
                                ins = pe.matmul(s.ps[:, bk, 0:w], lhsT=wp[:, g, cc, ec * 128:(ec + 1) * 128],
                                                rhs=dT[:, g * c.CPG + cc, c0:c0 + w],
                                                start=(cc == 0), stop=(cc == c.CPG - 1))
                            return ins
                        s.mm_group(emit, [wp.r, dT.r], s.psr[bk])
                        s.act(m[:, c0:c0 + w], s.ps[:, bk, 0:w], AF.Copy, [s.psr[bk], psc.r], [m.r],
                              scale=psc[:, och:och + 1])
                    s.store(X["mixT"][och], m[:], m, [s.XR["mixT"]])

    def rope_tm(s, xs, ro, nt, W, bi):
        U = W // 64
        X4 = xs[0:nt, 0:W].rearrange("p (u h f) -> p u h f", h=2, f=32)
        R4 = ro[0:nt, 0:W].rearrange("p (u h f) -> p u h f", h=2, f=32)
        cs = s.rcos[0:nt, bi, None, :].broadcast_to([nt, U, 32])
        sn = s.rsin[0:nt, bi, None, :].broadcast_to([nt, U, 32])
        x1, x2 = X4[:, :, 0, :], X4[:, :, 1, :]
        t = s.ropet
        T4 = t[0:nt, 0:W].rearrange("p (u h f) -> p u h f", h=2, f=32)
        ta, tb = T4[:, :, 0, :], T4[:, :, 1, :]
        s.dve(lambda e: e.tensor_tensor(out=ta, in0=x2, in1=sn, op=ALU.mult), [xs.r, s.rsin.r], [t.r])
        s.dve(lambda e: e.tensor_tensor(out=R4[:, :, 0, :], in0=x1, in1=cs, op=ALU.mult), [xs.r, s.rcos.r], [ro.r])
        s.dve(lambda e: e.tensor_tensor(out=R4[:, :, 0, :], in0=R4[:, :, 0, :], in1=ta, op=ALU.subtract),
              [t.r], [ro.r])
        s.pool(lambda e: e.tensor_tensor(out=tb, in0=x1, in1=sn, op=ALU.mult), [xs.r, s.rsin.r], [t.r])
        s.pool(lambda e: e.tensor_tensor(out=R4[:, :, 1, :], in0=x2, in1=cs, op=ALU.mult), [xs.r, s.rcos.r], [ro.r])
        s.pool(lambda e: e.tensor_tensor(out=R4[:, :, 1, :], in0=R4[:, :, 1, :], in1=tb, op=ALU.add),
               [t.r], [ro.r])

    def ph_inproj(s, l):
        c, I, X, O = s.c, s.I, s.X, s.O
        W = I["w_in"][l]
        groups = []
        off = c.PW
        for kind, width in (("q", c.DW), ("k", c.DW), ("v", c.DW), ("cq", c.QR), ("ckv", 512), ("kpe", 64)):
            o = 0
            while o < width:
                gw = min(512, width - o)
                groups.append((kind, o, off + o, gw))
                o += gw
            off += width
        nbh = (c.NB + 1) // 2
        halves = [c.tbs[0:nbh], c.tbs[nbh:]] if c.NB > 1 else [c.tbs]
        for hv in halves:
            s.inproj_half(l, groups, hv)
            s.sch.barrier()

    def inproj_half(s, l, groups, tbl):
        c, I, X, O = s.c, s.I, s.X, s.O
        W = I["w_in"][l]
        hc0 = tbl[0][0]
        hcn = sum(nt for _, nt in tbl)
        bi0 = c.tbs.index(tbl[0])
        with s.scope() as st:
            hT = s.load_xT(st, "hTr", "hT", hc0, hcn)
            slab = [s.sb(st, "islab", [128, c.KC, 512], BF) for _ in range(2)]
            x32 = [s.sb(st, "ix32", [128, 512], F32) for _ in range(2)]
            r32 = [s.sb(st, "ir32", [128, 512], F32) for _ in range(2)]
            rb = [s.sb(st, "irb", [128, 512], BF) for _ in range(2)]
            s.ropet = s.sb(st, "ropet", [128, 512], F32)
            tst = [s.sb(st, "itst", [128, 4, 128], BF) for _ in range(2)]
            gkv = s.sb(st, "gkv", [128, 512], F32)
            junk = s.sb(st, "ijunk", [128, 512], F32)
            stt = s.sb(st, "ist", [128, 4], F32)
            s.load(gkv[:], I["g_mla_kv"][l], gkv)
            s.load_slab(slab[0], W, groups[0][2], groups[0][3])
            it = 0
            for gi, (kind, o, wc0, gw) in enumerate(groups):
                sl = slab[gi % 2]
                if gi + 1 < len(groups):
                    s.load_slab(slab[(gi + 1) % 2], W, groups[gi + 1][2], groups[gi + 1][3])
                tslot = 0
                for bi_, (c0, nt) in enumerate(tbl):
                    bi = bi0 + bi_
                    bk = s.mbank()
                    samp = c0 >= c.P

                    def emit(pe, sl=sl, c0=c0, nt=nt, bk=bk, gw=gw):
                        ins = None
                        for k in range(c.KC):
                            ins = pe.matmul(s.ps[0:nt, bk, 0:gw], lhsT=hT[:, k, c0 - hc0:c0 - hc0 + nt], rhs=sl[:, k, 0:gw],
                                            start=(k == 0), stop=(k == c.KC - 1))
                        return ins
                    s.mm_group(emit, [sl.r, hT.r], s.psr[bk])
                    it += 1
                    xs, ro, b = x32[it % 2], r32[it % 2], rb[it % 2]
                    pv = s.ps[0:nt, bk, 0:gw]
                    nh = gw // 128 if gw >= 128 else 1
                    tw = 128 if gw >= 128 else gw

                    def tr_store(b, key, skey, hb0):
                        nonlocal tslot
                        tslot += 1
                        ts = tst[tslot % 2]

                        def ev(j0, cnt, pv2, bank):
                            s.act(ts[0:tw, j0:j0 + cnt, 0:nt], pv2, AF.Copy, [bank], [ts.r])
                        s.transposes(b, lambda j: b[0:nt, j * tw:(j + 1) * tw], nh, nt, ev, ncols=tw)
                        kk = skey if samp else key
                        cc = c.PAST if (samp and skey != key) else c0
                        if len(X[kk].shape) == 3:
                            dstap = X[kk][hb0:hb0 + nh, 0:tw, cc:cc + nt].rearrange("h p t -> p h t")
                            srcap = ts[0:tw, 0:nh, 0:nt]
                        else:
                            dstap = X[kk][0:tw, cc:cc + nt]
                            srcap = ts[0:tw, 0, 0:nt]
                        s.store(dstap, srcap, ts, [s.XR[kk]])

                    if kind in ("q", "k", "kpe"):
                        s.act(xs[0:nt, 0:gw], pv, AF.Copy, [s.psr[bk]], [xs.r])
                        s.rope_tm(xs, ro, nt, gw, bi)
                        s.act(b[0:nt, 0:gw], ro[0:nt, 0:gw], AF.Copy, [ro.r], [b.r])
                        if kind == "k":
                            dst = (O["sdk"][l, :, o:o + gw] if samp else O["pdk"][l, c0:c0 + nt, o:o + gw])
                            s.store(dst, ro[0:nt, 0:gw], ro, [s.OR])
                            tr_store(b, "kT", "kTs", o // 128)
                        elif kind == "kpe":
                            dst = (O["skpe"][l, :, :] if samp else O["pkpe"][l, c0:c0 + nt, :])
                            s.store(dst, ro[0:nt, 0:gw], ro, [s.OR])
                            tr_store(b, "kpeT", "kpeTs", 0)
                        else:
                            tr_store(b, "qT", "qT", o // 128)
                    elif kind == "v":
                        s.act(ro[0:nt, 0:gw], pv, AF.Copy, [s.psr[bk]], [ro.r])
                        s.dve(lambda e, ro=ro, b=b, nt=nt, gw=gw: e.tensor_copy(out=b[0:nt, 0:gw], in_=ro[0:nt, 0:gw]),
                              [ro.r], [b.r])
                        dst = (O["sdv"][l, :, o:o + gw] if samp else O["pdv"][l, c0:c0 + nt, o:o + gw])
                        s.store(dst, ro[0:nt, 0:gw], ro, [s.OR])
                        if samp:
                            s.store(X["vs"][c.PAST:c.PAST + nt, o:o + gw], b[0:nt, 0:gw], b, [s.XR["vs"]])
                        else:
                            s.store(X["vv"][c0:c0 + nt, o:o + gw], b[0:nt, 0:gw], b, [s.XR["vv"]])
                    elif kind == "cq":
                        s.act(b[0:nt, 0:gw], pv, AF.Copy, [s.psr[bk]], [b.r])
                        s.store(X["cq"][c0:c0 + nt, o:o + gw], b[0:nt, 0:gw], b, [s.XR["cq"]])
                    elif kind == "ckv":
                        s.act(junk[0:nt, :], pv, AF.Square, [s.psr[bk]], [junk.r, stt.r], accum_out=stt[0:nt, 0:1])
                        s.rstd(stt[0:nt, 1:2], stt[0:nt, 0:1], nt, 512, [], [stt.r])
                        s.dve(lambda e, ro=ro, pv=pv, nt=nt: e.scalar_tensor_tensor(
                            out=ro[0:nt, :], in0=pv, scalar=stt[0:nt, 1:2], in1=gkv[0:nt, :],
                            op0=ALU.mult, op1=ALU.mult), [s.psr[bk], stt.r, gkv.r], [ro.r])
                        s.act(b[0:nt, :], ro[0:nt, :], AF.Copy, [ro.r], [b.r])
                        dst = (O["sckv"][l, :, :] if samp else O["pckv"][l, c0:c0 + nt, :])
                        s.store(dst, ro[0:nt, :], ro, [s.OR])
                        tr_store(b, "ckvT", "ckvTs", 0)

    def ph_cache(s, l):
        c, I, X = s.c, s.I, s.X
        nkb = c.PAST // 128
        with s.scope() as st:
            cb = [s.sb(st, "ccb", [128, 4, c.DW], BF) for _ in range(2)]
            ts = [s.sb(st, "cts", [128, c.DH, 512], BF) for _ in range(2)]
            for key, skey, width, vkey in (("cdk", "kTs", c.DW, None), ("cdv", None, c.DW, "vs"),
                                           ("cckv", "ckvTs", 512, None), ("ckpe", "kpeTs", 64, None)):
                nh = max(1, width // 128)
                tw = min(128, width)
                for q0 in range(0, nkb, 4):
                    nq = min(4, nkb - q0)
                    b = cb[(q0 // 4) % 2]
                    t = ts[(q0 // 4) % 2]
                    s.load(b[:, 0:nq, 0:width],
                           I[key][l, q0 * 128:(q0 + nq) * 128, :].rearrange("(q p) w -> p q w", p=128), b, q="pool")
                    if vkey is not None:
                        s.store(X[vkey][q0 * 128:(q0 + nq) * 128, :].rearrange("(q p) w -> p q w", p=128),
                                b[:, 0:nq, 0:width], b, [s.XR[vkey]])
                        continue
                    for qq in range(nq):
                        def ev(j0, cnt, pv, bank, qq=qq, t=t):
                            s.dve(lambda e: e.tensor_copy(out=t[0:tw, j0:j0 + cnt, qq * 128:(qq + 1) * 128], in_=pv),
                                  [bank], [t.r])
                        s.transposes(b, lambda j, qq=qq, b=b: b[:, qq, j * tw:(j + 1) * tw], nh, 128, ev, ncols=tw)
                    if len(X[skey].shape) == 3:
                        s.store(X[skey][:, :, q0 * 128:(q0 + nq) * 128].rearrange("h p t -> p h t"),
                                t[0:tw, 0:nh, 0:nq * 128], t, [s.XR[skey]])
                    else:
                        s.store(X[skey][:, q0 * 128:(q0 + nq) * 128], t[0:tw, 0, 0:nq * 128], t, [s.XR[skey]])

    def ph_mlaproj(s, l):
        c, I, X = s.c, s.I, s.X
        with s.scope() as st:
            cqnT = s.sb(st, "cqnT", [128, c.QRC, c.NT], BF)
            wuq = s.sb(st, "wuq", [128, c.QRC, c.UQW], BF)
            gq = s.sb(st, "gq", [128, c.QRC], F32)
            s.load(gq[:], I["g_mla_q"][l], gq)
            nsp = 4
            step = c.UQW // nsp
            for i in range(nsp):
                s.load(wuq[:, :, i * step:(i + 1) * step],
                       I["w_mla_uq"][l][:, i * step:(i + 1) * step].rearrange("(k p) n -> p k n", p=128), wuq, q="pool")
            with s.scope() as st2:
                cb = [s.sb(st2, "cqb", [128, c.QR], BF) for _ in range(2)]
                cn = [s.sb(st2, "cqn", [128, c.QR], BF) for _ in range(2)]
                junk = s.sb(st2, "cqj", [128, c.QR], BF)
                stt = s.sb(st2, "cqs", [128, 4], F32)
                for bi, (c0, nt) in enumerate(c.tbs):
                    b, n = cb[bi % 2], cn[bi % 2]
                    s.load(b[0:nt, :], X["cq"][c0:c0 + nt, :], b, [s.XR["cq"]])
                    s.act(junk[0:nt, :], b[0:nt, :], AF.Square, [b.r], [junk.r, stt.r], accum_out=stt[0:nt, 0:1])
                    s.rstd(stt[0:nt, 1:2], stt[0:nt, 0:1], nt, c.QR, [], [stt.r])
                    s.act(n[0:nt, :], b[0:nt, :], AF.Copy, [b.r, stt.r], [n.r], scale=stt[0:nt, 1:2])

                    def ev(j0, cnt, pv, bank, c0=c0, nt=nt):
                        s.dve(lambda e: e.tensor_tensor(
                            out=cqnT[:, j0:j0 + cnt, c0:c0 + nt], in0=pv,
                            in1=gq[:, j0:j0 + cnt, None].broadcast_to([128, cnt, nt]), op=ALU.mult),
                            [bank, gq.r], [cqnT.r])
                    s.transposes(n, lambda j, n=n, nt=nt: n[0:nt, j * 128:(j + 1) * 128], c.QRC, nt, ev)
            with s.scope() as st2:
                qs = [s.sb(st2, "qns", [128, c.NT], BF) for _ in range(2)]
                for h in range(c.MH):
                    q = qs[h % 2]
                    for ti, (c0, w) in enumerate(c.tgs):
                        bk = s.mbank()

                        def emit(pe, h=h, c0=c0, w=w, bk=bk):
                            ins = None
                            for k in range(c.QRC):
                                ins = pe.matmul(s.ps[:, bk, 0:w], lhsT=wuq[:, k, h * 192:h * 192 + 128],
                                                rhs=cqnT[:, k, c0:c0 + w], start=(k == 0), stop=(k == c.QRC - 1))
                            return ins
                        s.mm_group(emit, [wuq.r, cqnT.r], s.psr[bk])
                        if ti % 2 == 0:
                            s.act(q[:, c0:c0 + w], s.ps[:, bk, 0:w], AF.Copy, [s.psr[bk]], [q.r])
                        else:
                            s.dve(lambda e, q=q, c0=c0, w=w, bk=bk: e.tensor_copy(out=q[:, c0:c0 + w], in_=s.ps[:, bk, 0:w]),
                                  [s.psr[bk]], [q.r])
                    s.store(X["qnT"][h], q[:], q, [s.XR["qnT"]])
                x32 = [s.sb(st2, "qx32", [128, 512], F32) for _ in range(2)]
                r32 = [s.sb(st2, "qr32", [128, 512], F32) for _ in range(2)]
                rb = [s.sb(st2, "qrb", [128, 512], BF) for _ in range(2)]
                s.ropet = s.sb(st2, "ropet2", [128, 512], F32)
                tst = [s.sb(st2, "qtst", [64, 8, 128], BF) for _ in range(2)]
                wv = wuq[:, :, :].rearrange("p k (h c) -> p k h c", c=192)
                it = 0
                for hg0 in range(0, c.MH, 8):
                    nh = min(8, c.MH - hg0)
                    gw = nh * 64
                    for bi, (c0, nt) in enumerate(c.tbs):
                        bk = s.mbank()

                        def emit(pe, hg0=hg0, nh=nh, c0=c0, nt=nt, bk=bk, gw=gw):
                            ins = None
                            for k in range(c.QRC):
                                ins = pe.matmul(s.ps[0:nt, bk, 0:gw].rearrange("p (h c) -> p h c", c=64),
                                                lhsT=cqnT[:, k, c0:c0 + nt], rhs=wv[:, k, hg0:hg0 + nh, 128:192],
                                                start=(k == 0), stop=(k == c.QRC - 1))
                            return ins
                        s.mm_group(emit, [wuq.r, cqnT.r], s.psr[bk])
                        it += 1
                        xs, ro, b, ts = x32[it % 2], r32[it % 2], rb[it % 2], tst[it % 2]
                        s.act(xs[0:nt, 0:gw], s.ps[0:nt, bk, 0:gw], AF.Copy, [s.psr[bk]], [xs.r])
                        s.rope_tm(xs, ro, nt, gw, bi)
                        s.act(b[0:nt, 0:gw], ro[0:nt, 0:gw], AF.Copy, [ro.r], [b.r])

                        def ev(j0, cnt, pv2, bank, ts=ts, nt=nt):
                            s.dve(lambda e: e.tensor_copy(out=ts[0:64, j0:j0 + cnt, 0:nt], in_=pv2), [bank], [ts.r])
                        s.transposes(b, lambda j, b=b, nt=nt: b[0:nt, j * 64:(j + 1) * 64], nh, nt, ev, ncols=64)
                        s.store(X["qpT"][hg0:hg0 + nh, :, c0:c0 + nt].rearrange("h p t -> p h t"),
                                ts[0:64, 0:nh, 0:nt], ts, [s.XR["qpT"]])
        with s.scope() as st:
            wuk = s.sb(st, "wuk", [128, 4, c.MH * 128], BF)
            wuv = s.sb(st, "wuv", [128, 4, c.MH * 128], BF)
            s.load(wuk[:], I["w_mla_uk"][l].rearrange("(k p) n -> p k n", p=128), wuk, q="pool")
            s.load(wuv[:], I["w_mla_uv"][l].rearrange("(k p) n -> p k n", p=128), wuv, q="pool")
            for ckey, nkey, vkey, nk in (("ckvT", "knT", "vh", c.P), ("ckvTs", "knTs", "vhs", c.NKS)):
                with s.scope() as st2:
                    ck = s.load_xT(st2, "ckr", ckey)
                    ks = [s.sb(st2, "kns", [128, nk], BF) for _ in range(2)]
                    vs_ = [s.sb(st2, "vhs", [128, c.MH * 128], BF) for _ in range(2)]
                    it = 0
                    for h in range(c.MH):
                        kk = ks[h % 2]
                        for k0 in range(0, nk, 512):
                            w = min(512, nk - k0)
                            bk = s.mbank()

                            def emit(pe, h=h, k0=k0, w=w, bk=bk):
                                ins = None
                                for r in range(4):
                                    ins = pe.matmul(s.ps[:, bk, 0:w], lhsT=wuk[:, r, h * 128:(h + 1) * 128],
                                                    rhs=ck[:, r, k0:k0 + w], start=(r == 0), stop=(r == 3))
                                return ins
                            s.mm_group(emit, [wuk.r, ck.r], s.psr[bk])
                            it += 1
                            if it % 2:
                                s.act(kk[:, k0:k0 + w], s.ps[:, bk, 0:w], AF.Copy, [s.psr[bk]], [kk.r])
                            else:
                                s.dve(lambda e, kk=kk, k0=k0, w=w, bk=bk: e.tensor_copy(out=kk[:, k0:k0 + w], in_=s.ps[:, bk, 0:w]),
                                      [s.psr[bk]], [kk.r])
                        s.store(X[nkey][h], kk[:], kk, [s.XR[nkey]])
                    for kb in range((nk + 127) // 128):
                        nr = min(128, nk - kb * 128)
                        vv_ = vs_[kb % 2]
                        for cg in range(0, c.MH * 128, 512):
                            bk = s.mbank()

                            def emit(pe, kb=kb, nr=nr, cg=cg, bk=bk):
                                ins = None
                                for r in range(4):
                                    ins = pe.matmul(s.ps[0:nr, bk, :], lhsT=ck[:, r, kb * 128:kb * 128 + nr],
                                                    rhs=wuv[:, r, cg:cg + 512], start=(r == 0), stop=(r == 3))
                                return ins
                            s.mm_group(emit, [wuv.r, ck.r], s.psr[bk])
                            it += 1
                            if it % 2:
                                s.act(vv_[0:nr, cg:cg + 512], s.ps[0:nr, bk, :], AF.Copy, [s.psr[bk]], [vv_.r])
                            else:
                                s.dve(lambda e, vv_=vv_, nr=nr, cg=cg, bk=bk: e.tensor_copy(
                                    out=vv_[0:nr, cg:cg + 512], in_=s.ps[0:nr, bk, :]), [s.psr[bk]], [vv_.r])
                        s.store(X[vkey][kb * 128:kb * 128 + nr, :], vv_[0:nr, :], vv_, [s.XR[vkey]])

    def sm_scores(s, chunks, nq, nk, scale, mask, Ssb, stt, col):
        for ci, k0 in enumerate(range(0, nk, 512)):
            w = min(512, nk - k0)
            bk = s.mbank()

            def emit(pe, k0=k0, w=w, bk=bk):
                ins = None
                for i, (qa, qr, kf, kr) in enumerate(chunks):
                    ins = pe.matmul(s.ps[0:nq, bk, 0:w], lhsT=qa, rhs=kf(k0, w),
                                    start=(i == 0), stop=(i == len(chunks) - 1))
                return ins
            s.mm_group(emit, [x[1] for x in chunks] + [x[3] for x in chunks], s.psr[bk])
            if ci % 3 == 2:
                s.dve(lambda e, k0=k0, w=w, bk=bk: e.tensor_scalar(
                    out=Ssb[0:nq, k0:k0 + w], in0=s.ps[0:nq, bk, 0:w], scalar1=scale, scalar2=None, op0=ALU.mult),
                    [s.psr[bk]], [Ssb.r])
            else:
                s.act(Ssb[0:nq, k0:k0 + w], s.ps[0:nq, bk, 0:w], AF.Copy, [s.psr[bk]], [Ssb.r], scale=scale)
        if mask:
            s.dve(lambda e: e.memset(Ssb[0:64, nk - 64:nk], NEG), [], [Ssb.r])
        s.dve(lambda e: e.tensor_reduce(out=stt[0:nq, col:col + 1], in_=Ssb[0:nq, 0:nk], axis=AX.X, op=ALU.max),
              [Ssb.r], [stt.r])
        s.dve(lambda e: e.tensor_scalar(out=stt[0:nq, col + 1:col + 2], in0=stt[0:nq, col:col + 1], scalar1=-1.0,
                                        scalar2=None, op0=ALU.mult), [], [stt.r])

    def sm_exp(s, nq, nk, Ssb, E, stt, col):
        s.act(E[0:nq, 0:nk], Ssb[0:nq, 0:nk], AF.Exp, [Ssb.r, stt.r], [E.r, stt.r],
              bias=stt[0:nq, col + 1:col + 2], accum_out=stt[0:nq, col + 2:col + 3])

    def softmax(s, chunks, nq, nk, scale, mask, Ssb, E, stt, col):
        s.sm_scores(chunks, nq, nk, scale, mask, Ssb, stt, col)
        s.sm_exp(nq, nk, Ssb, E, stt, col)

    def pv_T(s, Wt, nq, nk, WT):
        nfull = nk // 128
        rem = nk - nfull * 128

        def ev(j0, cnt, pv2, bank):
            s.dve(lambda e: e.tensor_copy(out=WT[:, j0:j0 + cnt, 0:nq], in_=pv2), [bank], [WT.r])
        if nfull:
            s.transposes(Wt, lambda j: Wt[0:nq, j * 128:(j + 1) * 128], nfull, nq, ev)
        if rem:
            def ev2(j0, cnt, pv2, bank):
                s.dve(lambda e: e.tensor_copy(out=WT[0:rem, nfull:nfull + 1, 0:nq], in_=pv2), [bank], [WT.r])
            s.transposes(Wt, lambda j: Wt[0:nq, nfull * 128:nk], 1, nq, ev2, ncols=rem)

    def pv_mm(s, nq, nk, WT, vfn, vres, dv):
        nfull = nk // 128
        rem = nk - nfull * 128
        nkb = nfull + (1 if rem else 0)
        bk = s.mbank()

        def emit(pe):
            ins = None
            for kb in range(nkb):
                nr = 128 if kb < nfull else rem
                ins = pe.matmul(s.ps[0:nq, bk, 0:dv], lhsT=WT[0:nr, kb, 0:nq], rhs=vfn(kb, nr),
                                start=(kb == 0), stop=(kb == nkb - 1))
            return ins
        s.mm_group(emit, [WT.r] + list(vres), s.psr[bk])
        return bk

    def pv(s, Wt, nq, nk, WT, vfn, vres, dv):
        s.pv_T(Wt, nq, nk, WT)
        return s.pv_mm(nq, nk, WT, vfn, vres, dv)

    def compute_lam(s, l):
        c, I = s.c, s.I
        lam_init = 0.8 - 0.6 * math.exp(-0.3 * l)
        with s.scope() as st:
            dl = s.sb(st, "dl", [128, 256], F32)
            pr = s.sb(st, "dlp", [128, 128], F32)
            s.load(dl[:], I["diff_lambda"][l], dl)
            s.dve(lambda e: e.tensor_tensor(out=pr[:, 0:64], in0=dl[:, 0:64], in1=dl[:, 64:128], op=ALU.mult), [dl.r], [pr.r])
            s.dve(lambda e: e.tensor_tensor(out=pr[:, 64:128], in0=dl[:, 128:192], in1=dl[:, 192:256], op=ALU.mult), [dl.r], [pr.r])
            s.dve(lambda e: e.tensor_reduce(out=s.st[:, 0:2], in_=pr[:, :].rearrange("p (a b) -> p a b", a=2),
                                            axis=AX.X, op=ALU.add), [pr.r], [s.st.r])
            s.act(s.st[:, 2:4], s.st[:, 0:2], AF.Exp, [s.st.r], [s.st.r])
            s.dve(lambda e: e.tensor_tensor(out=s.st[:, 4:5], in0=s.st[:, 3:4], in1=s.st[:, 2:3], op=ALU.subtract), [], [s.st.r])
            s.dve(lambda e: e.tensor_scalar(out=s.st[:, 4:5], in0=s.st[:, 4:5], scalar1=-lam_init, scalar2=None, op0=ALU.add),
                  [], [s.st.r])
            s.sch.barrier()
        return lam_init

    def ph_attn(s, l, kind):
        c, I, X = s.c, s.I, s.X
        diff = kind == "diff"
        heads = c.DH if diff else c.MH
        mixbase = c.PCH if diff else c.PCH + c.DH
        scale = 64 ** -0.5 if diff else 192 ** -0.5
        kkey, vkey, kskey, vskey = ("kT", "vv", "kTs", "vs") if diff else ("knT", "vh", "knTs", "vhs")
        NCMP = 2 if diff else 1
        if diff:
            lam_init = s.compute_lam(l)
        for hg0 in range(0, heads, 8):
            nhg = min(8, heads - hg0)
            for samp in (False, True):
                with s.scope() as st:
                    nkmax = c.NKS if samp else c.P
                    nkbmax = (nkmax + 127) // 128
                    nqmax = c.S if samp else 128
                    if not samp:
                        KT = s.sb(st, "KT", [128, nhg, c.P], BF)
                        s.load(KT[:], X[kkey][hg0:hg0 + nhg].rearrange("h p t -> p h t"), KT, [s.XR[kkey]])
                        V = s.sb(st, "V", [128, c.NB, nhg * 128], BF)
                        s.load(V[:], X[vkey][:, hg0 * 128:(hg0 + nhg) * 128].rearrange("(kb p) e -> p kb e", p=128),
                               V, [s.XR[vkey]])
                        if not diff:
                            KP = s.sb(st, "KP", [64, c.P], BF)
                            s.load(KP[:], X["kpeT"], KP, [s.XR["kpeT"]])
                    else:
                        KTs = [s.sb(st, "KTs", [128, c.NKS], BF) for _ in range(2)]
                        Vs = [s.sb(st, "Vs", [128, nkbmax, 128], BF) for _ in range(3)]
                        if not diff:
                            KP = s.sb(st, "KPs", [64, c.NKS], BF)
                            s.load(KP[:], X["kpeTs"], KP, [s.XR["kpeTs"]])
                    qb = [s.sb(st, "qb", [128, nhg, nqmax], BF) for _ in range(2)]
                    qp = [s.sb(st, "qp", [64, nhg, nqmax], BF) for _ in range(2)] if not diff else None
                    Ssb = [[s.sb(st, "Ssb", [nqmax, nkmax], F32) for _ in range(NCMP)] for _ in range(2)]
                    E = [[s.sb(st, "E", [nqmax, nkmax], BF) for _ in range(NCMP)] for _ in range(2)]
                    stts = [s.sb(st, "astt", [128, 16], F32) for _ in range(3)]
                    if diff:
                        nW = 2
                        nT = 1 if samp else 2
                        Wc = [s.sb(st, "Wc", [nqmax, nkmax], BF) for _ in range(nW)]
                        tmp = [s.sb(st, "Wtmp", [nqmax, nkmax], F32) for _ in range(nT)]
                        oall = [s.sb(st, "oall", [nqmax, nhg, 128], F32) for _ in range(2)]
                        osq = [s.sb(st, "osq", [nqmax, nhg, 128], F32) for _ in range(2)]
                        estt = s.sb(st, "estt", [128, 16], F32)
                        gsub = s.sb(st, "gsub", [128, 128], F32)
                        s.load(gsub[:], I["g_diff_sub"][l], gsub)
                    WT = [s.sb(st, "WT", [128, nkbmax, nqmax], BF) for _ in range(2)]
                    ob = [s.sb(st, "ob", [nqmax, nhg * 128], BF) for _ in range(2)]
                    ts = [s.sb(st, "ats", [128, nhg, nqmax], BF) for _ in range(2)]
                    blocks = [(c.NB, c.P, c.S)] if samp else [(i, i * 128, 128) for i in range(c.NB)]
                    items = [(bx, h) for bx in range(len(blocks)) for h in range(nhg)]
                    qkey = "qT" if diff else "qnT"
                    nfull_s = c.NKS // 128
                    rem_s = c.NKS - nfull_s * 128

                    def load_q(bx):
                        bi, c0, nq = blocks[bx]
                        q = qb[bx % 2]
                        s.load(q[:, :, 0:nq], X[qkey][hg0:hg0 + nhg, :, c0:c0 + nq].rearrange("h p t -> p h t"),
                               q, [s.XR[qkey]])
                        if not diff:
                            qq = qp[bx % 2]
                            s.load(qq[:, :, 0:nq], X["qpT"][hg0:hg0 + nhg, :, c0:c0 + nq].rearrange("h p t -> p h t"),
                                   qq, [s.XR["qpT"]])

                    def load_sample_k(h):
                        s.load(KTs[h % 2][:], X[kskey][hg0 + h], KTs[h % 2], [s.XR[kskey]])

                    def load_sample_v(h):
                        hh = hg0 + h
                        Vt = Vs[h % 3]
                        s.load(Vt[:, 0:nfull_s, :],
                               X[vskey][0:nfull_s * 128, hh * 128:(hh + 1) * 128].rearrange("(kb p) e -> p kb e", p=128),
                               Vt, [s.XR[vskey]])
                        if rem_s:
                            s.load(Vt[0:rem_s, nfull_s, :], X[vskey][nfull_s * 128:c.NKS, hh * 128:(hh + 1) * 128],
                                   Vt, [s.XR[vskey]])

                    def geom(t):
                        bx, h = items[t]
                        bi, c0, nq = blocks[bx]
                        nk = c.NKS if samp else (bi + 1) * 128
                        return bx, h, bi, c0, nq, nk

                    def S1(t):
                        bx, h, bi, c0, nq, nk = geom(t)
                        if h == 0 and bx + 1 < len(blocks):
                            load_q(bx + 1)
                        q = qb[bx % 2]
                        stt = stts[t % 3]
                        if samp:
                            if h + 1 < nhg:
                                load_sample_k(h + 1)
                            load_sample_v(h)
                            Kt = KTs[h % 2]
                            kf_full = lambda k0, w: Kt[:, k0:k0 + w]
                            kf_lo = lambda k0, w: Kt[0:64, k0:k0 + w]
                            kf_hi = lambda k0, w: Kt[64:128, k0:k0 + w]
                            kres = Kt.r
                        else:
                            kf_full = lambda k0, w: KT[:, h, k0:k0 + w]
                            kf_lo = lambda k0, w: KT[0:64, h, k0:k0 + w]
                            kf_hi = lambda k0, w: KT[64:128, h, k0:k0 + w]
                            kres = KT.r
                        mask = not samp
                        if diff:
                            s.sm_scores([(q[0:64, h, 0:nq], q.r, kf_lo, kres)], nq, nk, scale, mask, Ssb[t % 2][0], stt, 0)
                            s.sm_scores([(q[64:128, h, 0:nq], q.r, kf_hi, kres)], nq, nk, scale, mask, Ssb[t % 2][1], stt, 4)
                        else:
                            qq = qp[bx % 2]
                            kp_f = lambda k0, w: KP[0:64, k0:k0 + w]
                            s.sm_scores([(q[:, h, 0:nq], q.r, kf_full, kres), (qq[0:64, h, 0:nq], qq.r, kp_f, KP.r)],
                                        nq, nk, scale, mask, Ssb[t % 2][0], stt, 0)

                    def S2(t):
                        bx, h, bi, c0, nq, nk = geom(t)
                        stt = stts[t % 3]
                        for cm in range(NCMP):
                            s.sm_exp(nq, nk, Ssb[t % 2][cm], E[t % 2][cm], stt, 4 * cm)
                        if diff:
                            E0, E1 = E[t % 2]
                            wc, tm = Wc[t % nW], tmp[t % nT]
                            s.dve(lambda e: e.reciprocal(out=stt[0:nq, 8:9], in_=stt[0:nq, 2:3]), [], [stt.r])
                            s.dve(lambda e: e.reciprocal(out=stt[0:nq, 9:10], in_=stt[0:nq, 6:7]), [], [stt.r])
                            s.dve(lambda e: e.tensor_tensor(out=stt[0:nq, 9:10], in0=stt[0:nq, 9:10],
                                                            in1=s.st[0:nq, 4:5], op=ALU.mult), [s.st.r], [stt.r])
                            s.pool(lambda e: e.tensor_tensor(out=tm[0:nq, 0:nk], in0=E1[0:nq, 0:nk],
                                                             in1=stt[0:nq, 9:10].broadcast_to([nq, nk]), op=ALU.mult),
                                   [E1.r, stt.r], [tm.r])
                            s.dve(lambda e: e.scalar_tensor_tensor(out=wc[0:nq, 0:nk], in0=E0[0:nq, 0:nk],
                                                                   scalar=stt[0:nq, 8:9], in1=tm[0:nq, 0:nk],
                                                                   op0=ALU.mult, op1=ALU.add),
                                  [E0.r, tm.r, stt.r], [wc.r])
                        else:
                            s.dve(lambda e: e.reciprocal(out=stt[0:nq, 8:9], in_=stt[0:nq, 2:3]), [], [stt.r])

                    def S3a(t):
                        bx, h, bi, c0, nq, nk = geom(t)
                        src = Wc[t % nW] if diff else E[t % 2][0]
                        s.pv_T(src, nq, nk, WT[t % 2])

                    def S3b(t):
                        bx, h, bi, c0, nq, nk = geom(t)
                        stt = stts[t % 3]
                        o = ob[bx % 2]
                        wt = WT[t % 2]
                        if samp:
                            Vt = Vs[h % 3]
                            vfn = lambda kb, nr: Vt[0:nr, kb, :]
                            vres = [Vt.r]
                        else:
                            vfn = lambda kb, nr: V[0:nr, kb, h * 128:(h + 1) * 128]
                            vres = [V.r]
                        bk = s.pv_mm(nq, nk, wt, vfn, vres, 128)
                        if diff:
                            oa = oall[bx % 2]
                            s.act(oa[0:nq, h, :], s.ps[0:nq, bk, 0:128], AF.Copy, [s.psr[bk]], [oa.r])
                        else:
                            s.act(o[0:nq, h * 128:(h + 1) * 128], s.ps[0:nq, bk, 0:128], AF.Copy,
                                  [s.psr[bk], stt.r], [o.r], scale=stt[0:nq, 8:9])
                        if h != nhg - 1:
                            return
                        if diff:
                            oa, oq = oall[bx % 2], osq[bx % 2]
                            s.pool(lambda e: e.tensor_tensor(out=oq[0:nq], in0=oa[0:nq], in1=oa[0:nq], op=ALU.mult),
                                   [oa.r], [oq.r])
                            s.dve(lambda e: e.tensor_reduce(out=estt[0:nq, 0:nhg], in_=oq[0:nq], axis=AX.X, op=ALU.add),
                                  [oq.r], [estt.r])
                            s.rstd(estt[0:nq, 0:nhg], estt[0:nq, 0:nhg], nq, 128, [], [estt.r])
                            s.dve(lambda e: e.tensor_tensor(out=oq[0:nq], in0=oa[0:nq],
                                                            in1=estt[0:nq, 0:nhg, None].broadcast_to([nq, nhg, 128]),
                                                            op=ALU.mult), [oa.r, estt.r], [oq.r])
                            s.dve(lambda e: e.scalar_tensor_tensor(
                                out=o[0:nq, :].rearrange("p (h e) -> p h e", e=128), in0=oq[0:nq],
                                scalar=1.0 - lam_init, in1=gsub[0:nq, None, :].broadcast_to([nq, nhg, 128]),
                                op0=ALU.mult, op1=ALU.mult), [oq.r, gsub.r], [o.r])
                        tt = ts[bx % 2]

                        def ev(j0, cnt, pv2, bank):
                            s.act(tt[:, j0:j0 + cnt, 0:nq], pv2, AF.Copy, [bank], [tt.r])
                        s.transposes(o, lambda j: o[0:nq, j * 128:(j + 1) * 128], nhg, nq, ev)
                        mb = mixbase + hg0
                        s.store(X["mixT"][mb:mb + nhg, :, c0:c0 + nq].rearrange("h p t -> p h t"),
                                tt[:, 0:nhg, 0:nq], tt, [s.XR["mixT"]])

                    load_q(0)
                    if samp:
                        load_sample_k(0)
                    n = len(items)
                    for t in range(n + 2):
                        if 0 <= t - 2 < n:
                            S3a(t - 2)
                        if t < n:
                            S1(t)
                        if 0 <= t - 1 < n:
                            S2(t - 1)
                        if 0 <= t - 2 < n:
                            S3b(t - 2)
                s.sch.barrier()

    def ph_diffattn(s, l):
        s.ph_attn(l, "diff")

    def ph_mlaattn(s, l):
        s.ph_attn(l, "mla")

    def tm_gemm_to_fs(s, st, xT, KCn, W, l_unused=None):
        c, X = s.c, s.X
        slab = [s.sb(st, "oslab", [128, KCn, 512], BF) for _ in range(2)]
        o32 = [s.sb(st, "oo32", [128, 512], F32) for _ in range(3)]
        ng = c.D // 512
        s.load_slab(slab[0], W, 0, 512)
        it = 0
        for g in range(ng):
            sl = slab[g % 2]
            if g + 1 < ng:
                s.load_slab(slab[(g + 1) % 2], W, (g + 1) * 512, 512)
            for (c0, nt) in c.tbs:
                bk = s.mbank()

                def emit(pe, sl=sl, c0=c0, nt=nt, bk=bk):
                    ins = None
                    for k in range(KCn):
                        ins = pe.matmul(s.ps[0:nt, bk, :], lhsT=xT[:, k, c0:c0 + nt], rhs=sl[:, k, :],
                                        start=(k == 0), stop=(k == KCn - 1))
                    return ins
                s.mm_group(emit, [sl.r, xT.r], s.psr[bk])
                it += 1
                o = o32[it % 3]
                if it % 2:
                    s.act(o[0:nt, :], s.ps[0:nt, bk, :], AF.Copy, [s.psr[bk]], [o.r])
                else:
                    s.dve(lambda e, o=o, nt=nt, bk=bk: e.tensor_copy(out=o[0:nt, :], in_=s.ps[0:nt, bk, :]),
                          [s.psr[bk]], [o.r])
                s.store(X["fs"][c0:c0 + nt, g * 512:(g + 1) * 512], o[0:nt, :], o, [s.XR["fs"]])

    def ph_outproj(s, l):
        with s.scope() as st:
            mixT = s.load_xT(st, "mixr", "mixT")
            s.tm_gemm_to_fs(st, mixT, s.c.KC, s.I["w_out"][l])

    def ph_cross(s, l):
        c, I, X = s.c, s.I, s.X
        with s.scope() as st0:
            cqT = s.sb(st0, "xqT", [128, 4, c.NT], BF)
            oT = s.sb(st0, "xoT", [128, 4, c.NT], BF)
            with s.scope() as st:
                hT = s.load_xT(st, "hTr", "hT")
                slab = s.sb(st, "xslab", [128, c.KC, 512], BF)
                s.load_slab(slab, I["w_mem_q"][l], 0, 512)
                it = 0
                for ch in range(4):
                    for (c0, w) in c.tgs:
                        bk = s.mbank()

                        def emit(pe, ch=ch, c0=c0, w=w, bk=bk):
                            ins = None
                            for k in range(c.KC):
                                ins = pe.matmul(s.ps[:, bk, 0:w], lhsT=slab[:, k, ch * 128:(ch + 1) * 128],
                                                rhs=hT[:, k, c0:c0 + w], start=(k == 0), stop=(k == c.KC - 1))
                            return ins
                        s.mm_group(emit, [slab.r, hT.r], s.psr[bk])
                        it += 1
                        if it % 2:
                            s.act(cqT[:, ch, c0:c0 + w], s.ps[:, bk, 0:w], AF.Copy, [s.psr[bk]], [cqT.r])
                        else:
                            s.dve(lambda e, ch=ch, c0=c0, w=w, bk=bk: e.tensor_copy(out=cqT[:, ch, c0:c0 + w],
                                                                                   in_=s.ps[:, bk, 0:w]),
                                  [s.psr[bk]], [cqT.r])
            s.sch.barrier()
            with s.scope() as st:
                nmb = c.MEM // 128
                KM = {False: s.sb(st, "xkm", [128, 4, c.MEM], BF), True: s.sb(st, "xkms", [128, 4, c.MEM], BF)}
                VM = {False: s.sb(st, "xvm", [128, nmb, 512], BF), True: s.sb(st, "xvms", [128, nmb, 512], BF)}
                s.load(KM[False][:], X["mkT"].rearrange("h p m -> p h m"), KM[False], [s.XR["mkT"]])
                s.load(KM[True][:], X["mkTs"].rearrange("h p m -> p h m"), KM[True], [s.XR["mkTs"]])
                s.load(VM[False][:], X["mv"].rearrange("(kb p) e -> p kb e", p=128), VM[False], [s.XR["mv"]])
                s.load(VM[True][:], X["mvs"].rearrange("(kb p) e -> p kb e", p=128), VM[True], [s.XR["mvs"]])
                Ssb = s.sb(st, "xS", [128, c.MEM], F32)
                E = [s.sb(st, "xE", [128, c.MEM], BF) for _ in range(2)]
                WT = [s.sb(st, "xWT", [128, nmb, 128], BF) for _ in range(2)]
                ob = [s.sb(st, "xob", [128, 512], BF) for _ in range(2)]
                stt = s.sb(st, "xstt", [128, 16], F32)
                for bi, (c0, nq) in enumerate(c.tbs):
                    samp = c0 >= c.P
                    K, V = KM[samp], VM[samp]
                    o = ob[bi % 2]
                    for h in range(4):
                        Eh = E[h % 2]
                        s.softmax([(cqT[:, h, c0:c0 + nq], cqT.r, lambda k0, w, h=h, K=K: K[:, h, k0:k0 + w], K.r)],
                                  nq, c.MEM, 128 ** -0.5, False, Ssb, Eh, stt, 0)
                        s.dve(lambda e: e.reciprocal(out=stt[0:nq, 8:9], in_=stt[0:nq, 2:3]), [], [stt.r])
                        bk = s.pv(Eh, nq, c.MEM, WT[h % 2], lambda kb, nr, h=h, V=V: V[0:nr, kb, h * 128:(h + 1) * 128],
                                  [V.r], 128)
                        s.act(o[0:nq, h * 128:(h + 1) * 128], s.ps[0:nq, bk, 0:128], AF.Copy, [s.psr[bk], stt.r], [o.r],
                              scale=stt[0:nq, 8:9])

                    def ev(j0, cnt, pv2, bank, c0=c0, nq=nq):
                        s.dve(lambda e: e.tensor_copy(out=oT[:, j0:j0 + cnt, c0:c0 + nq], in_=pv2), [bank], [oT.r])
                    s.transposes(o, lambda j, o=o, nq=nq: o[0:nq, j * 128:(j + 1) * 128], 4, nq, ev)
            s.sch.barrier()
            with s.scope() as st:
                s.tm_gemm_to_fs(st, oT, 4, I["w_mem_o"][l])

    def ph_ffn1(s, l):
        c, I, X = s.c, s.I, s.X
        GW = 128
        ng = c.DFF // GW
        with s.scope() as st:
            hT = s.load_xT(st, "hTr", "hT")
            sg = [s.sb(st, "fsg", [128, c.KC, GW], BF) for _ in range(2)]
            su = [s.sb(st, "fsu", [128, c.KC, GW], BF) for _ in range(2)]
            g32 = [s.sb(st, "fg32", [128, 512], F32) for _ in range(2)]
            ast = [s.sb(st, "fast", [128, c.NT], BF) for _ in range(2)]
            s.load_slab(sg[0], I["w_gate"][l], 0, GW)
            s.load_slab(su[0], I["w_up"][l], 0, GW)
            it = 0
            for g in range(ng):
                wg, wu = sg[g % 2], su[g % 2]
                if g + 1 < ng:
                    s.load_slab(sg[(g + 1) % 2], I["w_gate"][l], (g + 1) * GW, GW)
                    s.load_slab(su[(g + 1) % 2], I["w_up"][l], (g + 1) * GW, GW)
                for j in range(GW // 128):
                    a = ast[(g * (GW // 128) + j) % 2]
                    for (c0, w) in c.tgs:
                        bg, bu = s.mbank(), s.mbank()

                        def emit(pe, wt=wg, j=j, c0=c0, w=w, bk=bg):
                            ins = None
                            for k in range(c.KC):
                                ins = pe.matmul(s.ps[:, bk, 0:w], lhsT=wt[:, k, j * 128:(j + 1) * 128],
                                                rhs=hT[:, k, c0:c0 + w], start=(k == 0), stop=(k == c.KC - 1))
                            return ins
                        s.mm_group(emit, [wg.r, hT.r], s.psr[bg])
                        s.mm_group(lambda pe, e_=emit, wt=wu, bk=bu: e_(pe, wt=wt, bk=bk), [wu.r, hT.r], s.psr[bu])
                        it += 1
                        t32 = g32[it % 2]
                        s.act(t32[:, 0:w], s.ps[:, bg, 0:w], AF.Silu, [s.psr[bg]], [t32.r])
                        s.dve(lambda e, a=a, t32=t32, c0=c0, w=w, bu=bu: e.tensor_tensor(
                            out=a[:, c0:c0 + w], in0=t32[:, 0:w], in1=s.ps[:, bu, 0:w], op=ALU.mult),
                            [t32.r, s.psr[bu]], [a.r])
                    s.store(X["actT"][g * (GW // 128) + j], a[:], a, [s.XR["actT"]])

    def ph_ffn2(s, l):
        c, I, X = s.c, s.I, s.X
        FB = 8
        RB = 6
        regions = [c.tbs[i:i + RB] for i in range(0, len(c.tbs), RB)]
        nfg = (c.FC + FB - 1) // FB
        ndg = c.D // 512
        with s.scope() as st:
            rcmax = max(sum(nt for _, nt in r) for r in regions)
            wsl = [s.sb(st, "f2w", [128, FB, 512], BF) for _ in range(nfg)]
            ab = [s.sb(st, "f2a", [128, FB, rcmax], BF) for _ in range(3)]
            o32 = [s.sb(st, "f2o", [128, 512], F32) for _ in range(3)]
            Wd = I["w_down"][l]

            def load_w(dg, fg):
                f0 = fg * FB
                nb = min(FB, c.FC - f0)
                s.load(wsl[fg][:, 0:nb, :],
                       Wd[f0 * 128:(f0 + nb) * 128, dg * 512:(dg + 1) * 512].rearrange("(f p) n -> p f n", p=128),
                       wsl[fg], q="pool")
            for fg in range(nfg):
                load_w(0, fg)
            items = [(dg, ri, fg) for dg in range(ndg) for ri in range(len(regions)) for fg in range(nfg)]

            def load_a(j):
                dg, ri, fg = items[j]
                reg = regions[ri]
                r0 = reg[0][0]
                rc = sum(nt for _, nt in reg)
                f0 = fg * FB
                nb = min(FB, c.FC - f0)
                a = ab[j % 3]
                s.load(a[:, 0:nb, 0:rc], X["actT"][f0:f0 + nb, :, r0:r0 + rc].rearrange("f p t -> p f t"),
                       a, [s.XR["actT"]])
            PF = 2
            for j in range(min(PF, len(items))):
                load_a(j)
            io = 0
            for j, (dg, ri, fg) in enumerate(items):
                if j + PF < len(items):
                    load_a(j + PF)
                reg = regions[ri]
                r0 = reg[0][0]
                banks = list(range(len(reg)))
                f0 = fg * FB
                nb = min(FB, c.FC - f0)
                a, w = ab[j % 3], wsl[fg]

                def emit(pe, a=a, w=w, f0=f0, nb=nb, reg=reg, r0=r0):
                    ins = None
                    for f in range(nb):
                        for i, (c0, nt) in enumerate(reg):
                            ins = pe.matmul(s.ps[0:nt, i, :], lhsT=a[:, f, c0 - r0:c0 - r0 + nt], rhs=w[:, f, :],
                                            start=(f0 + f == 0), stop=(f0 + f == c.FC - 1))
                    if f0 + nb < c.FC:
                        ins = pe.matmul(s.ps[0:1, 7, 0:1], lhsT=a[:, 0, 0:1], rhs=w[:, 0, 0:1], start=True, stop=True)
                    return ins
                s.sch.op("pe", emit, [a.r, w.r], [s.psr[7]] + [s.psr[i] for i in banks])
                if ri == len(regions) - 1 and dg + 1 < ndg:
                    load_w(dg + 1, fg)
                if fg != nfg - 1:
                    continue
                for i, (c0, nt) in enumerate(reg):
                    o = o32[io % 3]
                    io += 1
                    if io % 2:
                        s.act(o[0:nt, :], s.ps[0:nt, i, :], AF.Copy, [s.psr[i]], [o.r])
                    else:
                        s.dve(lambda e, o=o, nt=nt, i=i: e.tensor_copy(out=o[0:nt, :], in_=s.ps[0:nt, i, :]),
                              [s.psr[i]], [o.r])
                    s.store(X["fs"][c0:c0 + nt, dg * 512:(dg + 1) * 512], o[0:nt, :], o, [s.XR["fs"]])


def _rope_tables(cfg):
    half = 32
    inv = (np.float32(10000.0) ** (-np.arange(half, dtype=np.float32) / np.float32(half))).astype(np.float32)
    pos = np.concatenate([np.arange(cfg.P), cfg.PAST + np.arange(cfg.S)]).astype(np.float32)
    ang = (pos[:, None] * inv[None, :]).astype(np.float32)
    cos, sin = np.cos(ang).astype(np.float32), np.sin(ang).astype(np.float32)
    rc = np.zeros((128, cfg.NB + 1, 32), np.float32)
    rs = np.zeros((128, cfg.NB + 1, 32), np.float32)
    for bi, (c0, nt) in enumerate(cfg.tbs):
        rc[:nt, bi] = cos[c0:c0 + nt]
        rs[:nt, bi] = sin[c0:c0 + nt]
    return rc, rs


def _col(g, L):
    n = g.shape[1]
    return np.ascontiguousarray(g.reshape(L, n // 128, 128).transpose(0, 2, 1))


def _rep(g, L):
    return np.ascontiguousarray(np.broadcast_to(g.reshape(L, 1, -1), (L, 128, g.reshape(L, -1).shape[1])))


def make_in_maps(cfg, inp, ncores):
    L = cfg.L
    f = lambda a: np.ascontiguousarray(np.asarray(a, dtype=np.float32))
    rc, rs = _rope_tables(cfg)
    rcnt = np.zeros((128, 4, 16), np.float32)
    for g in range(4):
        w = 2 ** (g + 1)
        rcnt[:, g, :] = (1.0 / np.minimum(w, np.arange(16) + 1)).astype(np.float32)[None, :]
    shared = {
        "rcos": rc, "rsin": rs, "rcnt": rcnt,
        "identb": np.eye(128).astype(ml_dtypes.bfloat16), "identf": np.eye(128, dtype=np.float32),
        "memp": None,
    }
    for nm in ("g_mix_pre", "g_mem", "g_x_pre", "g_ff_pre", "g_mla_q", "pool_scale"):
        shared[nm] = _col(f(inp[nm]), L)
    for nm in ("g_mix_post", "g_x_post", "g_ff_post", "g_mla_kv", "g_diff_sub"):
        shared[nm] = _rep(f(inp[nm]), L)
    shared["diff_lambda"] = _rep(f(inp["diff_lambda"]).reshape(L, -1), L)
    for nm in ("w_in", "w_pool", "w_mla_uq", "w_out", "w_mem_k", "w_mem_v", "w_mem_q", "w_mem_o",
               "w_gate", "w_up", "w_down"):
        shared[nm] = f(inp[nm])
    shared["w_mla_uk"] = f(inp["w_mla_uk"]).reshape(L, 512, -1)
    shared["w_mla_uv"] = f(inp["w_mla_uv"]).reshape(L, 512, -1)
    maps = []
    for b in range(ncores):
        m = dict(shared)
        m["xp"] = f(inp["x_prompt"][b])
        m["xs"] = f(inp["x_sample"][b])
        m["cdk"] = f(np.asarray(inp["cache_diff_k"])[:, b]).reshape(L, cfg.PAST, -1)
        m["cdv"] = f(np.asarray(inp["cache_diff_v"])[:, b]).reshape(L, cfg.PAST, -1)
        m["cckv"] = f(np.asarray(inp["cache_mla_ckv"])[:, b])
        m["ckpe"] = f(np.asarray(inp["cache_mla_kpe"])[:, b])
        m["cpool"] = f(np.asarray(inp["cache_pool"])[:, b])
        m["cmk"] = f(np.asarray(inp["cache_mem_k"])[:, b]).reshape(L, cfg.MEM, -1)
        m["cmv"] = f(np.asarray(inp["cache_mem_v"])[:, b]).reshape(L, cfg.MEM, -1)
        m["memp"] = f(inp["mem_prompt"][b])
        maps.append(m)
    return maps


def gather_outputs(cfg, res, ncores):
    L = cfg.L
    st = lambda k: np.stack([np.asarray(res[b][k], dtype=np.float32) for b in range(ncores)])
    yp, ys = st("y_p"), st("y_s")

    def lb(k, tail):
        a = st(k)
        a = np.swapaxes(a, 0, 1)
        return np.ascontiguousarray(a.reshape(a.shape[:3] + tuple(tail)))
    return (yp, ys,
            lb("pdk", (cfg.DH, 128)), lb("pdv", (cfg.DH, 128)), lb("pckv", (512,)), lb("pkpe", (64,)),
            lb("ppool", (cfg.PW,)), lb("pmk", (4, 128)), lb("pmv", (4, 128)),
            lb("sdk", (cfg.DH, 128)), lb("sdv", (cfg.DH, 128)), lb("sckv", (512,)), lb("skpe", (64,)),
            lb("spool", (cfg.PW,)))


_CACHE = {}


def kernel(**inputs):
    cfg = Cfg()
    ncores = 8
    if "nc" not in _CACHE:
        _CACHE["nc"] = Builder(cfg).build()
    nc = _CACHE["nc"]
    in_maps = make_in_maps(cfg, inputs, ncores)
    res = run_bass_kernel_spmd(nc, in_maps, core_ids=list(range(ncores)))
    return gather_outputs(cfg, res.results, ncores)
```

```python
import math
from contextlib import ExitStack

import numpy as np
import ml_dtypes
import concourse.bass as bass
import concourse.mybir as mybir
from concourse.bass_utils import run_bass_kernel_spmd

F32 = mybir.dt.float32
BF = mybir.dt.bfloat16
AF = mybir.ActivationFunctionType
ALU = mybir.AluOpType
AX = mybir.AxisListType
EPS = 1e-6
NEG = -30000.0


class Cfg:
    def __init__(s, D=4096, SEQ=2048, DEC=16, PAST=4096, MEM=256, DFF=11008, L=2):
        s.D, s.P, s.S, s.PAST, s.MEM, s.DFF, s.L = D, SEQ, DEC, PAST, MEM, DFF, L
        s.KC = D // 128
        s.PW = D // 4
        s.PCH = s.PW // 128
        s.PG = s.PW // 4
        s.CPG = s.PG // 128
        s.DH = D // 4 // 128
        s.DW = s.DH * 128
        s.MH = D // 2 // 128
        s.QR = D // 4
        s.QRC = s.QR // 128
        s.UQW = s.MH * 192
        s.INW = s.PW + 3 * s.DW + s.QR + 512 + 64
        s.NT = SEQ + DEC
        s.NB = SEQ // 128
        s.FC = DFF // 128
        s.NKS = PAST + DEC
        s.tbs = [(i * 128, 128) for i in range(s.NB)] + [(SEQ, DEC)]
        s.tgs = []
        c = 0
        while c < SEQ:
            w = min(512, SEQ - c)
            s.tgs.append((c, w))
            c += w
        s.tgs.append((SEQ, DEC))


class Res:
    __slots__ = ("name", "w", "r", "sem", "cnt")

    def __init__(s, name):
        s.name, s.w, s.r, s.sem, s.cnt = name, {}, {}, None, 0


class T:
    def __init__(s, h, name):
        s.h, s.r = h, Res(name)

    def __getitem__(s, k):
        return s.h[k]


class Sched:
    def __init__(s, nc, es):
        s.nc, s.es = nc, es
        s.E = {"pe": nc.tensor, "act": nc.scalar, "dve": nc.vector, "pool": nc.gpsimd, "sp": nc.sync}
        s.csem = {e: es.enter_context(nc.semaphore("c_" + e)) for e in ("pe", "act", "dve", "pool")}
        s.ccnt = {e: 0 for e in s.csem}
        s.waited = {e: {} for e in s.E}
        s.nsem = 4
        s.phys = []
        s.inuse = []
        s.free = {"hw": [], "sw": []}

    def _deps(s, reads, writes):
        ev = {}
        for r in reads:
            for k, v in r.w.items():
                if v[1] > ev.get(k, (None, 0))[1]:
                    ev[k] = v
        for w in writes:
            for d in (w.w, w.r):
                for k, v in d.items():
                    if v[1] > ev.get(k, (None, 0))[1]:
                        ev[k] = v
        return ev

    def _wait(s, e, ev):
        for k, (sem, val) in ev.items():
            if e == "pe" and k == "c_pe":
                continue
            if s.waited[e].get(k, 0) < val:
                s.E[e].wait_ge(sem, val)
                s.waited[e][k] = val

    def _commit(s, key, evt, reads, writes):
        for w in writes:
            w.w[key] = evt
        for r in reads:
            r.r[key] = evt

    def op(s, e, emit, reads=(), writes=()):
        s._wait(e, s._deps(reads, writes))
        ins = emit(s.E[e])
        s.ccnt[e] += 1
        ins.then_inc(s.csem[e], 1)
        s._commit("c_" + e, (s.csem[e], s.ccnt[e]), reads, writes)

    def dma(s, q, out, in_, lane, reads=(), writes=()):
        s._wait(q, s._deps(reads, writes))
        kind = "sw" if q == "pool" else "hw"
        if lane.sem is None:
            lane.sem = {}
        if kind not in lane.sem:
            if s.free[kind]:
                ph = s.free[kind].pop()
            else:
                nm = "d%s%d" % (kind, len(s.phys))
                ph = [s.es.enter_context(s.nc.semaphore(nm)), nm, 0]
                s.phys.append(ph)
                s.nsem += 1
            lane.sem[kind] = ph
            s.inuse.append((lane, kind))
        ph = lane.sem[kind]
        ins = s.E[q].dma_start(out=out, in_=in_)
        ph[2] += 16
        ins.then_inc(ph[0], 16)
        s._commit(ph[1], (ph[0], ph[2]), reads, writes)

    def barrier(s):
        ev = {"c_" + e: (s.csem[e], s.ccnt[e]) for e in s.csem if s.ccnt[e]}
        for ph in s.phys:
            if ph[2]:
                ev[ph[1]] = (ph[0], ph[2])
        for e in s.E:
            s._wait(e, ev)
        for lane, kind in s.inuse:
            s.free[kind].append(lane.sem[kind])
            del lane.sem[kind]
        s.inuse = []

    def finish(s):
        s.barrier()


class Builder:
    def __init__(s, cfg, upto=99):
        s.c = cfg
        s.upto = upto
        s.nc = bass.Bass("TRN2", target_bir_lowering=False)
        s.es = ExitStack()
        s.sch = Sched(s.nc, s.es)
        s.uid = 0

    def scope(s):
        b = s

        class _Scope(ExitStack):
            def __exit__(self, *a):
                if a[0] is None:
                    b.sch.barrier()
                return super().__exit__(*a)
        return _Scope()

    def din(s, name, shape, dt=F32):
        return s.nc.dram_tensor(name, list(shape), dt, kind="ExternalInput").ap()

    def dout(s, name, shape, dt=F32):
        return s.nc.dram_tensor(name, list(shape), dt, kind="ExternalOutput").ap()

    def dscr(s, name, shape, dt=BF):
        return s.nc.dram_tensor(name, list(shape), dt, kind="Internal").ap()

    def sb(s, st, name, shape, dt):
        s.uid += 1
        nm = "%s_%d" % (name, s.uid)
        return T(st.enter_context(s.nc.sbuf_tensor(nm, list(shape), dt)), nm)

    def act(s, out, in_, func, reads, writes, **kw):
        s.sch.op("act", lambda e: e.activation(out=out, in_=in_, func=func, **kw), reads, writes)

    def dve(s, emit, reads, writes):
        s.sch.op("dve", emit, reads, writes)

    def pool(s, emit, reads, writes):
        s.sch.op("pool", emit, reads, writes)

    def load(s, dst_ap, src_ap, dst_t, src_res=(), q="sp"):
        s.sch.dma(q, dst_ap, src_ap, dst_t.r, reads=list(src_res), writes=[dst_t.r])

    def store(s, dst_ap, src_ap, src_t, dst_res=(), q="sp"):
        s.sch.dma(q, dst_ap, src_ap, src_t.r, reads=[src_t.r], writes=list(dst_res))

    def mm_group(s, emit, reads, bank):
        s.sch.op("pe", emit, reads, [bank])

    def rstd(s, out, ss, n, dim, reads, writes):
        s.act(out, ss, AF.Sqrt, list(reads) + [s.epsb.r], writes, bias=s.epsb[0:n, 0:1], scale=1.0 / dim)
        s.dve(lambda e: e.reciprocal(out=out, in_=out), [], writes)

    def transposes(s, src_t, src_ap_fn, n, nrows, dst_fn, bf=True, ncols=128):
        per = (8 if bf else 4)
        ident = s.identb if bf else s.identf
        j = 0
        while j < n:
            cnt = min(per, n - j)
            bk = s.tbank()
            pv = s.ps[:, bk, :]
            if bf:
                pv = pv.bitcast(BF)
            pv = pv.rearrange("p (c t) -> p c t", c=per)

            def emit(pe, j=j, cnt=cnt, pv=pv):
                ins = None
                for i in range(cnt):
                    ins = pe.transpose(out=pv[0:ncols, i, 0:nrows], in_=src_ap_fn(j + i),
                                       identity=ident[0:nrows, 0:nrows])
                return ins
            s.sch.op("pe", emit, [src_t.r, s.identb.r], [s.psr[bk]])
            dst_fn(j, cnt, pv[0:ncols, 0:cnt, 0:nrows], s.psr[bk])
            j += cnt

    def tbank(s):
        s._tb = (s._tb + 1) % 2
        return 6 + s._tb

    def mbank(s):
        s._mb = (s._mb + 1) % 6
        return s._mb

    def build(s):
        c, nc, es = s.c, s.nc, s.es
        L = c.L
        I = {}
        I["xp"] = s.din("xp", [c.P, c.D])
        I["xs"] = s.din("xs", [c.S, c.D])
        I["cdk"] = s.din("cdk", [L, c.PAST, c.DW])
        I["cdv"] = s.din("cdv", [L, c.PAST, c.DW])
        I["cckv"] = s.din("cckv", [L, c.PAST, 512])
        I["ckpe"] = s.din("ckpe", [L, c.PAST, 64])
        I["cpool"] = s.din("cpool", [L, 15, c.PW])
        I["cmk"] = s.din("cmk", [L, c.MEM, 512])
        I["cmv"] = s.din("cmv", [L, c.MEM, 512])
        I["memp"] = s.din("memp", [c.MEM, c.D])
        for nm in ("g_mix_pre", "g_mem", "g_x_pre", "g_ff_pre"):
            I[nm] = s.din(nm, [L, 128, c.KC])
        for nm in ("g_mix_post", "g_x_post", "g_ff_post"):
            I[nm] = s.din(nm, [L, 128, c.D])
        I["w_in"] = s.din("w_in", [L, c.D, c.INW])
        I["w_pool"] = s.din("w_pool", [L, 4, c.PG, c.PG])
        I["pool_scale"] = s.din("pool_scale", [L, 128, c.PCH])
        I["diff_lambda"] = s.din("diff_lambda", [L, 128, 256])
        I["g_diff_sub"] = s.din("g_diff_sub", [L, 128, 128])
        I["g_mla_q"] = s.din("g_mla_q", [L, 128, c.QRC])
        I["w_mla_uq"] = s.din("w_mla_uq", [L, c.QR, c.UQW])
        I["w_mla_uk"] = s.din("w_mla_uk", [L, 512, c.MH * 128])
        I["w_mla_uv"] = s.din("w_mla_uv", [L, 512, c.MH * 128])
        I["g_mla_kv"] = s.din("g_mla_kv", [L, 128, 512])
        I["w_out"] = s.din("w_out", [L, c.D, c.D])
        I["w_mem_k"] = s.din("w_mem_k", [L, c.D, 512])
        I["w_mem_v"] = s.din("w_mem_v", [L, c.D, 512])
        I["w_mem_q"] = s.din("w_mem_q", [L, c.D, 512])
        I["w_mem_o"] = s.din("w_mem_o", [L, 512, c.D])
        I["w_gate"] = s.din("w_gate", [L, c.D, c.DFF])
        I["w_up"] = s.din("w_up", [L, c.D, c.DFF])
        I["w_down"] = s.din("w_down", [L, c.DFF, c.D])
        I["rcos"] = s.din("rcos", [128, c.NB + 1, 32])
        I["rsin"] = s.din("rsin", [128, c.NB + 1, 32])
        I["rcnt"] = s.din("rcnt", [128, 4, 16])
        I["identb"] = s.din("identb", [128, 128], BF)
        I["identf"] = s.din("identf", [128, 128])
        s.I = I
        O = {}
        O["y_p"] = s.dout("y_p", [c.P, c.D])
        O["y_s"] = s.dout("y_s", [c.S, c.D])
        O["pdk"] = s.dout("pdk", [L, c.P, c.DW])
        O["pdv"] = s.dout("pdv", [L, c.P, c.DW])
        O["pckv"] = s.dout("pckv", [L, c.P, 512])
        O["pkpe"] = s.dout("pkpe", [L, c.P, 64])
        O["ppool"] = s.dout("ppool", [L, 15, c.PW])
        O["pmk"] = s.dout("pmk", [L, c.MEM, 512])
        O["pmv"] = s.dout("pmv", [L, c.MEM, 512])
        O["sdk"] = s.dout("sdk", [L, c.S, c.DW])
        O["sdv"] = s.dout("sdv", [L, c.S, c.DW])
        O["sckv"] = s.dout("sckv", [L, c.S, 512])
        O["skpe"] = s.dout("skpe", [L, c.S, 64])
        O["spool"] = s.dout("spool", [L, 15, c.PW])
        s.O = O
        X = {}
        X["xres"] = s.dscr("xres", [c.NT, c.D], F32)
        X["fs"] = s.dscr("fs", [c.NT, c.D], F32)
        X["hT"] = s.dscr("hT", [c.KC, 128, c.NT])
        X["mixT"] = s.dscr("mixT", [c.KC, 128, c.NT])
        X["dT"] = s.dscr("dT", [c.PCH, 128, c.NT])
        X["qT"] = s.dscr("qT", [c.DH, 128, c.NT])
        X["kT"] = s.dscr("kT", [c.DH, 128, c.P])
        X["kTs"] = s.dscr("kTs", [c.DH, 128, c.NKS])
        X["vv"] = s.dscr("vv", [c.P, c.DW])
        X["vs"] = s.dscr("vs", [c.NKS, c.DW])
        X["cq"] = s.dscr("cq", [c.NT, c.QR])
        X["ckvT"] = s.dscr("ckvT", [4, 128, c.P])
        X["ckvTs"] = s.dscr("ckvTs", [4, 128, c.NKS])
        X["kpeT"] = s.dscr("kpeT", [64, c.P])
        X["kpeTs"] = s.dscr("kpeTs", [64, c.NKS])
        X["knT"] = s.dscr("knT", [c.MH, 128, c.P])
        X["knTs"] = s.dscr("knTs", [c.MH, 128, c.NKS])
        X["vh"] = s.dscr("vh", [c.P, c.MH * 128])
        X["vhs"] = s.dscr("vhs", [c.NKS, c.MH * 128])
        X["qnT"] = s.dscr("qnT", [c.MH, 128, c.NT])
        X["qpT"] = s.dscr("qpT", [c.MH, 64, c.NT])
        X["mkT"] = s.dscr("mkT", [4, 128, c.MEM])
        X["mkTs"] = s.dscr("mkTs", [4, 128, c.MEM])
        X["mv"] = s.dscr("mv", [c.MEM, 512])
        X["mvs"] = s.dscr("mvs", [c.MEM, 512])
        X["actT"] = s.dscr("actT", [c.FC, 128, c.NT])
        s.X = X
        s.XR = {k: Res("x_" + k) for k in X}
        s.OR = Res("outputs")

        g = es
        s.ps_h = g.enter_context(nc.psum_tensor("ps", [128, 8, 512], F32))
        s.ps = s.ps_h
        s.psr = [Res("psb%d" % i) for i in range(8)]
        s._tb = 0
        s._mb = 0
        s.identb = s.sb(g, "identb", [128, 128], BF)
        s.identf = s.sb(g, "identf", [128, 128], F32)
        s.rcos = s.sb(g, "rcos", [128, c.NB + 1, 32], F32)
        s.rsin = s.sb(g, "rsin", [128, c.NB + 1, 32], F32)
        s.rcnt = s.sb(g, "rcnt", [128, 4, 16], F32)
        s.st = s.sb(g, "stats", [128, 16], F32)
        s.epsb = s.sb(g, "epsb", [128, 1], F32)
        s.dve(lambda e: e.memset(s.epsb[:], EPS), [], [s.epsb.r])
        s.load(s.identb[:], I["identb"], s.identb)
        s.load(s.identf[:], I["identf"], s.identf)
        s.load(s.rcos[:], I["rcos"], s.rcos)
        s.load(s.rsin[:], I["rsin"], s.rsin)
        s.load(s.rcnt[:], I["rcnt"], s.rcnt)

        ph = 0
        for l in range(L):
            steps = [
                lambda l=l: s.ph_memkv(l),
                lambda l=l: s.ph_norm(l, 0),
                lambda l=l: s.ph_pool(l),
                lambda l=l: s.ph_inproj(l),
                lambda l=l: s.ph_cache(l),
                lambda l=l: s.ph_mlaproj(l),
                lambda l=l: s.ph_diffattn(l),
                lambda l=l: s.ph_mlaattn(l),
                lambda l=l: s.ph_outproj(l),
                lambda l=l: s.ph_norm(l, 1),
                lambda l=l: s.ph_cross(l),
                lambda l=l: s.ph_norm(l, 2),
                lambda l=l: s.ph_ffn1(l),
                lambda l=l: s.ph_ffn2(l),
            ]
            for st in steps:
                if ph < s.upto:
                    st()
                    s.sch.barrier()
                ph += 1
        if ph < s.upto + 1:
            s.ph_norm(L, 0, final=True)
        s.sch.finish()
        es.close()
        return nc

    def ph_norm(s, l, which, final=False):
        c, I, X, O = s.c, s.I, s.X, s.O
        with s.scope() as st:
            has_f = not (l == 0 and which == 0)
            if which == 0:
                gpost = I["g_ff_post"][l - 1] if has_f else None
                gpre = None if final else I["g_mix_pre"][l]
            elif which == 1:
                gpost, gpre = I["g_mix_post"][l], I["g_x_pre"][l]
            else:
                gpost, gpre = I["g_x_post"][l], I["g_ff_pre"][l]
            xb = [s.sb(st, "nx", [128, c.D], F32) for _ in range(2)]
            fb = [s.sb(st, "nf", [128, c.D], F32) for _ in range(2)] if has_f else None
            junk = s.sb(st, "njunk", [128, c.D], BF)
            G = s.sb(st, "nG", [128, c.D], F32) if has_f else None
            gc = s.sb(st, "ngc", [128, c.KC], F32) if gpre is not None else None
            hb = [s.sb(st, "nhb", [128, c.D], BF) for _ in range(2)] if gpre is not None else None
            hs = [s.sb(st, "nhs", [128, c.KC, 512], BF) for _ in range(2)] if gpre is not None else None
            stt = s.sb(st, "nst", [128, 8], F32)
            if has_f:
                s.load(G[:], gpost, G)
            if gpre is not None:
                s.load(gc[:], gpre, gc)
            first = (l == 0 and which == 0)
            hslot = -1
            hcount = 0
            fg = s.sb(st, "nfg", [128, c.D], F32) if has_f else None

            def issue_loads(bi):
                c0, nt = c.tbs[bi]
                x = xb[bi % 2]
                if first:
                    src = I["xp"][c0:c0 + nt, :] if c0 < c.P else I["xs"][:, :]
                    s.load(x[0:nt, :], src, x)
                else:
                    s.load(x[0:nt, :], X["xres"][c0:c0 + nt, :], x, [s.XR["xres"]])
                if has_f:
                    f = fb[bi % 2]
                    s.load(f[0:nt, :], X["fs"][c0:c0 + nt, :], f, [s.XR["fs"]])
            issue_loads(0)
            for bi, (c0, nt) in enumerate(c.tbs):
                sl = bi % 2
                x, = (xb[sl],)
                if bi + 1 < len(c.tbs):
                    issue_loads(bi + 1)
                if has_f:
                    f = fb[sl]
                    s.act(junk[0:nt, :], f[0:nt, :], AF.Square, [f.r], [junk.r, stt.r], accum_out=stt[0:nt, 0:1])
                    s.dve(lambda e, f=f, nt=nt: e.tensor_tensor(
                        out=fg[0:nt, :], in0=f[0:nt, :], in1=G[0:nt, :], op=ALU.mult), [f.r, G.r], [fg.r])
                    s.rstd(stt[0:nt, 1:2], stt[0:nt, 0:1], nt, c.D, [], [stt.r])
                    s.dve(lambda e, x=x, nt=nt: e.scalar_tensor_tensor(
                        out=x[0:nt, :], in0=fg[0:nt, :], scalar=stt[0:nt, 1:2], in1=x[0:nt, :],
                        op0=ALU.mult, op1=ALU.add), [stt.r, fg.r], [x.r])
                    if final:
                        dst = O["y_p"][c0:c0 + nt, :] if c0 < c.P else O["y_s"][:, :]
                        s.store(dst, x[0:nt, :], x, [s.OR])
                    else:
                        s.store(X["xres"][c0:c0 + nt, :], x[0:nt, :], x, [s.XR["xres"]])
                elif first:
                    s.store(X["xres"][c0:c0 + nt, :], x[0:nt, :], x, [s.XR["xres"]])
                if gpre is None:
                    continue
                h = hb[sl]
                s.act(junk[0:nt, :], x[0:nt, :], AF.Square, [x.r], [junk.r, stt.r], accum_out=stt[0:nt, 2:3])
                s.rstd(stt[0:nt, 3:4], stt[0:nt, 2:3], nt, c.D, [], [stt.r])
                s.act(h[0:nt, :], x[0:nt, :], AF.Copy, [x.r, stt.r], [h.r], scale=stt[0:nt, 3:4])
                if hcount % 4 == 0 or nt != 128:
                    hslot += 1
                    hcol0 = c0
                    hoff = 0
                hst = hs[hslot % 2]

                def evac(j0, cnt, pv, bank, hst=hst, hoff=hoff, nt=nt):
                    s.dve(lambda e: e.tensor_tensor(
                        out=hst[:, j0:j0 + cnt, hoff:hoff + nt], in0=pv,
                        in1=gc[:, j0:j0 + cnt, None].broadcast_to([128, cnt, nt]), op=ALU.mult),
                        [bank, gc.r], [hst.r])
                s.transposes(h, lambda j, h=h, nt=nt: h[0:nt, j * 128:(j + 1) * 128], c.KC, nt, evac)
                hoff += nt
                hcount += 1
                last = (bi + 1 == len(c.tbs)) or (hcount % 4 == 0) or (c.tbs[bi + 1][1] != 128)
                if last:
                    s.store(X["hT"].rearrange("k p t -> p k t")[:, :, hcol0:hcol0 + hoff],
                            hst[:, :, 0:hoff], hst, [s.XR["hT"]])
                    hcount = 0

    def load_xT(s, st, name, key, col0=0, ncol=None):
        src = s.X[key]
        nch = src.shape[0]
        ncol = src.shape[2] - col0 if ncol is None else ncol
        t = s.sb(st, name, [128, nch, ncol], BF)
        step = max(1, nch // 4)
        for k0 in range(0, nch, step):
            k1 = min(nch, k0 + step)
            s.load(t[:, k0:k1, :], src.rearrange("k p t -> p k t")[:, k0:k1, col0:col0 + ncol], t, [s.XR[key]])
        return t

    def load_slab(s, slab, w_ap, c0, cw, kc0=0, kcn=None):
        kcn = (w_ap.shape[0] // 128 - kc0) if kcn is None else kcn
        src = w_ap[kc0 * 128:(kc0 + kcn) * 128, c0:c0 + cw].rearrange("(k p) n -> p k n", p=128)
        step = max(1, kcn // 2)
        for k0 in range(0, kcn, step):
            k1 = min(kcn, k0 + step)
            s.load(slab[:, k0:k1, 0:cw], src[:, k0:k1, :], slab, q="pool")

    def ph_memkv(s, l):
        c, I, X, O = s.c, s.I, s.X, s.O
        nmb = c.MEM // 128
        with s.scope() as st:
            xb = [s.sb(st, "mx", [128, c.D], F32) for _ in range(2)]
            junk = s.sb(st, "mjunk", [128, c.D], BF)
            hb = s.sb(st, "mhb", [128, c.D], BF)
            gc = s.sb(st, "mgc", [128, c.KC], F32)
            mT = s.sb(st, "mT", [128, c.KC, c.MEM], BF)
            slab = [s.sb(st, "mslab", [128, c.KC, 512], BF) for _ in range(2)]
            o32 = [s.sb(st, "mo32", [128, 512], F32) for _ in range(2)]
            ob = [s.sb(st, "mob", [128, 512], BF) for _ in range(2)]
            kts = s.sb(st, "mkts", [128, 4, c.MEM], BF)
            stt = s.sb(st, "mst", [128, 4], F32)
            s.load(gc[:], I["g_mem"][l], gc)
            s.load_slab(slab[0], I["w_mem_k"][l], 0, 512)
            s.load_slab(slab[1], I["w_mem_v"][l], 0, 512)
            for mb in range(nmb):
                x = xb[mb % 2]
                s.load(x[:], I["memp"][mb * 128:(mb + 1) * 128, :], x)
                s.act(junk[:], x[:], AF.Square, [x.r], [junk.r, stt.r], accum_out=stt[:, 0:1])
                s.rstd(stt[:, 1:2], stt[:, 0:1], 128, c.D, [], [stt.r])
                s.act(hb[:], x[:], AF.Copy, [x.r, stt.r], [hb.r], scale=stt[:, 1:2])

                def evac(j0, cnt, pv, bank, mb=mb):
                    s.dve(lambda e: e.tensor_tensor(
                        out=mT[:, j0:j0 + cnt, mb * 128:(mb + 1) * 128], in0=pv,
                        in1=gc[:, j0:j0 + cnt, None].broadcast_to([128, cnt, 128]), op=ALU.mult),
                        [bank, gc.r], [mT.r])
                s.transposes(hb, lambda j: hb[:, j * 128:(j + 1) * 128], c.KC, 128, evac)

            def kpost(obt, mb, dst_key, final):
                def ev2(j0, cnt, pv, bank):
                    s.dve(lambda e: e.tensor_copy(out=kts[:, j0:j0 + cnt, mb * 128:(mb + 1) * 128], in_=pv),
                          [bank], [kts.r])
                s.transposes(obt, lambda j: obt[:, j * 128:(j + 1) * 128], 4, 128, ev2)
                if final:
                    s.store(X[dst_key].rearrange("h p m -> p h m"), kts[:], kts, [s.XR[dst_key]])

            for wi in range(2):
                for mb in range(nmb):
                    bk = s.mbank()

                    def emit(pe, wi=wi, mb=mb, bk=bk):
                        ins = None
                        for k in range(c.KC):
                            ins = pe.matmul(s.ps[:, bk, :], lhsT=mT[:, k, mb * 128:(mb + 1) * 128],
                                            rhs=slab[wi][:, k, :], start=(k == 0), stop=(k == c.KC - 1))
                        return ins
                    s.mm_group(emit, [mT.r, slab[wi].r], s.psr[bk])
                    o, b = o32[mb % 2], ob[mb % 2]
                    s.act(o[:], s.ps[:, bk, :], AF.Copy, [s.psr[bk]], [o.r])
                    s.dve(lambda e, o=o, b=b: e.tensor_copy(out=b[:], in_=o[:]), [o.r], [b.r])
                    okey = "pmk" if wi == 0 else "pmv"
                    s.store(O[okey][l, mb * 128:(mb + 1) * 128, :], o[:], o, [s.OR])
                    if wi == 0:
                        kpost(b, mb, "mkT", mb == nmb - 1)
                    else:
                        s.store(X["mv"][mb * 128:(mb + 1) * 128, :], b[:], b, [s.XR["mv"]])
            for mb in range(nmb):
                b = ob[mb % 2]
                s.load(b[:], I["cmk"][l, mb * 128:(mb + 1) * 128, :], b, q="pool")
                kpost(b, mb, "mkTs", mb == nmb - 1)
            for mb in range(nmb):
                b = ob[mb % 2]
                s.load(b[:], I["cmv"][l, mb * 128:(mb + 1) * 128, :], b, q="pool")
                s.store(X["mvs"][mb * 128:(mb + 1) * 128, :], b[:], b, [s.XR["mvs"]])

    def ph_pool(s, l):
        c, I, X, O = s.c, s.I, s.X, s.O
        EP = 15 + c.P
        ES = 15 + c.S
        with s.scope() as st:
            hT = s.load_xT(st, "hTr", "hT")
            slab = [s.sb(st, "pslab", [128, c.KC, 128], BF) for _ in range(2)]
            ue = s.sb(st, "ue", [128, EP + ES], F32)
            A = s.sb(st, "pA", [128, EP + ES], F32)
            B = s.sb(st, "pB", [128, EP + ES], F32)
            dst = [s.sb(st, "pd", [128, c.NT], BF) for _ in range(2)]
            cp = s.sb(st, "cp", [15, c.PW], F32)
            po = s.sb(st, "po", [15, 2, c.PW], F32)
            s.load(cp[:], I["cpool"][l], cp)
            s.dve(lambda e: e.memset(ue[:, 0:15], 0.0), [], [ue.r])
            s.dve(lambda e: e.memset(A[:], 0.0), [], [A.r])
            s.pool(lambda e: e.memset(B[:], 0.0), [], [B.r])
            s.load_slab(slab[0], I["w_in"][l], 0, 128)
            for ch in range(c.PCH):
                sl = slab[ch % 2]
                if ch + 1 < c.PCH:
                    s.load_slab(slab[(ch + 1) % 2], I["w_in"][l], (ch + 1) * 128, 128)

                def evc(j0, cnt, pv, bank):
                    s.dve(lambda e: e.tensor_copy(out=ue[:, EP:EP + 15], in_=pv[:, 0, :]), [bank], [ue.r])
                s.transposes(cp, lambda j, ch=ch: cp[0:15, ch * 128:(ch + 1) * 128], 1, 15, evc, bf=False)
                for (c0, w) in c.tgs:
                    bk = s.mbank()

                    def emit(pe, sl=sl, c0=c0, w=w, bk=bk):
                        ins = None
                        for k in range(c.KC):
                            ins = pe.matmul(s.ps[:, bk, 0:w], lhsT=sl[:, k, :], rhs=hT[:, k, c0:c0 + w],
                                            start=(k == 0), stop=(k == c.KC - 1))
                        return ins
                    s.mm_group(emit, [sl.r, hT.r], s.psr[bk])
                    e0 = 15 + c0 if c0 < c.P else EP + 15
                    s.act(ue[:, e0:e0 + w], s.ps[:, bk, 0:w], AF.Copy, [s.psr[bk]], [ue.r])
                for gi, e0 in ((0, EP - 15), (1, EP + ES - 15)):
                    def evp(j0, cnt, pv, bank, gi=gi, ch=ch):
                        s.dve(lambda e: e.tensor_copy(out=po[0:15, gi, ch * 128:(ch + 1) * 128], in_=pv[:, 0, :]),
                              [bank], [po.r])
                    s.transposes(ue, lambda j, e0=e0: ue[:, e0:e0 + 15], 1, 128, evp, bf=False, ncols=15)
                g = ch // c.CPG
                cur, bufs = ue, [A, B]
                sh = 1
                for lev in range(g + 1):
                    nxt = bufs[lev % 2]
                    s.dve(lambda e, cur=cur, nxt=nxt, sh=sh: e.tensor_tensor(
                        out=nxt[:, sh:EP + ES], in0=cur[:, sh:EP + ES], in1=cur[:, 0:EP + ES - sh], op=ALU.add),
                        [cur.r], [nxt.r])
                    cur = nxt
                    sh *= 2
                wdw = 2 ** (g + 1)
                other = bufs[(g + 1) % 2]
                d = dst[ch % 2]
                s.dve(lambda e, cur=cur, other=other, wdw=wdw: e.scalar_tensor_tensor(
                    out=other[:, 15:EP + ES], in0=cur[:, 15:EP + ES], scalar=1.0 / wdw, in1=ue[:, 15:EP + ES],
                    op0=ALU.mult, op1=ALU.subtract), [cur.r, ue.r], [other.r])
                s.dve(lambda e, cur=cur, g=g: e.tensor_tensor(
                    out=cur[:, 15:30], in0=cur[:, 15:30], in1=s.rcnt[:, g, 0:15], op=ALU.mult), [s.rcnt.r], [cur.r])
                s.dve(lambda e, cur=cur, other=other: e.tensor_tensor(
                    out=other[:, 15:30], in0=cur[:, 15:30], in1=ue[:, 15:30], op=ALU.subtract),
                    [cur.r, ue.r], [other.r])
                s.act(d[:, 0:c.P], other[:, 15:EP], AF.Copy, [other.r], [d.r])
                s.act(d[:, c.P:c.NT], other[:, EP + 15:EP + ES], AF.Copy, [other.r], [d.r])
                s.store(X["dT"][ch], d[:], d, [s.XR["dT"]])
            s.store(O["ppool"][l], po[0:15, 0, :], po, [s.OR])
            s.store(O["spool"][l], po[0:15, 1, :], po, [s.OR])
        with s.scope() as st:
            dT = s.load_xT(st, "dTr", "dT")
            wp = s.sb(st, "wp", [128, 4, c.CPG, c.PG], BF)
            psc = s.sb(st, "psc", [128, c.PCH], F32)
            ms = [s.sb(st, "pms", [128, c.NT], BF) for _ in range(2)]
            s.load(wp[:], I["w_pool"][l].rearrange("g (cc p) e -> p g cc e", p=128), wp, q="pool")
            s.load(psc[:], I["pool_scale"][l], psc)
            for g in range(4):
                for ec in range(c.CPG):
                    och = g * c.CPG + ec
                    m = ms[och % 2]
                    for (c0, w) in c.tgs:
                        bk = s.mbank()

                        def emit(pe, g=g, ec=ec, c0=c0, w=w, bk=bk):
                            ins = None
                            for cc in range(c.CPG):
                                ins = pe.matmul(s.ps[:, bk, 0:w], lhsT=wp[:, g, cc, ec * 128:(ec + 1) * 128],
                                                rhs=dT[:, g * c.CPG + cc, c0:c0 + w],
                                                start=(cc == 0), stop=(cc == c.CPG - 1))
                            return ins
                        s.mm_group(emit, [wp.r, dT.r], s.psr[bk])
                        s.act(m[:, c0:c0 + w], s.ps[:, bk, 0:w], AF.Copy, [s.psr[bk], psc.r], [m.r],
                              scale=psc[:, och:och + 1])
                    s.store(X["mixT"][och], m[:], m, [s.XR["mixT"]])

    def rope_tm(s, xs, ro, nt, W, bi):
        U = W // 64
        X4 = xs[0:nt, 0:W].rearrange("p (u h f) -> p u h f", h=2, f=32)
        R4 = ro[0:nt, 0:W].rearrange("p (u h f) -> p u h f", h=2, f=32)
        cs = s.rcos[0:nt, bi, None, :].broadcast_to([nt, U, 32])
        sn = s.rsin[0:nt, bi, None, :].broadcast_to([nt, U, 32])
        x1, x2 = X4[:, :, 0, :], X4[:, :, 1, :]
        t = s.ropet
        T4 = t[0:nt, 0:W].rearrange("p (u h f) -> p u h f", h=2, f=32)
        ta, tb = T4[:, :, 0, :], T4[:, :, 1, :]
        s.dve(lambda e: e.tensor_tensor(out=ta, in0=x2, in1=sn, op=ALU.mult), [xs.r, s.rsin.r], [t.r])
        s.dve(lambda e: e.tensor_tensor(out=R4[:, :, 0, :], in0=x1, in1=cs, op=ALU.mult), [xs.r, s.rcos.r], [ro.r])
        s.dve(lambda e: e.tensor_tensor(out=R4[:, :, 0, :], in0=R4[:, :, 0, :], in1=ta, op=ALU.subtract),
              [t.r], [ro.r])
        s.pool(lambda e: e.tensor_tensor(out=tb, in0=x1, in1=sn, op=ALU.mult), [xs.r, s.rsin.r], [t.r])
        s.pool(lambda e: e.tensor_tensor(out=R4[:, :, 1, :], in0=x2, in1=cs, op=ALU.mult), [xs.r, s.rcos.r], [ro.r])
        s.pool(lambda e: e.tensor_tensor(out=R4[:, :, 1, :], in0=R4[:, :, 1, :], in1=tb, op=ALU.add),
               [t.r], [ro.r])

    def ph_inproj(s, l):
        c, I, X, O = s.c, s.I, s.X, s.O
        W = I["w_in"][l]
        groups = []
        off = c.PW
        for kind, width in (("q", c.DW), ("k", c.DW), ("v", c.DW), ("cq", c.QR), ("ckv", 512), ("kpe", 64)):
            o = 0
            while o < width:
                gw = min(512, width - o)
                groups.append((kind, o, off + o, gw))
                o += gw
            off += width
        nbh = (c.NB + 1) // 2
        halves = [c.tbs[0:nbh], c.tbs[nbh:]] if c.NB > 1 else [c.tbs]
        for hv in halves:
            s.inproj_half(l, groups, hv)
            s.sch.barrier()

    def inproj_half(s, l, groups, tbl):
        c, I, X, O = s.c, s.I, s.X, s.O
        W = I["w_in"][l]
        hc0 = tbl[0][0]
        hcn = sum(nt for _, nt in tbl)
        bi0 = c.tbs.index(tbl[0])
        with s.scope() as st:
            hT = s.load_xT(st, "hTr", "hT", hc0, hcn)
            slab = [s.sb(st, "islab", [128, c.KC, 512], BF) for _ in range(2)]
            x32 = [s.sb(st, "ix32", [128, 512], F32) for _ in range(2)]
            r32 = [s.sb(st, "ir32", [128, 512], F32) for _ in range(2)]
            rb = [s.sb(st, "irb", [128, 512], BF) for _ in range(2)]
            s.ropet = s.sb(st, "ropet", [128, 512], F32)
            tst = [s.sb(st, "itst", [128, 4, 128], BF) for _ in range(2)]
            gkv = s.sb(st, "gkv", [128, 512], F32)
            junk = s.sb(st, "ijunk", [128, 512], F32)
            stt = s.sb(st, "ist", [128, 4], F32)
            s.load(gkv[:], I["g_mla_kv"][l], gkv)
            s.load_slab(slab[0], W, groups[0][2], groups[0][3])
            it = 0
            for gi, (kind, o, wc0, gw) in enumerate(groups):
                sl = slab[gi % 2]
                if gi + 1 < len(groups):
                    s.load_slab(slab[(gi + 1) % 2], W, groups[gi + 1][2], groups[gi + 1][3])
                tslot = 0
                for bi_, (c0, nt) in enumerate(tbl):
                    bi = bi0 + bi_
                    bk = s.mbank()
                    samp = c0 >= c.P

                    def emit(pe, sl=sl, c0=c0, nt=nt, bk=bk, gw=gw):
                        ins = None
                        for k in range(c.KC):
                            ins = pe.matmul(s.ps[0:nt, bk, 0:gw], lhsT=hT[:, k, c0 - hc0:c0 - hc0 + nt], rhs=sl[:, k, 0:gw],
                                            start=(k == 0), stop=(k == c.KC - 1))
                        return ins
                    s.mm_group(emit, [sl.r, hT.r], s.psr[bk])
                    it += 1
                    xs, ro, b = x32[it % 2], r32[it % 2], rb[it % 2]
                    pv = s.ps[0:nt, bk, 0:gw]
                    nh = gw // 128 if gw >= 128 else 1
                    tw = 128 if gw >= 128 else gw

                    def tr_store(b, key, skey, hb0):
                        nonlocal tslot
                        tslot += 1
                        ts = tst[tslot % 2]

                        def ev(j0, cnt, pv2, bank):
                            s.act(ts[0:tw, j0:j0 + cnt, 0:nt], pv2, AF.Copy, [bank], [ts.r])
                        s.transposes(b, lambda j: b[0:nt, j * tw:(j + 1) * tw], nh, nt, ev, ncols=tw)
                        kk = skey if samp else key
                        cc = c.PAST if (samp and skey != key) else c0
                        if len(X[kk].shape) == 3:
                            dstap = X[kk][hb0:hb0 + nh, 0:tw, cc:cc + nt].rearrange("h p t -> p h t")
                            srcap = ts[0:tw, 0:nh, 0:nt]
                        else:
                            dstap = X[kk][0:tw, cc:cc + nt]
                            srcap = ts[0:tw, 0, 0:nt]
                        s.store(dstap, srcap, ts, [s.XR[kk]])

                    if kind in ("q", "k", "kpe"):
                        s.act(xs[0:nt, 0:gw], pv, AF.Copy, [s.psr[bk]], [xs.r])
                        s.rope_tm(xs, ro, nt, gw, bi)
                        s.act(b[0:nt, 0:gw], ro[0:nt, 0:gw], AF.Copy, [ro.r], [b.r])
                        if kind == "k":
                            dst = (O["sdk"][l, :, o:o + gw] if samp else O["pdk"][l, c0:c0 + nt, o:o + gw])
                            s.store(dst, ro[0:nt, 0:gw], ro, [s.OR])
                            tr_store(b, "kT", "kTs", o // 128)
                        elif kind == "kpe":
                            dst = (O["skpe"][l, :, :] if samp else O["pkpe"][l, c0:c0 + nt, :])
                            s.store(dst, ro[0:nt, 0:gw], ro, [s.OR])
                            tr_store(b, "kpeT", "kpeTs", 0)
                        else:
                            tr_store(b, "qT", "qT", o // 128)
                    elif kind == "v":
                        s.act(ro[0:nt, 0:gw], pv, AF.Copy, [s.psr[bk]], [ro.r])
                        s.dve(lambda e, ro=ro, b=b, nt=nt, gw=gw: e.tensor_copy(out=b[0:nt, 0:gw], in_=ro[0:nt, 0:gw]),
                              [ro.r], [b.r])
                        dst = (O["sdv"][l, :, o:o + gw] if samp else O["pdv"][l, c0:c0 + nt, o:o + gw])
                        s.store(dst, ro[0:nt, 0:gw], ro, [s.OR])
                        if samp:
                            s.store(X["vs"][c.PAST:c.PAST + nt, o:o + gw], b[0:nt, 0:gw], b, [s.XR["vs"]])
                        else:
                            s.store(X["vv"][c0:c0 + nt, o:o + gw], b[0:nt, 0:gw], b, [s.XR["vv"]])
                    elif kind == "cq":
                        s.act(b[0:nt, 0:gw], pv, AF.Copy, [s.psr[bk]], [b.r])
                        s.store(X["cq"][c0:c0 + nt, o:o + gw], b[0:nt, 0:gw], b, [s.XR["cq"]])
                    elif kind == "ckv":
                        s.act(junk[0:nt, :], pv, AF.Square, [s.psr[bk]], [junk.r, stt.r], accum_out=stt[0:nt, 0:1])
                        s.rstd(stt[0:nt, 1:2], stt[0:nt, 0:1], nt, 512, [], [stt.r])
                        s.dve(lambda e, ro=ro, pv=pv, nt=nt: e.scalar_tensor_tensor(
                            out=ro[0:nt, :], in0=pv, scalar=stt[0:nt, 1:2], in1=gkv[0:nt, :],
                            op0=ALU.mult, op1=ALU.mult), [s.psr[bk], stt.r, gkv.r], [ro.r])
                        s.act(b[0:nt, :], ro[0:nt, :], AF.Copy, [ro.r], [b.r])
                        dst = (O["sckv"][l, :, :] if samp else O["pckv"][l, c0:c0 + nt, :])
                        s.store(dst, ro[0:nt, :], ro, [s.OR])
                        tr_store(b, "ckvT", "ckvTs", 0)

    def ph_cache(s, l):
        c, I, X = s.c, s.I, s.X
        nkb = c.PAST // 128
        with s.scope() as st:
            cb = [s.sb(st, "ccb", [128, 4, c.DW], BF) for _ in range(2)]
            ts = [s.sb(st, "cts", [128, c.DH, 512], BF) for _ in range(2)]
            for key, skey, width, vkey in (("cdk", "kTs", c.DW, None), ("cdv", None, c.DW, "vs"),
                                           ("cckv", "ckvTs", 512, None), ("ckpe", "kpeTs", 64, None)):
                nh = max(1, width // 128)
                tw = min(128, width)
                for q0 in range(0, nkb, 4):
                    nq = min(4, nkb - q0)
                    b = cb[(q0 // 4) % 2]
                    t = ts[(q0 // 4) % 2]
                    s.load(b[:, 0:nq, 0:width],
                           I[key][l, q0 * 128:(q0 + nq) * 128, :].rearrange("(q p) w -> p q w", p=128), b, q="pool")
                    if vkey is not None:
                        s.store(X[vkey][q0 * 128:(q0 + nq) * 128, :].rearrange("(q p) w -> p q w", p=128),
                                b[:, 0:nq, 0:width], b, [s.XR[vkey]])
                        continue
                    for qq in range(nq):
                        def ev(j0, cnt, pv, bank, qq=qq, t=t):
                            s.dve(lambda e: e.tensor_copy(out=t[0:tw, j0:j0 + cnt, qq * 128:(qq + 1) * 128], in_=pv),
                                  [bank], [t.r])
                        s.transposes(b, lambda j, qq=qq, b=b: b[:, qq, j * tw:(j + 1) * tw], nh, 128, ev, ncols=tw)
                    if len(X[skey].shape) == 3:
                        s.store(X[skey][:, :, q0 * 128:(q0 + nq) * 128].rearrange("h p t -> p h t"),
                                t[0:tw, 0:nh, 0:nq * 128], t, [s.XR[skey]])
                    else:
                        s.store(X[skey][:, q0 * 128:(q0 + nq) * 128], t[0:tw, 0, 0:nq * 128], t, [s.XR[skey]])

    def ph_mlaproj(s, l):
        c, I, X = s.c, s.I, s.X
        with s.scope() as st:
            cqnT = s.sb(st, "cqnT", [128, c.QRC, c.NT], BF)
            wuq = s.sb(st, "wuq", [128, c.QRC, c.UQW], BF)
            gq = s.sb(st, "gq", [128, c.QRC], F32)
            s.load(gq[:], I["g_mla_q"][l], gq)
            nsp = 4
            step = c.UQW // nsp
            for i in range(nsp):
                s.load(wuq[:, :, i * step:(i + 1) * step],
                       I["w_mla_uq"][l][:, i * step:(i + 1) * step].rearrange("(k p) n -> p k n", p=128), wuq, q="pool")
            with s.scope() as st2:
                cb = [s.sb(st2, "cqb", [128, c.QR], BF) for _ in range(2)]
                cn = [s.sb(st2, "cqn", [128, c.QR], BF) for _ in range(2)]
                junk = s.sb(st2, "cqj", [128, c.QR], BF)
                stt = s.sb(st2, "cqs", [128, 4], F32)
                for bi, (c0, nt) in enumerate(c.tbs):
                    b, n = cb[bi % 2], cn[bi % 2]
                    s.load(b[0:nt, :], X["cq"][c0:c0 + nt, :], b, [s.XR["cq"]])
                    s.act(junk[0:nt, :], b[0:nt, :], AF.Square, [b.r], [junk.r, stt.r], accum_out=stt[0:nt, 0:1])
                    s.rstd(stt[0:nt, 1:2], stt[0:nt, 0:1], nt, c.QR, [], [stt.r])
                    s.act(n[0:nt, :], b[0:nt, :], AF.Copy, [b.r, stt.r], [n.r], scale=stt[0:nt, 1:2])

                    def ev(j0, cnt, pv, bank, c0=c0, nt=nt):
                        s.dve(lambda e: e.tensor_tensor(
                            out=cqnT[:, j0:j0 + cnt, c0:c0 + nt], in0=pv,
                            in1=gq[:, j0:j0 + cnt, None].broadcast_to([128, cnt, nt]), op=ALU.mult),
                            [bank, gq.r], [cqnT.r])
                    s.transposes(n, lambda j, n=n, nt=nt: n[0:nt, j * 128:(j + 1) * 128], c.QRC, nt, ev)
            with s.scope() as st2:
                qs = [s.sb(st2, "qns", [128, c.NT], BF) for _ in range(2)]
                for h in range(c.MH):
                    q = qs[h % 2]
                    for ti, (c0, w) in enumerate(c.tgs):
                        bk = s.mbank()

                        def emit(pe, h=h, c0=c0, w=w, bk=bk):
                            ins = None
                            for k in range(c.QRC):
                                ins = pe.matmul(s.ps[:, bk, 0:w], lhsT=wuq[:, k, h * 192:h * 192 + 128],
                                                rhs=cqnT[:, k, c0:c0 + w], start=(k == 0), stop=(k == c.QRC - 1))
                            return ins
                        s.mm_group(emit, [wuq.r, cqnT.r], s.psr[bk])
                        if ti % 2 == 0:
                            s.act(q[:, c0:c0 + w], s.ps[:, bk, 0:w], AF.Copy, [s.psr[bk]], [q.r])
                        else:
                            s.dve(lambda e, q=q, c0=c0, w=w, bk=bk: e.tensor_copy(out=q[:, c0:c0 + w], in_=s.ps[:, bk, 0:w]),
                                  [s.psr[bk]], [q.r])
                    s.store(X["qnT"][h], q[:], q, [s.XR["qnT"]])
                x32 = [s.sb(st2, "qx32", [128, 512], F32) for _ in range(2)]
                r32 = [s.sb(st2, "qr32", [128, 512], F32) for _ in range(2)]
                rb = [s.sb(st2, "qrb", [128, 512], BF) for _ in range(2)]
                s.ropet = s.sb(st2, "ropet2", [128, 512], F32)
                tst = [s.sb(st2, "qtst", [64, 8, 128], BF) for _ in range(2)]
                wv = wuq[:, :, :].rearrange("p k (h c) -> p k h c", c=192)
                it = 0
                for hg0 in range(0, c.MH, 8):
                    nh = min(8, c.MH - hg0)
                    gw = nh * 64
                    for bi, (c0, nt) in enumerate(c.tbs):
                        bk = s.mbank()

                        def emit(pe, hg0=hg0, nh=nh, c0=c0, nt=nt, bk=bk, gw=gw):
                            ins = None
                            for k in range(c.QRC):
                                ins = pe.matmul(s.ps[0:nt, bk, 0:gw].rearrange("p (h c) -> p h c", c=64),
                                                lhsT=cqnT[:, k, c0:c0 + nt], rhs=wv[:, k, hg0:hg0 + nh, 128:192],
                                                start=(k == 0), stop=(k == c.QRC - 1))
                            return ins
                        s.mm_group(emit, [wuq.r, cqnT.r], s.psr[bk])
                        it += 1
                        xs, ro, b, ts = x32[it % 2], r32[it % 2], rb[it % 2], tst[it % 2]
                        s.act(xs[0:nt, 0:gw], s.ps[0:nt, bk, 0:gw], AF.Copy, [s.psr[bk]], [xs.r])
                        s.rope_tm(xs, ro, nt, gw, bi)
                        s.act(b[0:nt, 0:gw], ro[0:nt, 0:gw], AF.Copy, [ro.r], [b.r])

                        def ev(j0, cnt, pv2, bank, ts=ts, nt=nt):
                            s.dve(lambda e: e.tensor_copy(out=ts[0:64, j0:j0 + cnt, 0:nt], in_=pv2), [bank], [ts.r])
                        s.transposes(b, lambda j, b=b, nt=nt: b[0:nt, j * 64:(j + 1) * 64], nh, nt, ev, ncols=64)
                        s.store(X["qpT"][hg0:hg0 + nh, :, c0:c0 + nt].rearrange("h p t -> p h t"),
                                ts[0:64, 0:nh, 0:nt], ts, [s.XR["qpT"]])
        with s.scope() as st:
            wuk = s.sb(st, "wuk", [128, 4, c.MH * 128], BF)
            wuv = s.sb(st, "wuv", [128, 4, c.MH * 128], BF)
            s.load(wuk[:], I["w_mla_uk"][l].rearrange("(k p) n -> p k n", p=128), wuk, q="pool")
            s.load(wuv[:], I["w_mla_uv"][l].rearrange("(k p) n -> p k n", p=128), wuv, q="pool")
            for ckey, nkey, vkey, nk in (("ckvT", "knT", "vh", c.P), ("ckvTs", "knTs", "vhs", c.NKS)):
                with s.scope() as st2:
                    ck = s.load_xT(st2, "ckr", ckey)
                    ks = [s.sb(st2, "kns", [128, nk], BF) for _ in range(2)]
                    vs_ = [s.sb(st2, "vhs", [128, c.MH * 128], BF) for _ in range(2)]
                    it = 0
                    for h in range(c.MH):
                        kk = ks[h % 2]
                        for k0 in range(0, nk, 512):
                            w = min(512, nk - k0)
                            bk = s.mbank()

                            def emit(pe, h=h, k0=k0, w=w, bk=bk):
                                ins = None
                                for r in range(4):
                                    ins = pe.matmul(s.ps[:, bk, 0:w], lhsT=wuk[:, r, h * 128:(h + 1) * 128],
                                                    rhs=ck[:, r, k0:k0 + w], start=(r == 0), stop=(r == 3))
                                return ins
                            s.mm_group(emit, [wuk.r, ck.r], s.psr[bk])
                            it += 1
                            if it % 2:
                                s.act(kk[:, k0:k0 + w], s.ps[:, bk, 0:w], AF.Copy, [s.psr[bk]], [kk.r])
                            else:
                                s.dve(lambda e, kk=kk, k0=k0, w=w, bk=bk: e.tensor_copy(out=kk[:, k0:k0 + w], in_=s.ps[:, bk, 0:w]),
                                      [s.psr[bk]], [kk.r])
                        s.store(X[nkey][h], kk[:], kk, [s.XR[nkey]])
                    for kb in range((nk + 127) // 128):
                        nr = min(128, nk - kb * 128)
                        vv_ = vs_[kb % 2]
                        for cg in range(0, c.MH * 128, 512):
                            bk = s.mbank()

                            def emit(pe, kb=kb, nr=nr, cg=cg, bk=bk):
                                ins = None
                                for r in range(4):
                                    ins = pe.matmul(s.ps[0:nr, bk, :], lhsT=ck[:, r, kb * 128:kb * 128 + nr],
                                                    rhs=wuv[:, r, cg:cg + 512], start=(r == 0), stop=(r == 3))
                                return ins
                            s.mm_group(emit, [wuv.r, ck.r], s.psr[bk])
                            it += 1
                            if it % 2:
                                s.act(vv_[0:nr, cg:cg + 512], s.ps[0:nr, bk, :], AF.Copy, [s.psr[bk]], [vv_.r])
                            else:
                                s.dve(lambda e, vv_=vv_, nr=nr, cg=cg, bk=bk: e.tensor_copy(
                                    out=vv_[0:nr, cg:cg + 512], in_=s.ps[0:nr, bk, :]), [s.psr[bk]], [vv_.r])
                        s.store(X[vkey][kb * 128:kb * 128 + nr, :], vv_[0:nr, :], vv_, [s.XR[vkey]])

    def sm_scores(s, chunks, nq, nk, scale, mask, Ssb, stt, col):
        for ci, k0 in enumerate(range(0, nk, 512)):
            w = min(512, nk - k0)
            bk = s.mbank()

            def emit(pe, k0=k0, w=w, bk=bk):
                ins = None
                for i, (qa, qr, kf, kr) in enumerate(chunks):
                    ins = pe.matmul(s.ps[0:nq, bk, 0:w], lhsT=qa, rhs=kf(k0, w),
                                    start=(i == 0), stop=(i == len(chunks) - 1))
                return ins
            s.mm_group(emit, [x[1] for x in chunks] + [x[3] for x in chunks], s.psr[bk])
            if ci % 3 == 2:
                s.dve(lambda e, k0=k0, w=w, bk=bk: e.tensor_scalar(
                    out=Ssb[0:nq, k0:k0 + w], in0=s.ps[0:nq, bk, 0:w], scalar1=scale, scalar2=None, op0=ALU.mult),
                    [s.psr[bk]], [Ssb.r])
            else:
                s.act(Ssb[0:nq, k0:k0 + w], s.ps[0:nq, bk, 0:w], AF.Copy, [s.psr[bk]], [Ssb.r], scale=scale)
        if mask:
            s.dve(lambda e: e.memset(Ssb[0:64, nk - 64:nk], NEG), [], [Ssb.r])
        s.dve(lambda e: e.tensor_reduce(out=stt[0:nq, col:col + 1], in_=Ssb[0:nq, 0:nk], axis=AX.X, op=ALU.max),
              [Ssb.r], [stt.r])
        s.dve(lambda e: e.tensor_scalar(out=stt[0:nq, col + 1:col + 2], in0=stt[0:nq, col:col + 1], scalar1=-1.0,
                                        scalar2=None, op0=ALU.mult), [], [stt.r])

    def sm_exp(s, nq, nk, Ssb, E, stt, col):
        s.act(E[0:nq, 0:nk], Ssb[0:nq, 0:nk], AF.Exp, [Ssb.r, stt.r], [E.r, stt.r],
              bias=stt[0:nq, col + 1:col + 2], accum_out=stt[0:nq, col + 2:col + 3])

    def softmax(s, chunks, nq, nk, scale, mask, Ssb, E, stt, col):
        s.sm_scores(chunks, nq, nk, scale, mask, Ssb, stt, col)
        s.sm_exp(nq, nk, Ssb, E, stt, col)

    def pv_T(s, Wt, nq, nk, WT):
        nfull = nk // 128
        rem = nk - nfull * 128

        def ev(j0, cnt, pv2, bank):
            s.dve(lambda e: e.tensor_copy(out=WT[:, j0:j0 + cnt, 0:nq], in_=pv2), [bank], [WT.r])
        if nfull:
            s.transposes(Wt, lambda j: Wt[0:nq, j * 128:(j + 1) * 128], nfull, nq, ev)
        if rem:
            def ev2(j0, cnt, pv2, bank):
                s.dve(lambda e: e.tensor_copy(out=WT[0:rem, nfull:nfull + 1, 0:nq], in_=pv2), [bank], [WT.r])
            s.transposes(Wt, lambda j: Wt[0:nq, nfull * 128:nk], 1, nq, ev2, ncols=rem)

    def pv_mm(s, nq, nk, WT, vfn, vres, dv):
        nfull = nk // 128
        rem = nk - nfull * 128
        nkb = nfull + (1 if rem else 0)
        bk = s.mbank()

        def emit(pe):
            ins = None
            for kb in range(nkb):
                nr = 128 if kb < nfull else rem
                ins = pe.matmul(s.ps[0:nq, bk, 0:dv], lhsT=WT[0:nr, kb, 0:nq], rhs=vfn(kb, nr),
                                start=(kb == 0), stop=(kb == nkb - 1))
            return ins
        s.mm_group(emit, [WT.r] + list(vres), s.psr[bk])
        return bk

    def pv(s, Wt, nq, nk, WT, vfn, vres, dv):
        s.pv_T(Wt, nq, nk, WT)
        return s.pv_mm(nq, nk, WT, vfn, vres, dv)

    def compute_lam(s, l):
        c, I = s.c, s.I
        lam_init = 0.8 - 0.6 * math.exp(-0.3 * l)
        with s.scope() as st:
            dl = s.sb(st, "dl", [128, 256], F32)
            pr = s.sb(st, "dlp", [128, 128], F32)
            s.load(dl[:], I["diff_lambda"][l], dl)
            s.dve(lambda e: e.tensor_tensor(out=pr[:, 0:64], in0=dl[:, 0:64], in1=dl[:, 64:128], op=ALU.mult), [dl.r], [pr.r])
            s.dve(lambda e: e.tensor_tensor(out=pr[:, 64:128], in0=dl[:, 128:192], in1=dl[:, 192:256], op=ALU.mult), [dl.r], [pr.r])
            s.dve(lambda e: e.tensor_reduce(out=s.st[:, 0:2], in_=pr[:, :].rearrange("p (a b) -> p a b", a=2),
                                            axis=AX.X, op=ALU.add), [pr.r], [s.st.r])
            s.act(s.st[:, 2:4], s.st[:, 0:2], AF.Exp, [s.st.r], [s.st.r])
            s.dve(lambda e: e.tensor_tensor(out=s.st[:, 4:5], in0=s.st[:, 3:4], in1=s.st[:, 2:3], op=ALU.subtract), [], [s.st.r])
            s.dve(lambda e: e.tensor_scalar(out=s.st[:, 4:5], in0=s.st[:, 4:5], scalar1=-lam_init, scalar2=None, op0=ALU.add),
                  [], [s.st.r])
            s.sch.barrier()
        return lam_init

    def ph_attn(s, l, kind):
        c, I, X = s.c, s.I, s.X
        diff = kind == "diff"
        heads = c.DH if diff else c.MH
        mixbase = c.PCH if diff else c.PCH + c.DH
        scale = 64 ** -0.5 if diff else 192 ** -0.5
        kkey, vkey, kskey, vskey = ("kT", "vv", "kTs", "vs") if diff else ("knT", "vh", "knTs", "vhs")
        NCMP = 2 if diff else 1
        if diff:
            lam_init = s.compute_lam(l)
        for hg0 in range(0, heads, 8):
            nhg = min(8, heads - hg0)
            for samp in (False, True):
                with s.scope() as st:
                    nkmax = c.NKS if samp else c.P
                    nkbmax = (nkmax + 127) // 128
                    nqmax = c.S if samp else 128
                    if not samp:
                        KT = s.sb(st, "KT", [128, nhg, c.P], BF)
                        s.load(KT[:], X[kkey][hg0:hg0 + nhg].rearrange("h p t -> p h t"), KT, [s.XR[kkey]])
                        V = s.sb(st, "V", [128, c.NB, nhg * 128], BF)
                        s.load(V[:], X[vkey][:, hg0 * 128:(hg0 + nhg) * 128].rearrange("(kb p) e -> p kb e", p=128),
                               V, [s.XR[vkey]])
                        if not diff:
                            KP = s.sb(st, "KP", [64, c.P], BF)
                            s.load(KP[:], X["kpeT"], KP, [s.XR["kpeT"]])
                    else:
                        KTs = [s.sb(st, "KTs", [128, c.NKS], BF) for _ in range(2)]
                        Vs = [s.sb(st, "Vs", [128, nkbmax, 128], BF) for _ in range(4)]
                        if not diff:
                            KP = s.sb(st, "KPs", [64, c.NKS], BF)
                            s.load(KP[:], X["kpeTs"], KP, [s.XR["kpeTs"]])
                    qb = [s.sb(st, "qb", [128, nhg, nqmax], BF) for _ in range(2)]
                    qp = [s.sb(st, "qp", [64, nhg, nqmax], BF) for _ in range(2)] if not diff else None
                    Ssb = [[s.sb(st, "Ssb", [nqmax, nkmax], F32) for _ in range(NCMP)] for _ in range(2)]
                    nE = 2
                    E = [[s.sb(st, "E", [nqmax, nkmax], BF) for _ in range(NCMP)] for _ in range(nE)]
                    stts = [s.sb(st, "astt", [128, 16], F32) for _ in range(4)]
                    if diff:
                        nW = 2
                        nT = 1 if samp else 2
                        Wc = [s.sb(st, "Wc", [nqmax, nkmax], BF) for _ in range(nW)]
                        tmp = [s.sb(st, "Wtmp", [nqmax, nkmax], F32) for _ in range(nT)]
                        oall = [s.sb(st, "oall", [nqmax, nhg, 128], F32) for _ in range(2)]
                        osq = [s.sb(st, "osq", [nqmax, nhg, 128], F32) for _ in range(2)]
                        estt = s.sb(st, "estt", [128, 16], F32)
                        gsub = s.sb(st, "gsub", [128, 128], F32)
                        s.load(gsub[:], I["g_diff_sub"][l], gsub)
                    WT = [s.sb(st, "WT", [128, nkbmax, nqmax], BF) for _ in range(2)]
                    ob = [s.sb(st, "ob", [nqmax, nhg * 128], BF) for _ in range(2)]
                    ts = [s.sb(st, "ats", [128, nhg, nqmax], BF) for _ in range(2)]
                    blocks = [(c.NB, c.P, c.S)] if samp else [(i, i * 128, 128) for i in range(c.NB)]
                    items = [(bx, h) for bx in range(len(blocks)) for h in range(nhg)]
                    qkey = "qT" if diff else "qnT"
                    nfull_s = c.NKS // 128
                    rem_s = c.NKS - nfull_s * 128

                    def load_q(bx):
                        bi, c0, nq = blocks[bx]
                        q = qb[bx % 2]
                        s.load(q[:, :, 0:nq], X[qkey][hg0:hg0 + nhg, :, c0:c0 + nq].rearrange("h p t -> p h t"),
                               q, [s.XR[qkey]])
                        if not diff:
                            qq = qp[bx % 2]
                            s.load(qq[:, :, 0:nq], X["qpT"][hg0:hg0 + nhg, :, c0:c0 + nq].rearrange("h p t -> p h t"),
                                   qq, [s.XR["qpT"]])

                    def load_sample_k(h):
                        s.load(KTs[h % 2][:], X[kskey][hg0 + h], KTs[h % 2], [s.XR[kskey]])

                    def load_sample_v(h):
                        hh = hg0 + h
                        Vt = Vs[h % 4]
                        s.load(Vt[:, 0:nfull_s, :],
                               X[vskey][0:nfull_s * 128, hh * 128:(hh + 1) * 128].rearrange("(kb p) e -> p kb e", p=128),
                               Vt, [s.XR[vskey]])
                        if rem_s:
                            s.load(Vt[0:rem_s, nfull_s, :], X[vskey][nfull_s * 128:c.NKS, hh * 128:(hh + 1) * 128],
                                   Vt, [s.XR[vskey]])

                    def geom(t):
                        bx, h = items[t]
                        bi, c0, nq = blocks[bx]
                        nk = c.NKS if samp else (bi + 1) * 128
                        return bx, h, bi, c0, nq, nk

                    def S1(t):
                        bx, h, bi, c0, nq, nk = geom(t)
                        if h == 0 and bx + 1 < len(blocks):
                            load_q(bx + 1)
                        q = qb[bx % 2]
                        stt = stts[t % 4]
                        if samp:
                            if h + 1 < nhg:
                                load_sample_k(h + 1)
                            load_sample_v(h)
                            Kt = KTs[h % 2]
                            kf_full = lambda k0, w: Kt[:, k0:k0 + w]
                            kf_lo = lambda k0, w: Kt[0:64, k0:k0 + w]
                            kf_hi = lambda k0, w: Kt[64:128, k0:k0 + w]
                            kres = Kt.r
                        else:
                            kf_full = lambda k0, w: KT[:, h, k0:k0 + w]
                            kf_lo = lambda k0, w: KT[0:64, h, k0:k0 + w]
                            kf_hi = lambda k0, w: KT[64:128, h, k0:k0 + w]
                            kres = KT.r
                        mask = not samp
                        if diff:
                            s.sm_scores([(q[0:64, h, 0:nq], q.r, kf_lo, kres)], nq, nk, scale, mask, Ssb[t % 2][0], stt, 0)
                            s.sm_scores([(q[64:128, h, 0:nq], q.r, kf_hi, kres)], nq, nk, scale, mask, Ssb[t % 2][1], stt, 4)
                        else:
                            qq = qp[bx % 2]
                            kp_f = lambda k0, w: KP[0:64, k0:k0 + w]
                            s.sm_scores([(q[:, h, 0:nq], q.r, kf_full, kres), (qq[0:64, h, 0:nq], qq.r, kp_f, KP.r)],
                                        nq, nk, scale, mask, Ssb[t % 2][0], stt, 0)

                    def S2(t):
                        bx, h, bi, c0, nq, nk = geom(t)
                        stt = stts[t % 4]
                        for cm in range(NCMP):
                            s.sm_exp(nq, nk, Ssb[t % 2][cm], E[t % nE][cm], stt, 4 * cm)
                        if diff:
                            E0, E1 = E[t % nE]
                            wc, tm = Wc[t % nW], tmp[t % nT]
                            s.dve(lambda e: e.reciprocal(out=stt[0:nq, 8:9], in_=stt[0:nq, 2:3]), [], [stt.r])
                            s.dve(lambda e: e.reciprocal(out=stt[0:nq, 9:10], in_=stt[0:nq, 6:7]), [], [stt.r])
                            s.dve(lambda e: e.tensor_tensor(out=stt[0:nq, 9:10], in0=stt[0:nq, 9:10],
                                                            in1=s.st[0:nq, 4:5], op=ALU.mult), [s.st.r], [stt.r])
                            s.pool(lambda e: e.tensor_tensor(out=tm[0:nq, 0:nk], in0=E1[0:nq, 0:nk],
                                                             in1=stt[0:nq, 9:10].broadcast_to([nq, nk]), op=ALU.mult),
                                   [E1.r, stt.r], [tm.r])
                            s.dve(lambda e: e.scalar_tensor_tensor(out=wc[0:nq, 0:nk], in0=E0[0:nq, 0:nk],
                                                                   scalar=stt[0:nq, 8:9], in1=tm[0:nq, 0:nk],
                                                                   op0=ALU.mult, op1=ALU.add),
                                  [E0.r, tm.r, stt.r], [wc.r])
                        else:
                            s.dve(lambda e: e.reciprocal(out=stt[0:nq, 8:9], in_=stt[0:nq, 2:3]), [], [stt.r])

                    def S3a(t):
                        bx, h, bi, c0, nq, nk = geom(t)
                        src = Wc[t % nW] if diff else E[t % nE][0]
                        s.pv_T(src, nq, nk, WT[t % 2])

                    def S3b(t):
                        bx, h, bi, c0, nq, nk = geom(t)
                        stt = stts[t % 4]
                        o = ob[bx % 2]
                        wt = WT[t % 2]
                        if samp:
                            Vt = Vs[h % 4]
                            vfn = lambda kb, nr: Vt[0:nr, kb, :]
                            vres = [Vt.r]
                        else:
                            vfn = lambda kb, nr: V[0:nr, kb, h * 128:(h + 1) * 128]
                            vres = [V.r]
                        bk = s.pv_mm(nq, nk, wt, vfn, vres, 128)
                        if diff:
                            oa = oall[bx % 2]
                            s.act(oa[0:nq, h, :], s.ps[0:nq, bk, 0:128], AF.Copy, [s.psr[bk]], [oa.r])
                        else:
                            s.act(o[0:nq, h * 128:(h + 1) * 128], s.ps[0:nq, bk, 0:128], AF.Copy,
                                  [s.psr[bk], stt.r], [o.r], scale=stt[0:nq, 8:9])
                        if h != nhg - 1:
                            return
                        if diff:
                            oa, oq = oall[bx % 2], osq[bx % 2]
                            s.pool(lambda e: e.tensor_tensor(out=oq[0:nq], in0=oa[0:nq], in1=oa[0:nq], op=ALU.mult),
                                   [oa.r], [oq.r])
                            s.dve(lambda e: e.tensor_reduce(out=estt[0:nq, 0:nhg], in_=oq[0:nq], axis=AX.X, op=ALU.add),
                                  [oq.r], [estt.r])
                            s.rstd(estt[0:nq, 0:nhg], estt[0:nq, 0:nhg], nq, 128, [], [estt.r])
                            s.dve(lambda e: e.tensor_tensor(out=oq[0:nq], in0=oa[0:nq],
                                                            in1=estt[0:nq, 0:nhg, None].broadcast_to([nq, nhg, 128]),
                                                            op=ALU.mult), [oa.r, estt.r], [oq.r])
                            s.dve(lambda e: e.scalar_tensor_tensor(
                                out=o[0:nq, :].rearrange("p (h e) -> p h e", e=128), in0=oq[0:nq],
                                scalar=1.0 - lam_init, in1=gsub[0:nq, None, :].broadcast_to([nq, nhg, 128]),
                                op0=ALU.mult, op1=ALU.mult), [oq.r, gsub.r], [o.r])
                        tt = ts[bx % 2]

                        def ev(j0, cnt, pv2, bank):
                            s.act(tt[:, j0:j0 + cnt, 0:nq], pv2, AF.Copy, [bank], [tt.r])
                        s.transposes(o, lambda j: o[0:nq, j * 128:(j + 1) * 128], nhg, nq, ev)
                        mb = mixbase + hg0
                        s.store(X["mixT"][mb:mb + nhg, :, c0:c0 + nq].rearrange("h p t -> p h t"),
                                tt[:, 0:nhg, 0:nq], tt, [s.XR["mixT"]])

                    load_q(0)
                    if samp:
                        load_sample_k(0)
                    n = len(items)
                    for t in range(n + 3):
                        if 0 <= t - 3 < n:
                            S3a(t - 3)
                        if t < n:
                            S1(t)
                        if 0 <= t - 1 < n:
                            S2(t - 1)
                        if 0 <= t - 3 < n:
                            S3b(t - 3)
                s.sch.barrier()

    def ph_diffattn(s, l):
        s.ph_attn(l, "diff")

    def ph_mlaattn(s, l):
        s.ph_attn(l, "mla")

    def tm_gemm_to_fs(s, st, xT, KCn, W, l_unused=None):
        c, X = s.c, s.X
        slab = [s.sb(st, "oslab", [128, KCn, 512], BF) for _ in range(2)]
        o32 = [s.sb(st, "oo32", [128, 512], F32) for _ in range(3)]
        ng = c.D // 512
        s.load_slab(slab[0], W, 0, 512)
        it = 0
        for g in range(ng):
            sl = slab[g % 2]
            if g + 1 < ng:
                s.load_slab(slab[(g + 1) % 2], W, (g + 1) * 512, 512)
            for (c0, nt) in c.tbs:
                bk = s.mbank()

                def emit(pe, sl=sl, c0=c0, nt=nt, bk=bk):
                    ins = None
                    for k in range(KCn):
                        ins = pe.matmul(s.ps[0:nt, bk, :], lhsT=xT[:, k, c0:c0 + nt], rhs=sl[:, k, :],
                                        start=(k == 0), stop=(k == KCn - 1))
                    return ins
                s.mm_group(emit, [sl.r, xT.r], s.psr[bk])
                it += 1
                o = o32[it % 3]
                if it % 2:
                    s.act(o[0:nt, :], s.ps[0:nt, bk, :], AF.Copy, [s.psr[bk]], [o.r])
                else:
                    s.dve(lambda e, o=o, nt=nt, bk=bk: e.tensor_copy(out=o[0:nt, :], in_=s.ps[0:nt, bk, :]),
                          [s.psr[bk]], [o.r])
                s.store(X["fs"][c0:c0 + nt, g * 512:(g + 1) * 512], o[0:nt, :], o, [s.XR["fs"]])

    def ph_outproj(s, l):
        with s.scope() as st:
            mixT = s.load_xT(st, "mixr", "mixT")
            s.tm_gemm_to_fs(st, mixT, s.c.KC, s.I["w_out"][l])

    def ph_cross(s, l):
        c, I, X = s.c, s.I, s.X
        with s.scope() as st0:
            cqT = s.sb(st0, "xqT", [128, 4, c.NT], BF)
            oT = s.sb(st0, "xoT", [128, 4, c.NT], BF)
            with s.scope() as st:
                hT = s.load_xT(st, "hTr", "hT")
                slab = s.sb(st, "xslab", [128, c.KC, 512], BF)
                s.load_slab(slab, I["w_mem_q"][l], 0, 512)
                it = 0
                for ch in range(4):
                    for (c0, w) in c.tgs:
                        bk = s.mbank()

                        def emit(pe, ch=ch, c0=c0, w=w, bk=bk):
                            ins = None
                            for k in range(c.KC):
                                ins = pe.matmul(s.ps[:, bk, 0:w], lhsT=slab[:, k, ch * 128:(ch + 1) * 128],
                                                rhs=hT[:, k, c0:c0 + w], start=(k == 0), stop=(k == c.KC - 1))
                            return ins
                        s.mm_group(emit, [slab.r, hT.r], s.psr[bk])
                        it += 1
                        if it % 2:
                            s.act(cqT[:, ch, c0:c0 + w], s.ps[:, bk, 0:w], AF.Copy, [s.psr[bk]], [cqT.r])
                        else:
                            s.dve(lambda e, ch=ch, c0=c0, w=w, bk=bk: e.tensor_copy(out=cqT[:, ch, c0:c0 + w],
                                                                                   in_=s.ps[:, bk, 0:w]),
                                  [s.psr[bk]], [cqT.r])
            s.sch.barrier()
            with s.scope() as st:
                nmb = c.MEM // 128
                KM = {False: s.sb(st, "xkm", [128, 4, c.MEM], BF), True: s.sb(st, "xkms", [128, 4, c.MEM], BF)}
                VM = {False: s.sb(st, "xvm", [128, nmb, 512], BF), True: s.sb(st, "xvms", [128, nmb, 512], BF)}
                s.load(KM[False][:], X["mkT"].rearrange("h p m -> p h m"), KM[False], [s.XR["mkT"]])
                s.load(KM[True][:], X["mkTs"].rearrange("h p m -> p h m"), KM[True], [s.XR["mkTs"]])
                s.load(VM[False][:], X["mv"].rearrange("(kb p) e -> p kb e", p=128), VM[False], [s.XR["mv"]])
                s.load(VM[True][:], X["mvs"].rearrange("(kb p) e -> p kb e", p=128), VM[True], [s.XR["mvs"]])
                Ssb = s.sb(st, "xS", [128, c.MEM], F32)
                E = [s.sb(st, "xE", [128, c.MEM], BF) for _ in range(2)]
                WT = [s.sb(st, "xWT", [128, nmb, 128], BF) for _ in range(2)]
                ob = [s.sb(st, "xob", [128, 512], BF) for _ in range(2)]
                stt = s.sb(st, "xstt", [128, 16], F32)
                for bi, (c0, nq) in enumerate(c.tbs):
                    samp = c0 >= c.P
                    K, V = KM[samp], VM[samp]
                    o = ob[bi % 2]
                    for h in range(4):
                        Eh = E[h % 2]
                        s.softmax([(cqT[:, h, c0:c0 + nq], cqT.r, lambda k0, w, h=h, K=K: K[:, h, k0:k0 + w], K.r)],
                                  nq, c.MEM, 128 ** -0.5, False, Ssb, Eh, stt, 0)
                        s.dve(lambda e: e.reciprocal(out=stt[0:nq, 8:9], in_=stt[0:nq, 2:3]), [], [stt.r])
                        bk = s.pv(Eh, nq, c.MEM, WT[h % 2], lambda kb, nr, h=h, V=V: V[0:nr, kb, h * 128:(h + 1) * 128],
                                  [V.r], 128)
                        s.act(o[0:nq, h * 128:(h + 1) * 128], s.ps[0:nq, bk, 0:128], AF.Copy, [s.psr[bk], stt.r], [o.r],
                              scale=stt[0:nq, 8:9])

                    def ev(j0, cnt, pv2, bank, c0=c0, nq=nq):
                        s.dve(lambda e: e.tensor_copy(out=oT[:, j0:j0 + cnt, c0:c0 + nq], in_=pv2), [bank], [oT.r])
                    s.transposes(o, lambda j, o=o, nq=nq: o[0:nq, j * 128:(j + 1) * 128], 4, nq, ev)
            s.sch.barrier()
            with s.scope() as st:
                s.tm_gemm_to_fs(st, oT, 4, I["w_mem_o"][l])

    def ph_ffn1(s, l):
        c, I, X = s.c, s.I, s.X
        GW = 128
        ng = c.DFF // GW
        with s.scope() as st:
            hT = s.load_xT(st, "hTr", "hT")
            sg = [s.sb(st, "fsg", [128, c.KC, GW], BF) for _ in range(2)]
            su = [s.sb(st, "fsu", [128, c.KC, GW], BF) for _ in range(2)]
            g32 = [s.sb(st, "fg32", [128, 512], F32) for _ in range(2)]
            ast = [s.sb(st, "fast", [128, c.NT], BF) for _ in range(2)]
            s.load_slab(sg[0], I["w_gate"][l], 0, GW)
            s.load_slab(su[0], I["w_up"][l], 0, GW)
            it = 0
            for g in range(ng):
                wg, wu = sg[g % 2], su[g % 2]
                if g + 1 < ng:
                    s.load_slab(sg[(g + 1) % 2], I["w_gate"][l], (g + 1) * GW, GW)
                    s.load_slab(su[(g + 1) % 2], I["w_up"][l], (g + 1) * GW, GW)
                for j in range(GW // 128):
                    a = ast[(g * (GW // 128) + j) % 2]
                    for (c0, w) in c.tgs:
                        bg, bu = s.mbank(), s.mbank()

                        def emit(pe, wt=wg, j=j, c0=c0, w=w, bk=bg):
                            ins = None
                            for k in range(c.KC):
                                ins = pe.matmul(s.ps[:, bk, 0:w], lhsT=wt[:, k, j * 128:(j + 1) * 128],
                                                rhs=hT[:, k, c0:c0 + w], start=(k == 0), stop=(k == c.KC - 1))
                            return ins
                        s.mm_group(emit, [wg.r, hT.r], s.psr[bg])
                        s.mm_group(lambda pe, e_=emit, wt=wu, bk=bu: e_(pe, wt=wt, bk=bk), [wu.r, hT.r], s.psr[bu])
                        it += 1
                        t32 = g32[it % 2]
                        s.act(t32[:, 0:w], s.ps[:, bg, 0:w], AF.Silu, [s.psr[bg]], [t32.r])
                        s.dve(lambda e, a=a, t32=t32, c0=c0, w=w, bu=bu: e.tensor_tensor(
                            out=a[:, c0:c0 + w], in0=t32[:, 0:w], in1=s.ps[:, bu, 0:w], op=ALU.mult),
                            [t32.r, s.psr[bu]], [a.r])
                    s.store(X["actT"][g * (GW // 128) + j], a[:], a, [s.XR["actT"]])

    def ph_ffn2(s, l):
        c, I, X = s.c, s.I, s.X
        FB = 8
        RB = 6
        regions = [c.tbs[i:i + RB] for i in range(0, len(c.tbs), RB)]
        nfg = (c.FC + FB - 1) // FB
        ndg = c.D // 512
        with s.scope() as st:
            rcmax = max(sum(nt for _, nt in r) for r in regions)
            wsl = [s.sb(st, "f2w", [128, FB, 512], BF) for _ in range(nfg)]
            ab = [s.sb(st, "f2a", [128, FB, rcmax], BF) for _ in range(3)]
            o32 = [s.sb(st, "f2o", [128, 512], F32) for _ in range(3)]
            Wd = I["w_down"][l]

            def load_w(dg, fg):
                f0 = fg * FB
                nb = min(FB, c.FC - f0)
                s.load(wsl[fg][:, 0:nb, :],
                       Wd[f0 * 128:(f0 + nb) * 128, dg * 512:(dg + 1) * 512].rearrange("(f p) n -> p f n", p=128),
                       wsl[fg], q="pool")
            for fg in range(nfg):
                load_w(0, fg)
            items = [(dg, ri, fg) for dg in range(ndg) for ri in range(len(regions)) for fg in range(nfg)]

            def load_a(j):
                dg, ri, fg = items[j]
                reg = regions[ri]
                r0 = reg[0][0]
                rc = sum(nt for _, nt in reg)
                f0 = fg * FB
                nb = min(FB, c.FC - f0)
                a = ab[j % 3]
                s.load(a[:, 0:nb, 0:rc], X["actT"][f0:f0 + nb, :, r0:r0 + rc].rearrange("f p t -> p f t"),
                       a, [s.XR["actT"]])
            PF = 2
            for j in range(min(PF, len(items))):
                load_a(j)
            io = 0
            for j, (dg, ri, fg) in enumerate(items):
                if j + PF < len(items):
                    load_a(j + PF)
                reg = regions[ri]
                r0 = reg[0][0]
                banks = list(range(len(reg)))
                f0 = fg * FB
                nb = min(FB, c.FC - f0)
                a, w = ab[j % 3], wsl[fg]

                def emit(pe, a=a, w=w, f0=f0, nb=nb, reg=reg, r0=r0):
                    ins = None
                    for f in range(nb):
                        for i, (c0, nt) in enumerate(reg):
                            ins = pe.matmul(s.ps[0:nt, i, :], lhsT=a[:, f, c0 - r0:c0 - r0 + nt], rhs=w[:, f, :],
                                            start=(f0 + f == 0), stop=(f0 + f == c.FC - 1))
                    if f0 + nb < c.FC:
                        ins = pe.matmul(s.ps[0:1, 7, 0:1], lhsT=a[:, 0, 0:1], rhs=w[:, 0, 0:1], start=True, stop=True)
                    return ins
                s.sch.op("pe", emit, [a.r, w.r], [s.psr[7]] + [s.psr[i] for i in banks])
                if ri == len(regions) - 1 and dg + 1 < ndg:
                    load_w(dg + 1, fg)
                if fg != nfg - 1:
                    continue
                for i, (c0, nt) in enumerate(reg):
                    o = o32[io % 3]
                    io += 1
                    if io % 2:
                        s.act(o[0:nt, :], s.ps[0:nt, i, :], AF.Copy, [s.psr[i]], [o.r])
                    else:
                        s.dve(lambda e, o=o, nt=nt, i=i: e.tensor_copy(out=o[0:nt, :], in_=s.ps[0:nt, i, :]),
                              [s.psr[i]], [o.r])
                    s.store(X["fs"][c0:c0 + nt, dg * 512:(dg + 1) * 512], o[0:nt, :], o, [s.XR["fs"]])


def _rope_tables(cfg):
    half = 32
    inv = (np.float32(10000.0) ** (-np.arange(half, dtype=np.float32) / np.float32(half))).astype(np.float32)
    pos = np.concatenate([np.arange(cfg.P), cfg.PAST + np.arange(cfg.S)]).astype(np.float32)
    ang = (pos[:, None] * inv[None, :]).astype(np.float32)
    cos, sin = np.cos(ang).astype(np.float32), np.sin(ang).astype(np.float32)
    rc = np.zeros((128, cfg.NB + 1, 32), np.float32)
    rs = np.zeros((128, cfg.NB + 1, 32), np.float32)
    for bi, (c0, nt) in enumerate(cfg.tbs):
        rc[:nt, bi] = cos[c0:c0 + nt]
        rs[:nt, bi] = sin[c0:c0 + nt]
    return rc, rs


def _col(g, L):
    n = g.shape[1]
    return np.ascontiguousarray(g.reshape(L, n // 128, 128).transpose(0, 2, 1))


def _rep(g, L):
    return np.ascontiguousarray(np.broadcast_to(g.reshape(L, 1, -1), (L, 128, g.reshape(L, -1).shape[1])))


def make_in_maps(cfg, inp, ncores):
    L = cfg.L
    f = lambda a: np.ascontiguousarray(np.asarray(a, dtype=np.float32))
    rc, rs = _rope_tables(cfg)
    rcnt = np.zeros((128, 4, 16), np.float32)
    for g in range(4):
        w = 2 ** (g + 1)
        rcnt[:, g, :] = (1.0 / np.minimum(w, np.arange(16) + 1)).astype(np.float32)[None, :]
    shared = {
        "rcos": rc, "rsin": rs, "rcnt": rcnt,
        "identb": np.eye(128).astype(ml_dtypes.bfloat16), "identf": np.eye(128, dtype=np.float32),
        "memp": None,
    }
    for nm in ("g_mix_pre", "g_mem", "g_x_pre", "g_ff_pre", "g_mla_q", "pool_scale"):
        shared[nm] = _col(f(inp[nm]), L)
    for nm in ("g_mix_post", "g_x_post", "g_ff_post", "g_mla_kv", "g_diff_sub"):
        shared[nm] = _rep(f(inp[nm]), L)
    shared["diff_lambda"] = _rep(f(inp["diff_lambda"]).reshape(L, -1), L)
    for nm in ("w_in", "w_pool", "w_mla_uq", "w_out", "w_mem_k", "w_mem_v", "w_mem_q", "w_mem_o",
               "w_gate", "w_up", "w_down"):
        shared[nm] = f(inp[nm])
    shared["w_mla_uk"] = f(inp["w_mla_uk"]).reshape(L, 512, -1)
    shared["w_mla_uv"] = f(inp["w_mla_uv"]).reshape(L, 512, -1)
    maps = []
    for b in range(ncores):
        m = dict(shared)
        m["xp"] = f(inp["x_prompt"][b])
        m["xs"] = f(inp["x_sample"][b])
        m["cdk"] = f(np.asarray(inp["cache_diff_k"])[:, b]).reshape(L, cfg.PAST, -1)
        m["cdv"] = f(np.asarray(inp["cache_diff_v"])[:, b]).reshape(L, cfg.PAST, -1)
        m["cckv"] = f(np.asarray(inp["cache_mla_ckv"])[:, b])
        m["ckpe"] = f(np.asarray(inp["cache_mla_kpe"])[:, b])
        m["cpool"] = f(np.asarray(inp["cache_pool"])[:, b])
        m["cmk"] = f(np.asarray(inp["cache_mem_k"])[:, b]).reshape(L, cfg.MEM, -1)
        m["cmv"] = f(np.asarray(inp["cache_mem_v"])[:, b]).reshape(L, cfg.MEM, -1)
        m["memp"] = f(inp["mem_prompt"][b])
        maps.append(m)
    return maps


def gather_outputs(cfg, res, ncores):
    L = cfg.L
    st = lambda k: np.stack([np.asarray(res[b][k], dtype=np.float32) for b in range(ncores)])
    yp, ys = st("y_p"), st("y_s")

    def lb(k, tail):
        a = st(k)
        a = np.swapaxes(a, 0, 1)
        return np.ascontiguousarray(a.reshape(a.shape[:3] + tuple(tail)))
    return (yp, ys,
            lb("pdk", (cfg.DH, 128)), lb("pdv", (cfg.DH, 128)), lb("pckv", (512,)), lb("pkpe", (64,)),
            lb("ppool", (cfg.PW,)), lb("pmk", (4, 128)), lb("pmv", (4, 128)),
            lb("sdk", (cfg.DH, 128)), lb("sdv", (cfg.DH, 128)), lb("sckv", (512,)), lb("skpe", (64,)),
            lb("spool", (cfg.PW,)))


_CACHE = {}


def kernel(**inputs):
    cfg = Cfg()
    ncores = 8
    if "nc" not in _CACHE:
        _CACHE["nc"] = Builder(cfg).build()
    nc = _CACHE["nc"]
    in_maps = make_in_maps(cfg, inputs, ncores)
    res = run_bass_kernel_spmd(nc, in_maps, core_ids=list(range(ncores)))
    return gather_outputs(cfg, res.results, ncores)
```
